# Optimizing a Trainium2 kernel written in Bass

```python
import jax, jax.numpy as jnp
from jax import lax
import numpy as np

D_MODEL = 1024
BATCH = 4
SEQ = 8192
DEPTH = 1

CHUNK = 64
SUB_CHUNK = 16
HG_DK = 128
HG_HEADS = D_MODEL // HG_DK
HG_DV = D_MODEL // HG_HEADS
HG_FD = HG_HEADS * HG_DK
HG_DI = HG_HEADS * HG_DV
LB_TAIL_LOGIT = 2.0
GM_BLOCK = 128
GM_WIDTH = D_MODEL
GM_GROUPS = 8
GM_CG = GM_WIDTH // GM_GROUPS
D_FF = -(-8 * D_MODEL // (3 * 256)) * 256
IN_WIDTH = 2 * HG_FD + 2 * HG_DI + 2 * GM_WIDTH + 2 * D_MODEL
IN_SPLITS = (HG_FD, 2 * HG_FD, 2 * HG_FD + HG_DI, 2 * HG_FD + 2 * HG_DI,
             2 * HG_FD + 2 * HG_DI + GM_WIDTH, 2 * HG_FD + 2 * HG_DI + 2 * GM_WIDTH,
             2 * HG_FD + 2 * HG_DI + 2 * GM_WIDTH + D_MODEL)
DEEPNORM_ALPHA = (2.0 * DEPTH) ** 0.25
DEEPNORM_BETA = (8.0 * DEPTH) ** -0.25
LN_EPS = 1e-5
RMS_EPS = 1e-6

kernel_name = "hybrid_hgrn2_gmlp_deepnorm_adaln_block"


def _layer_norm(x):
    x = x.astype(jnp.float32)
    xc = x - jnp.mean(x, axis=-1, keepdims=True)
    return xc * lax.rsqrt(jnp.mean(xc * xc, axis=-1, keepdims=True) + LN_EPS)


def _to_chunks(t, heads):
    b, s, _ = t.shape
    return t.reshape(b, s // CHUNK, CHUNK, heads, -1).transpose(0, 3, 1, 2, 4)


def hgrn2_mixer(zq, zf, zi, zg, lower_bound, norm_w):
    b_, s_, _ = zq.shape
    n_sub = CHUNK // SUB_CHUNK
    f = lower_bound + (1.0 - lower_bound) * jax.nn.sigmoid(zf.astype(jnp.float32))
    q = _to_chunks(jax.nn.silu(zq.astype(jnp.float32)), HG_HEADS)
    k = _to_chunks(1.0 - f, HG_HEADS)
    log_f = _to_chunks(jnp.log(f), HG_HEADS)
    v = _to_chunks(zi.astype(jnp.float32), HG_HEADS)
    cum = jnp.cumsum(log_f, axis=3)
    cum_last = cum[:, :, :, -1:, :]

    u_chunk = jnp.einsum('bhnlk,bhnlv->bhnkv', k * jnp.exp(cum_last - cum), v)
    decay = jnp.exp(cum_last[:, :, :, 0, :])

    def step(state, inp):
        dec, upd = inp
        return dec[..., None] * state + upd, state

    state0 = jnp.zeros((b_, HG_HEADS, HG_DK, HG_DV), jnp.float32)
    _, s_prev = lax.scan(step, state0, (jnp.moveaxis(decay, 2, 0), jnp.moveaxis(u_chunk, 2, 0)))
    s_prev = jnp.moveaxis(s_prev, 0, 2)
    o_inter = jnp.einsum('bhnlk,bhnkv->bhnlv', q * jnp.exp(cum), s_prev)

    shp = q.shape[:3] + (n_sub, SUB_CHUNK, HG_DK)
    ref = (cum - log_f)[:, :, :, ::SUB_CHUNK, :]
    q_sub = q.reshape(shp) * jnp.exp(cum.reshape(shp) - ref[:, :, :, :, None, :])
    key_pos = jnp.arange(CHUNK)
    sub_idx = jnp.arange(n_sub)
    key_ok = key_pos[None, :] < (sub_idx[:, None] + 1) * SUB_CHUNK
    expo = ref[:, :, :, :, None, :] - cum[:, :, :, None, :, :]
    k_sub = k[:, :, :, None] * jnp.exp(jnp.where(key_ok[:, :, None], expo, -jnp.inf))
    scores = jnp.einsum('bhnimk,bhnisk->bhnims', q_sub, k_sub)
    q_pos = sub_idx[:, None] * SUB_CHUNK + jnp.arange(SUB_CHUNK)[None, :]
    causal = key_pos[None, None, :] <= q_pos[:, :, None]
    scores = jnp.where(causal, scores, 0.0)
    o_intra = jnp.einsum('bhnims,bhnsv->bhnimv', scores, v).reshape(o_inter.shape)

    o = (o_inter + o_intra).transpose(0, 2, 3, 1, 4).reshape(b_, s_, HG_HEADS, HG_DV)
    o = o * lax.rsqrt(jnp.mean(o * o, axis=-1, keepdims=True) + RMS_EPS) * norm_w
    gate = jax.nn.silu(zg.astype(jnp.float32)).reshape(b_, s_, HG_HEADS, HG_DV)
    return (o * gate).reshape(b_, s_, HG_DI)


def gmlp_mixer(zu, zv, ln_w, ln_b, w_s, b_s):
    b_, s_, _ = zu.shape
    u = jax.nn.gelu(zu.astype(jnp.float32), approximate=False)
    v = jax.nn.gelu(zv.astype(jnp.float32), approximate=False)
    v = _layer_norm(v) * ln_w + ln_b
    v = v.reshape(b_, s_ // GM_BLOCK, GM_BLOCK, GM_GROUPS, GM_CG)
    pos = jnp.arange(GM_BLOCK) // CHUNK
    mask = pos[:, None] >= pos[None, :]
    w = jnp.where(mask[None], w_s, 0.0)
    sv = jnp.einsum('gts,bnsgc->bntgc', w, v) + b_s.T[None, None, :, :, None]
    return u * sv.reshape(b_, s_, GM_WIDTH)


def setup_inputs(seed: int = 0) -> dict:
    key = jax.random.key(seed)
    ks = jax.random.split(key, 24)

    def nrm(k, shape, scale):
        return jax.random.normal(k, shape, jnp.float32) * scale

    L = DEPTH
    return {
        "x": nrm(ks[0], (BATCH, SEQ, D_MODEL), 1.0),
        "c": nrm(ks[1], (BATCH, D_MODEL), 1.0),
        "w_ada": nrm(ks[2], (L, D_MODEL, 6 * D_MODEL), D_MODEL ** -0.5),
        "b_ada": nrm(ks[3], (L, 6 * D_MODEL), 0.01),
        "w_in": nrm(ks[4], (L, D_MODEL, IN_WIDTH), D_MODEL ** -0.5),
        "b_gate": nrm(ks[5], (L, 2, D_MODEL), 0.01),
        "hgrn_lb_logits": nrm(ks[6], (L + 1, HG_FD), 0.1).at[L].add(LB_TAIL_LOGIT),
        "hgrn_norm_w": 1.0 + nrm(ks[7], (L, HG_DV), 0.02),
        "w_proj_a": nrm(ks[8], (L, HG_DI, D_MODEL), HG_DI ** -0.5),
        "gmlp_ln_w": 1.0 + nrm(ks[9], (L, GM_WIDTH), 0.02),
        "gmlp_ln_b": nrm(ks[10], (L, GM_WIDTH), 0.02),
        "gmlp_ws": nrm(ks[11], (L, GM_GROUPS, GM_BLOCK, GM_BLOCK), GM_BLOCK ** -0.5),
        "gmlp_bs": 1.0 + nrm(ks[12], (L, GM_GROUPS, GM_BLOCK), 0.1),
        "w_proj_b": nrm(ks[13], (L, GM_WIDTH, D_MODEL), GM_WIDTH ** -0.5),
        "w_out": nrm(ks[14], (L, D_MODEL, D_MODEL), DEEPNORM_BETA * D_MODEL ** -0.5),
        "ln1_w": 1.0 + nrm(ks[15], (L, D_MODEL), 0.02),
        "ln1_b": nrm(ks[16], (L, D_MODEL), 0.02),
        "w_ffn_in": nrm(ks[17], (L, D_MODEL, 2 * D_FF), D_MODEL ** -0.5),
        "w_ffn_out": nrm(ks[18], (L, D_FF, D_MODEL), DEEPNORM_BETA * D_FF ** -0.5),
        "ln2_w": 1.0 + nrm(ks[19], (L, D_MODEL), 0.02),
        "ln2_b": nrm(ks[20], (L, D_MODEL), 0.02),
    }


def reference(x, c, w_ada, b_ada, w_in, b_gate, hgrn_lb_logits, hgrn_norm_w, w_proj_a,
              gmlp_ln_w, gmlp_ln_b, gmlp_ws, gmlp_bs, w_proj_b, w_out, ln1_w, ln1_b,
              w_ffn_in, w_ffn_out, ln2_w, ln2_b):
    lower_bounds = jnp.cumsum(jax.nn.softmax(hgrn_lb_logits.astype(jnp.float32), axis=0), axis=0)
    h = x.astype(jnp.float32)
    cond = jax.nn.silu(c.astype(jnp.float32))
    for l in range(DEPTH):
        mod = cond @ w_ada[l] + b_ada[l]
        sh1, sc1, g1, sh2, sc2, g2 = [m[:, None, :] for m in jnp.split(mod, 6, axis=-1)]

        u = _layer_norm(h) * (1.0 + sc1) + sh1
        z = u @ w_in[l]
        zq, zf, zi, zg, zu, zv, zga, zgb = jnp.split(z, IN_SPLITS, axis=-1)
        y_a = hgrn2_mixer(zq, zf, zi, zg, lower_bounds[l], hgrn_norm_w[l]) @ w_proj_a[l]
        y_b = gmlp_mixer(zu, zv, gmlp_ln_w[l], gmlp_ln_b[l], gmlp_ws[l], gmlp_bs[l]) @ w_proj_b[l]
        gate_a = jax.nn.sigmoid(zga + b_gate[l, 0])
        gate_b = jax.nn.sigmoid(zgb + b_gate[l, 1])
        mix = (gate_a * y_a + gate_b * y_b) @ w_out[l]
        h = _layer_norm(DEEPNORM_ALPHA * h + g1 * mix) * ln1_w[l] + ln1_b[l]

        u2 = _layer_norm(h) * (1.0 + sc2) + sh2
        a, bb = jnp.split(u2 @ w_ffn_in[l], 2, axis=-1)
        ffn = (jax.nn.silu(a) * bb) @ w_ffn_out[l]
        h = _layer_norm(DEEPNORM_ALPHA * h + g2 * ffn) * ln2_w[l] + ln2_b[l]
    return h.astype(x.dtype)
```

```python
from contextlib import ExitStack

import ml_dtypes
import numpy as np

import concourse.bass as bass
import concourse.mybir as mybir
from concourse.bass_utils import run_bass_kernel_spmd

F32 = mybir.dt.float32
BF16 = mybir.dt.bfloat16
AF = mybir.ActivationFunctionType
ALU = mybir.AluOpType

P = 128
D = 1024
T = 512
NB = T // P
NCH = T // 64
H = 8
DFF = 2816
NFF = DFF // P
ALPHA = 2.0 ** 0.25
LN_EPS = 1e-5
RMS_EPS = 1e-6
NROWS = 81
NSLOT = 5
RSTD_POW = True
N_CORES = 8
SEQ = 8192
BATCH = 4


class Buf:
    __slots__ = ("name", "w", "r")

    def __init__(self, name):
        self.name = name
        self.w = {}
        self.r = {}


class Chan:
    __slots__ = ("sem", "count", "key")

    def __init__(self, sem, key):
        self.sem = sem
        self.count = 0
        self.key = key


class Sched:
    def __init__(self):
        self.ops = []
        self.last = {}
        self.bar = {}

    def add(self, eng, fn, reads=(), writes=(), chan=None, skip_self=False):
        oid = len(self.ops)
        key = chan.key if chan is not None else eng
        deps = dict(self.bar)

        def need(evs):
            for k, o in evs.items():
                if skip_self and k == key:
                    continue
                if deps.get(k, -1) < o:
                    deps[k] = o

        for b in reads:
            need(b.w)
        for b in writes:
            need(b.w)
            need(b.r)
        val = None
        if chan is not None:
            chan.count += 1
            val = chan.count * 16
        self.ops.append(dict(eng=eng, fn=fn, deps=deps, chan=chan, key=key, marked=False, val=val))
        for b in reads:
            b.r[key] = oid
        for b in writes:
            b.w = {key: oid}
            b.r = {}
        self.last[key] = oid
        return oid

    def barrier(self, exclude=()):
        self.bar = {k: v for k, v in self.last.items() if k not in exclude}

    def emit(self, nc, sems):
        ops = self.ops
        for op in ops:
            for k, o in op["deps"].items():
                if k == "pe" and op["eng"] == "pe" and op["chan"] is None:
                    continue
                ops[o]["marked"] = True
        cnt = {}
        for op in ops:
            if op["chan"] is None and op["marked"]:
                cnt[op["key"]] = cnt.get(op["key"], 0) + 1
                op["val"] = cnt[op["key"]]
        by_eng = {}
        for op in ops:
            by_eng.setdefault(op["eng"], []).append(op)

        def body(engname):
            def f(e):
                waited = {}
                for op in by_eng.get(engname, []):
                    for k, o in sorted(op["deps"].items(), key=lambda kv: str(kv[0])):
                        if k == "pe" and engname == "pe" and op["chan"] is None:
                            continue
                        p = ops[o]
                        sem = p["chan"].sem if p["chan"] is not None else sems[p["key"]]
                        v = p["val"]
                        if waited.get(k, 0) >= v:
                            continue
                        e.wait_ge(sem, v)
                        waited[k] = v
                    if op["fn"] is None:
                        continue
                    ins = op["fn"](e)
                    if op["chan"] is not None:
                        ins.then_inc(op["chan"].sem, 16)
                    elif op["marked"]:
                        ins.then_inc(sems[engname], 1)

            return f

        with nc.Block() as block:
            block.tensor(body("pe"))
            block.scalar(body("act"))
            block.vector(body("dve"))
            block.gpsimd(body("pool"))
            block.sync(body("sp"))


def build(n_tiles, dumps=None):
    nc = bass.Bass("TRN2", target_bir_lowering=False)
    Tc = n_tiles * T
    es = ExitStack()
    S = Sched()
    dump_list = []

    def din(name, shape, dtype=F32):
        return nc.dram_tensor(name, list(shape), dtype, kind="ExternalInput").ap()

    x_main = din("x_main", [Tc, D])
    x_warm = din("x_warm", [Tc, D])
    rows_in = din("rows", [NROWS, P])
    flag_in = din("flag", [P, 1])
    w_ada = din("w_ada", [D, 6 * D])
    b_ada = din("b_ada", [1, 6 * D])
    w_in = din("w_in", [D, 8 * D])
    w_pa = din("w_proj_a", [D, D])
    w_pb = din("w_proj_b", [D, D])
    w_o = din("w_out", [D, D])
    w_fi = din("w_ffn_in", [D, 2 * DFF])
    w_fo = din("w_ffn_out", [DFF, D])
    lnb_in = din("gmlp_ln_b", [1, D])
    ws_in = din("gmlp_ws", [H, P, P])
    bs_in = din("gmlp_bs", [1, H * P])
    ln1w_in = din("ln1_w", [1, D])
    ln1b_in = din("ln1_b", [1, D])
    ln2w_in = din("ln2_w", [1, D])
    ln2b_in = din("ln2_b", [1, D])
    identf_in = din("ident_f", [P, P])
    identb_in = din("ident_b", [P, P], BF16)
    mask2_in = din("mask2", [P, P])
    m64_in = din("m64", [P, T])
    m32_in = din("m32", [P, T])
    out_d = nc.dram_tensor("out", [Tc, D], F32, kind="ExternalOutput").ap()

    wi_bf = nc.dram_tensor("wi_bf", [16, P, 8, 512], BF16).ap()
    wpa_bf = nc.dram_tensor("wpa_bf", [2, P, 8, 512], BF16).ap()
    wpb_bf = nc.dram_tensor("wpb_bf", [2, P, 8, 512], BF16).ap()
    wo_bf = nc.dram_tensor("wo_bf", [2, P, 8, 512], BF16).ap()
    wfi_bf = nc.dram_tensor("wfi_bf", [11, P, 8, 512], BF16).ap()
    wfo_bf = nc.dram_tensor("wfo_bf", [6, P, 8, 512], BF16).ap()

    nsem = [0]

    def new_sem(name):
        nsem[0] += 1
        return es.enter_context(nc.semaphore(name))

    sems = {k: new_sem("s_" + k) for k in ("pe", "act", "dve", "pool", "sp")}

    def new_chan(name):
        return Chan(new_sem("c_" + name), ("dma", name))

    def sb(name, shape, dtype=F32):
        return es.enter_context(nc.sbuf_tensor("sb_" + name, list(shape), dtype))

    def dma(q, out, in_, reads, writes, chan, skip_self=False):
        S.add(q, lambda e: e.dma_start(out=out, in_=in_), reads, writes, chan=chan, skip_self=skip_self)

    def mm(out, lhsT, rhs, start, stop, reads, writes):
        S.add("pe", lambda e: e.matmul(out, lhsT, rhs, start=start, stop=stop), reads, writes)

    def tr(out, in_, ident, reads, writes):
        S.add("pe", lambda e: e.transpose(out, in_, ident), reads, writes)

    def act(out, in_, func, reads, writes, scale=1.0, bias=0.0):
        S.add("act", lambda e: e.activation(out=out, in_=in_, func=func, bias=bias, scale=scale), reads, writes)

    def tt(eng, out, in0, in1, op, reads, writes):
        S.add(eng, lambda e: e.tensor_tensor(out=out, in0=in0, in1=in1, op=op), reads, writes)

    def ts(eng, out, in0, s1, s2, op0, op1, reads, writes):
        if op1 is None:
            S.add(eng, lambda e: e.tensor_scalar(out=out, in0=in0, scalar1=s1, scalar2=None, op0=op0), reads, writes)
        else:
            S.add(eng, lambda e: e.tensor_scalar(out=out, in0=in0, scalar1=s1, scalar2=s2, op0=op0, op1=op1),
                  reads, writes)

    def stt(out, in0, scalar, in1, op0, op1, reads, writes):
        S.add("dve", lambda e: e.scalar_tensor_tensor(out=out, in0=in0, scalar=scalar, in1=in1, op0=op0, op1=op1),
              reads, writes)

    def copy(eng, out, in_, reads, writes):
        if eng == "act":
            act(out, in_, AF.Copy, reads, writes)
        else:
            S.add(eng, lambda e: e.tensor_copy(out=out, in_=in_), reads, writes)

    def dump(name, ap, buf):
        if dumps is None or name not in dumps:
            return
        shape = list(ap.shape)
        d = nc.dram_tensor("dbg_" + name, shape, ap.dtype, kind="ExternalOutput").ap()
        ch = new_chan("dbg_" + name)
        b = Buf("dbgd_" + name)
        dma("pool", d, ap, [buf], [b], ch)
        dump_list.append(b)

    NBANK = 8
    banks = [es.enter_context(nc.psum_tensor("bank%d" % i, [P, 512], F32)) for i in range(NBANK)]
    bank_bufs = [Buf("bank%d" % i) for i in range(NBANK)]
    rings = {"all": [0, list(range(NBANK))], "z": [0, [0, 1, 2]], "m": [0, [3, 4, 5, 6, 7]]}

    def psum(pool="all"):
        r = rings[pool]
        i = r[1][r[0] % len(r[1])]
        r[0] += 1
        return banks[i], bank_bufs[i]

    ident_f = sb("ident_f", [P, P])
    ident_b = sb("ident_b", [P, P], BF16)
    mask2 = sb("mask2", [P, P])
    m64 = sb("m64", [P, T])
    m32 = sb("m32", [P, T])
    ones_f = sb("ones_f", [P, P])
    ones_b = sb("ones_b", [P, P], BF16)
    colv = sb("colv", [P, NROWS])
    modv = sb("modv", [P, 32])
    scp = sb("scp", [P, 16])
    lbv = sb("lbv", [P, 24])
    flag = sb("flag", [P, 1])
    wmT = sb("wmT", [P, H, P], BF16)
    r_sb = sb("r_sb", [P, H, P])
    g1_bc = sb("g1_bc", [P, D])
    g2_bc = sb("g2_bc", [P, D])
    ln1w_bc = sb("ln1w_bc", [P, D])
    ln1b_bc = sb("ln1b_bc", [P, D])
    ln2w_bc = sb("ln2w_bc", [P, D])
    ln2b_bc = sb("ln2b_bc", [P, D])
    carry = sb("carry", [P, H, P])
    B_const = Buf("consts")
    B_carry = [Buf("carry%d" % h) for h in range(H)]

    conv_A = Buf("conv_A")
    conv_B = Buf("conv_B")
    ch_A = new_chan("conv_A")
    ch_B = new_chan("conv_B")

    def w3(w):
        return w.rearrange("(kc p) n -> p kc n", p=P)

    pieces = {}

    deferred = []

    def conv(name, dst, srcs, grpbuf, ch, kcn=8):
        for src, co in srcs:
            ncols = src.shape[2]
            args = ("pool", dst[:, 0:kcn, co:co + ncols], src, [], [grpbuf], ch, True)
            if grpbuf is conv_A:
                dma(*args)
            else:
                deferred.append(args)
        pieces[name] = (dst, grpbuf, kcn)

    def issue_deferred(n):
        for _ in range(min(n, len(deferred))):
            dma(*deferred.pop(0))

    NAMES_IN = ["q0", "q1", "f0", "f1", "i0", "i1", "g0", "g1", "u0", "u1", "v0", "v1", "ga0", "ga1", "gb0", "gb1"]
    first = ["f0", "f1", "i0", "i1"]
    order = first + [n for n in NAMES_IN if n not in first]
    for n in order:
        j = NAMES_IN.index(n)
        grp = (conv_A, ch_A) if n in first else (conv_B, ch_B)
        conv(n, wi_bf[j], [(w3(w_in)[:, :, j * 512:(j + 1) * 512], 0)], *grp)
    for j in range(2):
        conv("pa%d" % j, wpa_bf[j], [(w3(w_pa)[:, :, j * 512:(j + 1) * 512], 0)], conv_B, ch_B)
        conv("pb%d" % j, wpb_bf[j], [(w3(w_pb)[:, :, j * 512:(j + 1) * 512], 0)], conv_B, ch_B)
    for j in range(2):
        conv("o%d" % j, wo_bf[j], [(w3(w_o)[:, :, j * 512:(j + 1) * 512], 0)], conv_B, ch_B)
    for m in range(11):
        conv("fi%d" % m, wfi_bf[m],
             [(w3(w_fi)[:, :, 256 * m:256 * (m + 1)], 0),
              (w3(w_fi)[:, :, DFF + 256 * m:DFF + 256 * (m + 1)], 256)], conv_B, ch_B)
    for kg in range(3):
        k0, k1 = kg * 8, min(NFF, kg * 8 + 8)
        for half in range(2):
            conv("fo%d_%d" % (kg, half), wfo_bf[kg * 2 + half],
                 [(w3(w_fo)[:, k0:k1, half * 512:(half + 1) * 512], 0)], conv_B, ch_B, kcn=k1 - k0)

    ch_c = new_chan("consts")
    for dst, src in ((ident_f, identf_in), (ident_b, identb_in), (mask2, mask2_in), (m64, m64_in), (m32, m32_in),
                     (flag, flag_in)):
        dma("sp", dst[:], src, [], [B_const], ch_c, skip_self=True)
    for dst, src in ((ln1w_bc, ln1w_in), (ln1b_bc, ln1b_in), (ln2w_bc, ln2w_in), (ln2b_bc, ln2b_in)):
        dma("sp", dst[:], src[0:1, :].broadcast_to([P, D]), [], [B_const], ch_c, skip_self=True)

    with ExitStack() as ses:
        def ssb(name, shape, dtype=F32):
            return ses.enter_context(nc.sbuf_tensor("ss_" + name, list(shape), dtype))

        rows_sb = ssb("rows_sb", [NROWS, P])
        cond2 = ssb("cond2", [P, 8, 2])
        condB = ssb("condB", [P, 8, P])
        wa = [ssb("wa%d" % i, [P, 8, 512]) for i in range(2)]
        bada_bc = ssb("bada_bc", [P, 2, D])
        lnb_bc = ssb("lnb_bc", [P, D])
        ws_sb = ssb("ws_sb", [P, H, P])
        wmT_f = ssb("wmT_f", [P, H, P])
        bs_sb = ssb("bs_sb", [1, H * P])
        dl = ssb("dl", [P, 8])
        B_rows = Buf("rows")
        B_set = Buf("setup_small")
        B_wa = [Buf("wa0"), Buf("wa1")]
        ch_wa = [new_chan("wa0"), new_chan("wa1")]
        B_condB = Buf("condB")
        B_gm = Buf("gm")

        dma("sp", rows_sb[:], rows_in, [], [B_const], ch_c, skip_self=True)
        dma("sp", bada_bc[:, 0, :], b_ada[0:1, 2 * D:3 * D].broadcast_to([P, D]), [], [B_const], ch_c, skip_self=True)
        dma("sp", bada_bc[:, 1, :], b_ada[0:1, 5 * D:6 * D].broadcast_to([P, D]), [], [B_const], ch_c, skip_self=True)
        dma("sp", lnb_bc[:], lnb_in[0:1, :].broadcast_to([P, D]), [], [B_const], ch_c, skip_self=True)
        dma("sp", ws_sb[:], ws_in.rearrange("g t s -> t g s"), [], [B_const], ch_c, skip_self=True)
        dma("sp", bs_sb[:], bs_in, [], [B_const], ch_c, skip_self=True)

        S.add("dve", lambda e: e.memset(ones_f[:], 1.0), [], [B_const])
        S.add("dve", lambda e: e.memset(ones_b[:], 1.0), [], [B_const])
        S.add("dve", lambda e: e.memset(carry[:], 0.0), [], B_carry)
        for i in range(NBANK):
            S.add("dve", (lambda i: lambda e: e.memset(banks[i][:], 0.0))(i), [], [bank_bufs[i]])

        pb, pbuf = psum()
        tr(pb[:, 0:NROWS], rows_sb[:], ident_f[0:NROWS, 0:NROWS], [B_const], [pbuf])
        copy("dve", colv[:], pb[:, 0:NROWS], [pbuf], [B_set])
        act(cond2[:, :, 0], colv[:, 0:8], AF.Silu, [B_set], [B_condB])
        act(cond2[:, :, 1], colv[:, 0:8], AF.Silu, [B_set], [B_condB])
        for kc in range(8):
            ts("dve", condB[:, kc, :], ones_f[:], cond2[:, kc, 0:1], None, ALU.mult, None, [B_condB, B_const],
               [B_condB])
        tt("dve", dl[:], colv[:, 32:40], colv[:, 40:48], ALU.subtract, [B_set], [B_set])
        act(lbv[:, 0:8], dl[:], AF.Sigmoid, [B_set], [B_set])
        ts("dve", lbv[:, 8:16], lbv[:, 0:8], -1.0, 1.0, ALU.mult, ALU.add, [B_set], [B_set])
        ts("dve", lbv[:, 16:24], lbv[:, 0:8], -1.0, None, ALU.add, None, [B_set], [B_set])

        modp, modpbuf = psum()
        modp_v = modp[:, 0:64].rearrange("p (a b) -> p a b", b=2)
        wa3 = w3(w_ada)
        col_of = {0: 0, 1: 8, 3: 16, 4: 24}
        for ct in range(12):
            gi = ct // 2
            sl = ct % 2
            dma("sp", wa[sl][:], wa3[:, :, ct * 512:(ct + 1) * 512], [], [B_wa[sl]], ch_wa[sl])
            if gi in col_of:
                for jb in range(4):
                    col = col_of[gi] + (ct % 2) * 4 + jb
                    for kc in range(8):
                        mm(modp_v[:, col, :], wa[sl][:, kc, jb * P:(jb + 1) * P], cond2[:, kc, :], kc == 0, kc == 7,
                           [B_wa[sl], B_condB], [modpbuf])
            else:
                gb_, gbuf = psum()
                for kc in range(8):
                    mm(gb_[:], condB[:, kc, :], wa[sl][:, kc, :], kc == 0, kc == 7, [B_wa[sl], B_condB], [gbuf])
                gdst = g1_bc if gi == 2 else g2_bc
                half = ct % 2
                tt("dve", gdst[:, half * 512:(half + 1) * 512], gb_[:],
                   bada_bc[:, 0 if gi == 2 else 1, half * 512:(half + 1) * 512], ALU.add, [gbuf, B_const], [B_const])
        tt("dve", modv[:], modp_v[:, 0:32, 0], colv[:, 49:81], ALU.add, [modpbuf, B_set], [B_set])
        ts("dve", scp[:, 0:8], modv[:, 8:16], 1.0, None, ALU.add, None, [B_set], [B_set])
        ts("dve", scp[:, 8:16], modv[:, 24:32], 1.0, None, ALU.add, None, [B_set], [B_set])

        for g in range(H):
            pb, pbuf = psum()
            tr(pb[:, 0:P], ws_sb[:, g, :], ident_f[:], [B_const], [pbuf])
            copy("dve", wmT_f[:, g, :], pb[:, 0:P], [pbuf], [B_gm])
        S.add("dve", lambda e: e.memset(wmT_f[64:128, :, 0:64], 0.0), [], [B_gm])
        copy("dve", wmT[:], wmT_f[:], [B_gm], [B_const])
        for g in range(H):
            pb, pbuf = psum()
            mm(pb[:, 0:P], lnb_bc[:, g * P:(g + 1) * P], wmT_f[:, g, :], True, False, [B_const, B_gm], [pbuf])
            mm(pb[:, 0:P], ones_f[0:1, :], bs_sb[0:1, g * P:(g + 1) * P], False, True, [B_const], [pbuf])
            copy("dve", r_sb[:, g, :], pb[:, 0:P], [pbuf], [B_const])
        S.barrier(exclude=(ch_A.key, ch_B.key))

    sh1 = modv[:, 0:8]
    sc1p = scp[:, 0:8]
    sh2 = modv[:, 16:24]
    sc2p = scp[:, 8:16]
    lb = lbv[:, 0:8]
    oml = lbv[:, 8:16]
    noml = lbv[:, 16:24]
    normw = colv[:, 48:49]
    lnw_col = colv[:, 8:16]
    bga = colv[:, 16:24]
    bgb = colv[:, 24:32]

    h_sb = sb("h_sb", [P, NB, D])
    arena1 = sb("arena1", [P, 6 * 8 * T], BF16)

    def a1(i, a):
        return arena1[:, i * 8 * T:(i + 1) * 8 * T].rearrange("p (a b) -> p a b", a=a)

    uT = a1(0, 8)
    v_sb = a1(1, NB)
    vn_sb = a1(2, NB)
    uG = a1(3, H)
    hgT = a1(4, H)
    ubT = a1(5, H)
    u2T = a1(0, 8)
    actT = arena1[:, 8 * T:(8 + NFF) * T].rearrange("p (a b) -> p a b", a=NFF)
    wslots = [sb("wslot%d" % i, [P, 8, 512], BF16) for i in range(NSLOT)]
    tbig = sb("tbig", [P, 4, T])
    tq, tsg, tgt, tlf = tbig[:, 0, :], tbig[:, 1, :], tbig[:, 2, :], tbig[:, 3, :]
    mixT = tbig[:, :, :].bitcast(BF16).rearrange("p a (c b) -> p (a c) b", b=T)
    tkk = sb("tkk", [P, T])
    tcum = sb("tcum", [P, T])
    tc32 = sb("tc32", [P, T])
    te1 = sb("te1", [P, T])
    te3 = sb("te3", [P, T])
    ten = te3
    trc = sb("trc", [P, T])
    td1 = sb("td1", [P, NCH, 32])
    qt2 = [sb("qt%d" % i, [P, T], BF16) for i in range(2)]
    qs2 = [sb("qs%d" % i, [P, T], BF16) for i in range(2)]
    kd2 = [sb("kd%d" % i, [P, T], BF16) for i in range(2)]
    K02 = [sb("K0_%d" % i, [P, NCH, 32], BF16) for i in range(2)]
    K12 = [sb("K1_%d" % i, [P, NCH, 64], BF16) for i in range(2)]
    dec2 = [sb("dec%d" % i, [P, NCH]) for i in range(2)]
    osq2 = [sb("osq%d" % i, [P, T], BF16) for i in range(2)]
    osb2 = [sb("osb%d" % i, [P, T]) for i in range(2)]
    kdT_sb = sb("kdT_sb", [P, NB, P], BF16)
    A_sb = sb("A_sb", [P, NB, P], BF16)
    Sst = sb("Sst", [P, NCH + 1, P])
    Sbf = sb("Sbf", [P, NCH, P], BF16)
    tokb = [arena1[:, 32 * T + i * D:32 * T + (i + 1) * D] for i in range(NB)]
    tok = [arena1[:, 40 * T + i * 2 * D:40 * T + (i + 1) * 2 * D].bitcast(F32) for i in range(2)]
    st6 = [sb("st6_%d" % i, [P, 12]) for i in range(NB)]
    mv = [sb("mv%d" % i, [P, 4]) for i in range(NB)]
    ftmp = [sb("ftmp%d" % i, [P, T]) for i in range(2)]
    ftmp2 = [sb("ftmp2_%d" % i, [P, T]) for i in range(2)]
    neghalf = sb("neghalf", [P, 1])
    S.add("pool", lambda e: e.memset(neghalf[:], -0.5), [], [B_const])

    B_h = [Buf("h%d" % b) for b in range(NB)]
    ch_h = [new_chan("h%d" % b) for b in range(NB)]
    B_uT = [Buf("uT%d" % b) for b in range(NB)]
    B_v = [[Buf("v%d_%d" % (b, j)) for j in range(2)] for b in range(NB)]
    B_vn = [Buf("vn%d" % b) for b in range(NB)]
    B_uG = [Buf("uG%d" % g) for g in range(H)]
    B_hgT = [Buf("hgT%d" % h) for h in range(H)]
    B_ubT = [Buf("ubT%d" % g) for g in range(H)]
    B_mixT = [Buf("mixT%d" % o) for o in range(8)]
    B_u2T = [Buf("u2T%d" % b) for b in range(NB)]
    B_actT = [Buf("actT%d" % j) for j in range(NFF)]
    B_slot = [Buf("slot%d" % i) for i in range(NSLOT)]
    ch_slot = [new_chan("slot%d" % i) for i in range(NSLOT)]
    B_t = {n: Buf("t_" + n) for n in ("q", "sg", "gt", "lf", "kk", "cum", "c32", "e1", "e3", "en", "rc", "d1", "qt",
                                     "qs", "kd", "K0", "K1a", "K1b", "kdT", "A", "osq", "Sst", "Sbf")}
    B_t["en"] = B_t["e3"]
    B_p = [{n: Buf("p%d_%s" % (i, n)) for n in ("qt", "qs", "kd", "K0", "K1a", "K1b", "dec", "osq", "osb")}
           for i in range(2)]

    B_tok = [Buf("tok0"), Buf("tok1")]
    B_tokb = [Buf("tokb%d" % i) for i in range(NB)]
    B_mv = [Buf("mv%d" % i) for i in range(NB)]
    B_ftmp = [Buf("ftmp0"), Buf("ftmp1")]
    B_ftmp2 = [Buf("ftmp2_0"), Buf("ftmp2_1")]
    gate3 = [ftmp2[0], ftmp2[1], tbig[:, 2, :]]
    B_gate3 = [B_ftmp2[0], B_ftmp2[1], B_t["gt"]]
    tokc = [0]
    ftc = [0]

    wmap = {}
    wnext = [0]

    def wload(name):
        if name in wmap:
            return
        i = wnext[0]
        wnext[0] = (i + 1) % NSLOT
        for k in [k for k, vv in wmap.items() if vv == i]:
            del wmap[k]
        wmap[name] = i
        src, gbuf, kcn = pieces[name]
        dma("sp", wslots[i][:, 0:kcn, :], src[:, 0:kcn, :], [gbuf], [B_slot[i]], ch_slot[i])

    wseq = [[]]

    def wget(name):
        wload(name)
        seq = wseq[0]
        if name in seq:
            i = seq.index(name)
            if i + 1 < len(seq):
                wload(seq[i + 1])
        i = wmap[name]
        return wslots[i], B_slot[i]

    def layernorm_stats_multi(items):
        for src_ap, src_bufs, k in items:
            S.add("dve", (lambda src_ap, k: lambda e: e.bn_stats(out=st6[k][:, 0:6], in_=src_ap[:, 0:512]))(src_ap, k),
                  src_bufs, [B_mv[k]])
            S.add("dve", (lambda src_ap, k: lambda e: e.bn_stats(out=st6[k][:, 6:12], in_=src_ap[:, 512:1024]))(src_ap, k),
                  src_bufs, [B_mv[k]])
        for src_ap, src_bufs, k in items:
            S.add("dve", (lambda k: lambda e: e.bn_aggr(out=mv[k][:, 0:2], in_=st6[k][:, 0:12]))(k), [B_mv[k]],
                  [B_mv[k]])
            ts("dve", mv[k][:, 2:3], mv[k][:, 1:2], LN_EPS, None, ALU.add, None, [B_mv[k]], [B_mv[k]])
        for src_ap, src_bufs, k in items:
            if RSTD_POW:
                tt("pool", mv[k][:, 2:3], mv[k][:, 2:3], neghalf[:], ALU.pow, [B_mv[k], B_const], [B_mv[k]])
            else:
                act(mv[k][:, 2:3], mv[k][:, 2:3], AF.Ln, [B_mv[k]], [B_mv[k]])
                act(mv[k][:, 2:3], mv[k][:, 2:3], AF.Exp, [B_mv[k]], [B_mv[k]], scale=-0.5)
        for src_ap, src_bufs, k in items:
            stt(mv[k][:, 3:4], mv[k][:, 0:1], -1.0, mv[k][:, 2:3], ALU.mult, ALU.mult, [B_mv[k]], [B_mv[k]])

    def layernorm_stats(src_ap, src_bufs, k):
        layernorm_stats_multi([(src_ap, src_bufs, k)])

    def to_featmajor(src_bf, src_bufs, dstT, dst_buf, blk, scale_cols, bias_cols):
        pb, pbuf = psum()
        pv = pb.bitcast(BF16)[:, :].rearrange("p (a b) -> p a b", a=8)
        for kc in range(8):
            tr(pv[:, kc, :], src_bf[:, kc * P:(kc + 1) * P], ident_b[:], src_bufs + [B_const], [pbuf])
        for kc in range(8):
            act(dstT[:, kc, blk * P:(blk + 1) * P], pv[:, kc, :], AF.Identity, [pbuf, B_const], [dst_buf],
                scale=scale_cols[:, kc:kc + 1], bias=bias_cols[:, kc:kc + 1])

    def load_x(src, ti):
        for b in range(NB):
            r0 = (ti * NB + b) * P
            dma("act", h_sb[:, b, :], src[r0:r0 + P, :], [], [B_h[b]], ch_h[b])

    def adaln_all(dstT, B_dst, scale_cols, bias_cols):
        layernorm_stats_multi([(h_sb[:, blk, :], [B_h[blk]], blk) for blk in range(NB)])
        for blk in range(NB):
            act(tokb[blk][:], h_sb[:, blk, :], AF.Identity, [B_h[blk], B_mv[blk]], [B_tokb[blk]],
                scale=mv[blk][:, 2:3], bias=mv[blk][:, 3:4])
        for j in range(4):
            pb, pbuf = psum()
            pv = pb.bitcast(BF16)[:, :].rearrange("p (k a b) -> p k a b", k=2, a=NB)
            for k2 in range(2):
                kc = 2 * j + k2
                for blk in range(NB):
                    tr(pv[:, k2, blk, :], tokb[blk][:, kc * P:(kc + 1) * P], ident_b[:], [B_tokb[blk], B_const],
                       [pbuf])
            for k2 in range(2):
                kc = 2 * j + k2
                act(dstT[:, kc, :], pv[:, k2, :, :].rearrange("p a b -> p (a b)"), AF.Identity, [pbuf, B_const],
                    B_dst, scale=scale_cols[:, kc:kc + 1], bias=bias_cols[:, kc:kc + 1])

    def proj_feat(wt, wbuf, off, xT, xbufs, pool="all"):
        pb, pbuf = psum(pool)
        for kc in range(8):
            mm(pb[:], wt[:, kc, off:off + P], xT[:, kc, :], kc == 0, kc == 7, [wbuf] + xbufs, [pbuf])
        return pb, pbuf

    def proj_tok(wt, wbuf, xT, xbuf, blk):
        pb, pbuf = psum()
        for kc in range(8):
            mm(pb[:], xT[:, kc, blk * P:(blk + 1) * P], wt[:, kc, :], kc == 0, kc == 7, [wbuf, xbuf], [pbuf])
        return pb, pbuf

    def make_v():
        for j in range(2):
            wt, wbuf = wget("i%d" % j)
            for blk in range(NB):
                pb, pbuf = proj_tok(wt, wbuf, uT, B_uT[blk], blk)
                copy("act" if blk % 2 == 0 else "dve", v_sb[:, blk, j * 512:(j + 1) * 512], pb[:], [pbuf],
                     [B_v[blk][j]])

    c3 = lambda t_: t_[:, :].rearrange("p (c s) -> p c s", s=64)

    def hgrn_head_z(h, full):
        j, off = h // 4, (h % 4) * P
        res = {}
        names = ("f", "q", "g") if full else ("f",)
        for nm in names:
            wt, wbuf = wget("%s%d" % (nm, j))
            res[nm] = proj_feat(wt, wbuf, off, uT, B_uT, pool="z")
            yield res

    def hgrn_head_ew(h, z, full):
        B = B_t
        p = h % 2
        Bp = B_p[p]
        qt, qs, kd, K0, K1, dec = qt2[p], qs2[p], kd2[p], K02[p], K12[p], dec2[p]
        tgt_, B_tgt = gate3[h % 3], B_gate3[h % 3]
        zf, zfb = z["f"]
        act(tsg[:], zf[:], AF.Sigmoid, [zfb], [B["sg"]])
        yield
        if full:
            zq, zqb = z["q"]
            zg, zgb = z["g"]
            act(tq[:], zq[:], AF.Silu, [zqb], [B["q"]])
            yield
            act(tgt_[:], zg[:], AF.Silu, [zgb], [B_tgt])
            yield
        act(tlf[:], tsg[:], AF.Ln, [B["sg"], B_const], [B["lf"]], scale=oml[:, h:h + 1], bias=lb[:, h:h + 1])
        yield
        S.add("dve", lambda e: e.tensor_tensor_scan(out=tcum[:], data0=m64[:], data1=tlf[:], initial=0.0,
                                                    op0=ALU.mult, op1=ALU.add), [B["lf"], B_const], [B["cum"]])
        yield
        ts("dve", tkk[:], tsg[:], noml[:, h:h + 1], oml[:, h:h + 1], ALU.mult, ALU.add, [B["sg"], B_const], [B["kk"]])
        yield
        act(te1[:], tcum[:], AF.Exp, [B["cum"]], [B["e1"]])
        yield
        cum3 = c3(tcum)
        tt("dve", c3(trc), cum3[:, :, 63:64].broadcast_to([P, NCH, 64]), cum3, ALU.subtract, [B["cum"]], [B["rc"]])
        yield
        act(trc[:], trc[:], AF.Exp, [B["rc"]], [B["rc"]])
        yield
        if full:
            S.add("dve", lambda e: e.tensor_tensor_scan(out=tc32[:], data0=m32[:], data1=tlf[:], initial=0.0,
                                                        op0=ALU.mult, op1=ALU.add), [B["lf"], B_const], [B["c32"]])
            yield
        copy("pool", dec[:], c3(te1)[:, :, 63], [B["e1"]], [Bp["dec"]])
        yield
        tt("dve", kd[:], tkk[:], trc[:], ALU.mult, [B["kk"], B["rc"]], [Bp["kd"]])
        yield
        if full:
            act(te3[:], tc32[:], AF.Exp, [B["c32"]], [B["e3"]])
            yield
            tt("dve", qt[:], tq[:], te1[:], ALU.mult, [B["q"], B["e1"]], [Bp["qt"]])
            yield
            tt("dve", qs[:], tq[:], te3[:], ALU.mult, [B["q"], B["e3"]], [Bp["qs"]])
            yield
            act(ten[:], tc32[:], AF.Exp, [B["c32"]], [B["en"]], scale=-1.0)
            yield
            kk3, en3 = c3(tkk), c3(ten)
            tt("dve", td1[:], cum3[:, :, 31:32].broadcast_to([P, NCH, 32]), cum3[:, :, 0:32], ALU.subtract,
               [B["cum"]], [B["d1"]])
            yield
            act(td1[:], td1[:], AF.Exp, [B["d1"]], [B["d1"]])
            yield
            tt("dve", K0[:], kk3[:, :, 0:32], en3[:, :, 0:32], ALU.mult, [B["kk"], B["en"]], [Bp["K0"]])
            yield
            tt("dve", K1[:, :, 32:64], kk3[:, :, 32:64], en3[:, :, 32:64], ALU.mult, [B["kk"], B["en"]], [Bp["K1b"]])
            yield
            tt("dve", K1[:, :, 0:32], kk3[:, :, 0:32], td1[:], ALU.mult, [B["kk"], B["d1"]], [Bp["K1a"]])
            yield

    def bank(i):
        return banks[i], bank_bufs[i]

    def hgrn_state_path(h, full):
        B = B_t
        p = h % 2
        Bp = B_p[p]
        kd, dec = kd2[p], dec2[p]
        vb_all = [B_v[b][h // 4] for b in range(NB)]
        vcol = slice(h * P, (h + 1) * P)
        pk, pkb = bank(3)
        pkv = pk.bitcast(BF16)[:, 0:NB * P].rearrange("p (a b) -> p a b", a=NB)
        for blk in range(NB):
            tr(pkv[:, blk, :], kd[:, blk * P:(blk + 1) * P], ident_b[:], [Bp["kd"], B_const], [pkb])
        yield
        copy("act", kdT_sb[:], pkv, [pkb], [B["kdT"]])
        yield
        Ue, Ueb = bank(4)
        Uo, Uob = bank(5)
        Uv = (Ue[:, :].rearrange("p (a b) -> p a b", a=NB), Uo[:, :].rearrange("p (a b) -> p a b", a=NB))
        Ub = (Ueb, Uob)
        for c in range(NCH):
            blk, po = c // 2, (c % 2) * 64
            mm(Uv[c % 2][:, blk, :], kdT_sb[po:po + 64, blk, :], v_sb[po:po + 64, blk, vcol], True, True,
               [B["kdT"], vb_all[blk]], [Ub[c % 2]])
        yield
        copy("pool", Sst[:, 0, :], carry[:, h, :], [B_carry[h]], [B["Sst"]])
        yield
        for c in range(NCH):
            stt(Sst[:, c + 1, :], Sst[:, c, :], dec[:, c:c + 1], Uv[c % 2][:, c // 2, :], ALU.mult,
                ALU.add, [B["Sst"], Bp["dec"], Ub[c % 2]], [B["Sst"]])
            yield
        copy("pool", carry[:, h, :], Sst[:, NCH, :], [B["Sst"]], [B_carry[h]])
        yield
        if full:
            copy("pool", Sbf[:], Sst[:, 0:NCH, :], [B["Sst"]], [B["Sbf"]])
            yield

    def hgrn_score_path(h):
        B = B_t
        p = h % 2
        Bp = B_p[p]
        qs, K0, K1 = qs2[p], K02[p], K12[p]
        pa, pab = bank(6)
        pav = pa[:, :].rearrange("p (a b) -> p a b", a=NB)
        S.add("dve", lambda e: e.memset(pa[:], 0.0), [], [pab])
        yield
        for c in range(NCH):
            blk, po = c // 2, (c % 2) * 64
            mm(pav[po:po + 32, blk, po:po + 32], K0[:, c, :], qs[:, c * 64:c * 64 + 32], True, True,
               [Bp["K0"], Bp["qs"]], [pab])
            mm(pav[po:po + 64, blk, po + 32:po + 64], K1[:, c, :], qs[:, c * 64 + 32:c * 64 + 64], True, True,
               [Bp["K1a"], Bp["K1b"], Bp["qs"]], [pab])
            if c % 4 == 3:
                yield
        mask_b = bass.AP(mask2[:].tensor, mask2[:].offset, [list(mask2[:].ap[0]), [0, NB], [1, P]])
        tt("dve", A_sb[:], pav, mask_b, ALU.mult, [pab, B_const], [B["A"]])
        yield

    def hgrn_head_small(h, full):
        B = B_t
        p = h % 2
        Bp = B_p[p]
        qt = qt2[p]
        vb_all = [B_v[b][h // 4] for b in range(NB)]
        vcol = slice(h * P, (h + 1) * P)
        active = [hgrn_state_path(h, full)] + ([hgrn_score_path(h)] if full else [])
        while active:
            for g in list(active):
                try:
                    next(g)
                    yield
                except StopIteration:
                    active.remove(g)
        if not full:
            return
        po_, pob = bank(3)
        for blk in range(NB):
            mm(po_[:, blk * P:(blk + 1) * P], v_sb[:, blk, vcol], A_sb[:, blk, :], True, False,
               [vb_all[blk], B["A"]], [pob])
            for c in (2 * blk, 2 * blk + 1):
                mm(po_[:, c * 64:(c + 1) * 64], Sbf[:, c, :], qt[:, c * 64:(c + 1) * 64], False, c % 2 == 1,
                   [B["Sbf"], Bp["qt"]], [pob])
            if blk % 2 == 1:
                yield
        copy("dve", osb2[p][:], po_[:], [pob], [Bp["osb"]])
        yield
        act(osq2[p][:], osb2[p][:], AF.Square, [Bp["osb"]], [Bp["osq"]])
        yield

    def hgrn_tail(h):
        p = h % 2
        Bp = B_p[p]
        tX = ftmp[p]
        tgt_, B_tgt = gate3[h % 3], B_gate3[h % 3]
        pss, pssb = bank(7)
        mm(pss[:], ones_b[:], osq2[p][:], True, True, [Bp["osq"], B_const], [pssb])
        yield
        act(tX[:], pss[:], AF.Ln, [pssb], [B_ftmp[p]], scale=1.0 / P, bias=RMS_EPS)
        yield
        act(tX[:], tX[:], AF.Exp, [B_ftmp[p]], [B_ftmp[p]], scale=-0.5)
        yield
        tt("dve", tX[:], osb2[p][:], tX[:], ALU.mult, [Bp["osb"], B_ftmp[p]], [B_ftmp[p]])
        yield
        stt(hgT[:, h, :], tX[:], normw, tgt_[:], ALU.mult, ALU.mult, [B_ftmp[p], B_tgt, B_const],
            [B_hgT[h]] + B_tokb)
        yield

    def delayed(gen, n):
        for _ in range(n):
            yield
        yield from gen

    def run_merged(gens):
        active = [g for g in gens if g is not None]
        while active:
            for g in list(active):
                try:
                    next(g)
                except StopIteration:
                    active.remove(g)

    def hgrn_all(full):
        zres = {}

        def zgen(h):
            last = None
            for last in hgrn_head_z(h, full):
                yield
            zres[h] = last

        run_merged([zgen(0)])
        for i in range(-1, H + 1):
            gens = []
            if full and 1 <= i <= H:
                gens.append(hgrn_tail(i - 1))
            if 0 <= i < H:
                gens.append(hgrn_head_small(i, full))
            if i + 1 < H:
                gens.append(hgrn_head_ew(i + 1, zres[i + 1], full))
            if i + 2 < H:
                gens.append(delayed(zgen(i + 2), 4 if full else 2))
            run_merged(gens)


    def gmlp():
        for g in range(H):
            wt, wbuf = wget("u%d" % (g // 4))
            pb, pbuf = proj_feat(wt, wbuf, (g % 4) * P, uT, B_uT)
            act(uG[:, g, :], pb[:], AF.Gelu, [pbuf], [B_uG[g]])
        for blk in range(NB):
            k = tokc[0] % 2
            tokc[0] += 1
            for j in range(2):
                wt, wbuf = wget("v%d" % j)
                pb, pbuf = proj_tok(wt, wbuf, uT, B_uT[blk], blk)
                act(tok[k][:, j * 512:(j + 1) * 512], pb[:], AF.Gelu, [pbuf], [B_tok[k]])
            layernorm_stats(tok[k], [B_tok[k]], k)
            ts("dve", vn_sb[:, blk, :], tok[k][:], mv[k][:, 0:1], mv[k][:, 2:3], ALU.subtract, ALU.mult,
               [B_tok[k], B_mv[k]], [B_vn[blk]])
        for g in range(H):
            pb, pbuf = psum()
            for blk in range(NB):
                mm(pb[:, blk * P:(blk + 1) * P], vn_sb[:, blk, g * P:(g + 1) * P], wmT[:, g, :], True, True,
                   [B_vn[blk], B_const], [pbuf])
            k = ftc[0] % 2
            ftc[0] += 1
            r_b = bass.AP(r_sb[:].tensor, r_sb[:, g, :].offset, [list(r_sb[:].ap[0]), [0, NB], [1, P]])
            stt(ftmp[k][:, :].rearrange("p (a b) -> p a b", a=NB), pb[:, :].rearrange("p (a b) -> p a b", a=NB),
                lnw_col[:, g:g + 1], r_b, ALU.mult, ALU.add, [pbuf, B_const], [B_ftmp[k]])
            tt("pool", ubT[:, g, :], ftmp[k][:], uG[:, g, :], ALU.mult, [B_ftmp[k], B_uG[g]],
               [B_ubT[g], B_tok[0], B_tok[1]])

    def mixer_out():
        for oc in range(8):
            j, off = oc // 4, (oc % 4) * P
            wa_, wab = wget("pa%d" % j)
            pya, pyab = proj_feat(wa_, wab, off, hgT, B_hgT)
            wb_, wbb = wget("pb%d" % j)
            pyb, pybb = proj_feat(wb_, wbb, off, ubT, B_ubT)
            wga, wgab = wget("ga%d" % j)
            pga, pgab = proj_feat(wga, wgab, off, uT, B_uT)
            wgb, wgbb = wget("gb%d" % j)
            pgb, pgbb = proj_feat(wgb, wgbb, off, uT, B_uT)
            k = ftc[0] % 2
            ftc[0] += 1
            act(ftmp[k][:], pga[:], AF.Sigmoid, [pgab, B_const], [B_ftmp[k]], bias=bga[:, oc:oc + 1])
            act(ftmp2[k][:], pgb[:], AF.Sigmoid, [pgbb, B_const], [B_ftmp2[k]], bias=bgb[:, oc:oc + 1])
            tt("dve", ftmp[k][:], ftmp[k][:], pya[:], ALU.mult, [B_ftmp[k], pyab], [B_ftmp[k]])
            tt("dve", ftmp2[k][:], ftmp2[k][:], pyb[:], ALU.mult, [B_ftmp2[k], pybb], [B_ftmp2[k]])
            tt("pool", mixT[:, oc, :], ftmp[k][:], ftmp2[k][:], ALU.add, [B_ftmp[k], B_ftmp2[k]],
               [B_mixT[oc], B_t["q"], B_t["sg"], B_t["gt"], B_t["lf"]])

    def post_ln_all(w_bc, b_bc):
        layernorm_stats_multi([(h_sb[:, blk, :], [B_h[blk]], blk) for blk in range(NB)])
        for blk in range(NB):
            act(h_sb[:, blk, :], h_sb[:, blk, :], AF.Identity, [B_h[blk], B_mv[blk]], [B_h[blk]],
                scale=mv[blk][:, 2:3], bias=mv[blk][:, 3:4])
        for blk in range(NB):
            tt("dve" if blk % 2 == 0 else "pool", h_sb[:, blk, :], h_sb[:, blk, :], w_bc[:], ALU.mult,
               [B_h[blk], B_const], [B_h[blk]])
        for blk in range(NB):
            tt("pool" if blk == 2 else "dve", h_sb[:, blk, :], h_sb[:, blk, :], b_bc[:], ALU.add,
               [B_h[blk], B_const], [B_h[blk]])

    def residual_update(pb, pbuf, blk, hs, g_bc):
        k = ftc[0] % 2
        ftc[0] += 1
        tt("dve", ftmp[k][:], pb[:], g_bc[:, hs], ALU.mult, [pbuf, B_const], [B_ftmp[k]])
        stt(h_sb[:, blk, hs], h_sb[:, blk, hs], ALPHA, ftmp[k][:], ALU.mult, ALU.add, [B_h[blk], B_ftmp[k]],
            [B_h[blk]])

    def mix_and_ln1():
        for blk in range(NB):
            for half in range(2):
                wt, wbuf = wget("o%d" % half)
                pb, pbuf = psum()
                for kc in range(8):
                    mm(pb[:], mixT[:, kc, blk * P:(blk + 1) * P], wt[:, kc, :], kc == 0, kc == 7,
                       [wbuf] + B_mixT, [pbuf])
                residual_update(pb, pbuf, blk, slice(half * 512, (half + 1) * 512), g1_bc)
        post_ln_all(ln1w_bc, ln1b_bc)
        adaln_all(u2T, B_u2T, sc2p, sh2)

    def ffn():
        for m in range(11):
            wt, wbuf = wget("fi%d" % m)
            for jj in range(2):
                j = 2 * m + jj
                pa_, pab_ = proj_feat(wt, wbuf, jj * P, u2T, B_u2T)
                pbb, pbbb = proj_feat(wt, wbuf, 256 + jj * P, u2T, B_u2T)
                k = ftc[0] % 2
                ftc[0] += 1
                act(ftmp2[k][:], pa_[:], AF.Silu, [pab_], [B_ftmp2[k]])
                tt("dve", actT[:, j, :], ftmp2[k][:], pbb[:], ALU.mult, [B_ftmp2[k], pbbb], [B_actT[j]])
        for half in range(2):
            hs = slice(half * 512, (half + 1) * 512)
            pbs = [psum() for _ in range(NB)]
            for kg in range(3):
                wt, wbuf = wget("fo%d_%d" % (kg, half))
                k0, k1 = kg * 8, min(NFF, kg * 8 + 8)
                for blk in range(NB):
                    for kc in range(k0, k1):
                        mm(pbs[blk][0][:], actT[:, kc, blk * P:(blk + 1) * P], wt[:, kc - k0, :], kc == 0,
                           kc == NFF - 1, [wbuf, B_actT[kc]], [pbs[blk][1]])
            for blk in range(NB):
                residual_update(pbs[blk][0], pbs[blk][1], blk, hs, g2_bc)

    B_out = [Buf("out%d" % b) for b in range(NB)]

    def ln2_and_store(ti):
        post_ln_all(ln2w_bc, ln2b_bc)
        for blk in range(NB):
            r0 = (ti * NB + blk) * P
            dma("act", out_d[r0:r0 + P, :], h_sb[:, blk, :], [B_h[blk]], [B_out[blk]], ch_h[blk])

    SEQ_WARM = ["i0", "i1", "f0", "f1"]
    SEQ_MAIN = (["i0", "i1", "q0", "f0", "g0", "q1", "f1", "g1", "u0", "u1", "v0", "v1",
                 "pa0", "pb0", "ga0", "gb0", "pa1", "pb1", "ga1", "gb1", "o0", "o1"]
                + ["fi%d" % m for m in range(11)]
                + ["fo%d_%d" % (kg, half) for half in range(2) for kg in range(3)])

    for ti in range(n_tiles):
        wseq[0] = SEQ_WARM + (SEQ_WARM[:1] if ti + 1 < n_tiles else SEQ_MAIN[:1])
        issue_deferred(len(deferred) if ti == n_tiles - 1 else -(-len(deferred) // (n_tiles - ti)))
        load_x(x_warm, ti)
        adaln_all(uT, B_uT, sc1p, sh1)
        make_v()
        hgrn_all(False)
    for h in range(H):
        ts("dve", carry[:, h, :], carry[:, h, :], flag[:, 0:1], None, ALU.mult, None, [B_carry[h], B_const],
           [B_carry[h]])
    dump("carry", carry[:], B_carry[H - 1])

    for ti in range(n_tiles):
        wseq[0] = SEQ_MAIN + (SEQ_MAIN[:1] if ti + 1 < n_tiles else [])
        load_x(x_main, ti)
        adaln_all(uT, B_uT, sc1p, sh1)
        if ti == 0:
            dump("uT", uT, B_uT[NB - 1])
        make_v()
        if ti == 0:
            dump("v", v_sb, B_v[NB - 1][1])
        hgrn_all(True)
        if ti == 0:
            dump("hgT", hgT, B_hgT[H - 1])
        gmlp()
        if ti == 0:
            dump("ubT", ubT, B_ubT[H - 1])
        mixer_out()
        if ti == 0:
            dump("mixT", mixT, B_mixT[7])
        wload("o0")
        wload("o1")
        S.barrier()
        mix_and_ln1()
        if ti == 0:
            dump("u2T", u2T, B_u2T[NB - 1])
            dump("h1", h_sb[:], B_h[NB - 1])
        ffn()
        ln2_and_store(ti)
        S.barrier()

    S.add("sp", None, B_out + dump_list, [])
    S.emit(nc, sems)
    es.close()
    return nc


_CONST_CACHE = {}


def _consts():
    if not _CONST_CACHE:
        s = np.arange(P)
        mask2 = ((s[:, None] // 64 == s[None, :] // 64) & (s[:, None] <= s[None, :])).astype(np.float32)
        t = np.arange(T)
        m64 = np.broadcast_to((t % 64 != 0).astype(np.float32), (P, T)).copy()
        m32 = np.broadcast_to((t % 32 != 0).astype(np.float32), (P, T)).copy()
        _CONST_CACHE.update(
            ident_f=np.eye(P, dtype=np.float32),
            ident_b=np.eye(P, dtype=np.float32).astype(ml_dtypes.bfloat16),
            mask2=mask2, m64=m64, m32=m32)
    return _CONST_CACHE


def make_in_maps(inp, n_tiles, seq):
    f = lambda a: np.ascontiguousarray(np.asarray(a, dtype=np.float32))
    x = f(inp["x"])
    Tc = n_tiles * T
    assert seq == 2 * Tc
    shared = dict(
        w_ada=f(inp["w_ada"][0]), b_ada=f(inp["b_ada"][0]).reshape(1, -1), w_in=f(inp["w_in"][0]),
        w_proj_a=f(inp["w_proj_a"][0]), w_proj_b=f(inp["w_proj_b"][0]), w_out=f(inp["w_out"][0]),
        w_ffn_in=f(inp["w_ffn_in"][0]), w_ffn_out=f(inp["w_ffn_out"][0]),
        gmlp_ln_b=f(inp["gmlp_ln_b"][0]).reshape(1, -1), gmlp_ws=f(inp["gmlp_ws"][0]),
        gmlp_bs=f(inp["gmlp_bs"][0]).reshape(1, -1),
        ln1_w=f(inp["ln1_w"][0]).reshape(1, -1), ln1_b=f(inp["ln1_b"][0]).reshape(1, -1),
        ln2_w=f(inp["ln2_w"][0]).reshape(1, -1), ln2_b=f(inp["ln2_b"][0]).reshape(1, -1),
        **_consts())
    b_ada = f(inp["b_ada"][0])
    maps = []
    for core in range(N_CORES):
        b, half = core // 2, core % 2
        rows = np.concatenate([
            f(inp["c"])[b].reshape(8, P),
            f(inp["gmlp_ln_w"][0]).reshape(8, P),
            f(inp["b_gate"][0, 0]).reshape(8, P),
            f(inp["b_gate"][0, 1]).reshape(8, P),
            f(inp["hgrn_lb_logits"][0]).reshape(8, P),
            f(inp["hgrn_lb_logits"][1]).reshape(8, P),
            f(inp["hgrn_norm_w"][0]).reshape(1, P),
            b_ada[0:D].reshape(8, P), b_ada[D:2 * D].reshape(8, P),
            b_ada[3 * D:4 * D].reshape(8, P), b_ada[4 * D:5 * D].reshape(8, P)], axis=0)
        m = dict(shared)
        m["x_main"] = np.ascontiguousarray(x[b, half * Tc:(half + 1) * Tc])
        m["x_warm"] = np.ascontiguousarray(x[b, 0:Tc])
        m["rows"] = np.ascontiguousarray(rows)
        m["flag"] = np.full((P, 1), float(half), np.float32)
        maps.append(m)
    return maps


_NC_CACHE = {}


def run(inp, n_tiles, seq, dumps=None):
    key = (n_tiles, tuple(sorted(dumps)) if dumps else None)
    if key not in _NC_CACHE:
        _NC_CACHE[key] = build(n_tiles, dumps)
    nc = _NC_CACHE[key]
    maps = make_in_maps(inp, n_tiles, seq)
    res = run_bass_kernel_spmd(nc, maps, core_ids=list(range(N_CORES)))
    Tc = n_tiles * T
    out = np.empty((BATCH, seq, D), np.float32)
    for core in range(N_CORES):
        b, half = core // 2, core % 2
        out[b, half * Tc:(half + 1) * Tc] = res.results[core]["out"]
    return out, res


def kernel(**inputs):
    out, _ = run(inputs, SEQ // 2 // T, SEQ)
    return out
```

```python
from contextlib import ExitStack

import ml_dtypes
import numpy as np

import concourse.bass as bass
import concourse.mybir as mybir
from concourse.bass_utils import run_bass_kernel_spmd

F32 = mybir.dt.float32
BF16 = mybir.dt.bfloat16
AF = mybir.ActivationFunctionType
ALU = mybir.AluOpType

P = 128
D = 1024
T = 512
NB = T // P
NCH = T // 64
H = 8
DFF = 2816
NFF = DFF // P
ALPHA = 2.0 ** 0.25
LN_EPS = 1e-5
RMS_EPS = 1e-6
NROWS = 81
NSLOT = 5
RSTD_POW = True
N_CORES = 8
SEQ = 8192
BATCH = 4


class Buf:
    __slots__ = ("name", "w", "r")

    def __init__(self, name):
        self.name = name
        self.w = {}
        self.r = {}


class Chan:
    __slots__ = ("sem", "count", "key")

    def __init__(self, sem, key):
        self.sem = sem
        self.count = 0
        self.key = key


class Sched:
    def __init__(self):
        self.ops = []
        self.last = {}
        self.bar = {}

    def add(self, eng, fn, reads=(), writes=(), chan=None, skip_self=False):
        oid = len(self.ops)
        key = chan.key if chan is not None else eng
        deps = dict(self.bar)

        def need(evs):
            for k, o in evs.items():
                if skip_self and k == key:
                    continue
                if deps.get(k, -1) < o:
                    deps[k] = o

        for b in reads:
            need(b.w)
        for b in writes:
            need(b.w)
            need(b.r)
        val = None
        if chan is not None:
            chan.count += 1
            val = chan.count * 16
        self.ops.append(dict(eng=eng, fn=fn, deps=deps, chan=chan, key=key, marked=False, val=val))
        for b in reads:
            b.r[key] = oid
        for b in writes:
            b.w = {key: oid}
            b.r = {}
        self.last[key] = oid
        return oid

    def barrier(self, exclude=()):
        self.bar = {k: v for k, v in self.last.items() if k not in exclude}

    def emit(self, nc, sems):
        ops = self.ops
        for op in ops:
            for k, o in op["deps"].items():
                if k == "pe" and op["eng"] == "pe" and op["chan"] is None:
                    continue
                ops[o]["marked"] = True
        cnt = {}
        for op in ops:
            if op["chan"] is None and op["marked"]:
                cnt[op["key"]] = cnt.get(op["key"], 0) + 1
                op["val"] = cnt[op["key"]]
        by_eng = {}
        for op in ops:
            by_eng.setdefault(op["eng"], []).append(op)

        def body(engname):
            def f(e):
                waited = {}
                for op in by_eng.get(engname, []):
                    for k, o in sorted(op["deps"].items(), key=lambda kv: str(kv[0])):
                        if k == "pe" and engname == "pe" and op["chan"] is None:
                            continue
                        p = ops[o]
                        sem = p["chan"].sem if p["chan"] is not None else sems[p["key"]]
                        v = p["val"]
                        if waited.get(k, 0) >= v:
                            continue
                        e.wait_ge(sem, v)
                        waited[k] = v
                    if op["fn"] is None:
                        continue
                    ins = op["fn"](e)
                    if op["chan"] is not None:
                        ins.then_inc(op["chan"].sem, 16)
                    elif op["marked"]:
                        ins.then_inc(sems[engname], 1)

            return f

        with nc.Block() as block:
            block.tensor(body("pe"))
            block.scalar(body("act"))
            block.vector(body("dve"))
            block.gpsimd(body("pool"))
            block.sync(body("sp"))


def build(n_tiles, dumps=None):
    nc = bass.Bass("TRN2", target_bir_lowering=False)
    Tc = n_tiles * T
    es = ExitStack()
    S = Sched()
    dump_list = []

    def din(name, shape, dtype=F32):
        return nc.dram_tensor(name, list(shape), dtype, kind="ExternalInput").ap()

    x_main = din("x_main", [Tc, D])
    x_warm = din("x_warm", [Tc, D])
    rows_in = din("rows", [NROWS, P])
    flag_in = din("flag", [P, 1])
    w_ada = din("w_ada", [D, 6 * D])
    b_ada = din("b_ada", [1, 6 * D])
    w_in = din("w_in", [D, 8 * D])
    w_pa = din("w_proj_a", [D, D])
    w_pb = din("w_proj_b", [D, D])
    w_o = din("w_out", [D, D])
    w_fi = din("w_ffn_in", [D, 2 * DFF])
    w_fo = din("w_ffn_out", [DFF, D])
    lnb_in = din("gmlp_ln_b", [1, D])
    ws_in = din("gmlp_ws", [H, P, P])
    bs_in = din("gmlp_bs", [1, H * P])
    ln1w_in = din("ln1_w", [1, D])
    ln1b_in = din("ln1_b", [1, D])
    ln2w_in = din("ln2_w", [1, D])
    ln2b_in = din("ln2_b", [1, D])
    identf_in = din("ident_f", [P, P])
    identb_in = din("ident_b", [P, P], BF16)
    mask2_in = din("mask2", [P, P])
    m64_in = din("m64", [P, T])
    m32_in = din("m32", [P, T])
    out_d = nc.dram_tensor("out", [Tc, D], F32, kind="ExternalOutput").ap()

    wi_bf = nc.dram_tensor("wi_bf", [16, P, 8, 512], BF16).ap()
    wpa_bf = nc.dram_tensor("wpa_bf", [2, P, 8, 512], BF16).ap()
    wpb_bf = nc.dram_tensor("wpb_bf", [2, P, 8, 512], BF16).ap()
    wo_bf = nc.dram_tensor("wo_bf", [2, P, 8, 512], BF16).ap()
    wfi_bf = nc.dram_tensor("wfi_bf", [11, P, 8, 512], BF16).ap()
    wfo_bf = nc.dram_tensor("wfo_bf", [6, P, 8, 512], BF16).ap()

    nsem = [0]

    def new_sem(name):
        nsem[0] += 1
        return es.enter_context(nc.semaphore(name))

    sems = {k: new_sem("s_" + k) for k in ("pe", "act", "dve", "pool", "sp")}

    def new_chan(name):
        return Chan(new_sem("c_" + name), ("dma", name))

    def sb(name, shape, dtype=F32):
        return es.enter_context(nc.sbuf_tensor("sb_" + name, list(shape), dtype))

    def dma(q, out, in_, reads, writes, chan, skip_self=False):
        S.add(q, lambda e: e.dma_start(out=out, in_=in_), reads, writes, chan=chan, skip_self=skip_self)

    def mm(out, lhsT, rhs, start, stop, reads, writes):
        S.add("pe", lambda e: e.matmul(out, lhsT, rhs, start=start, stop=stop), reads, writes)

    def tr(out, in_, ident, reads, writes):
        S.add("pe", lambda e: e.transpose(out, in_, ident), reads, writes)

    def act(out, in_, func, reads, writes, scale=1.0, bias=0.0):
        S.add("act", lambda e: e.activation(out=out, in_=in_, func=func, bias=bias, scale=scale), reads, writes)

    def tt(eng, out, in0, in1, op, reads, writes):
        S.add(eng, lambda e: e.tensor_tensor(out=out, in0=in0, in1=in1, op=op), reads, writes)

    def ts(eng, out, in0, s1, s2, op0, op1, reads, writes):
        if op1 is None:
            S.add(eng, lambda e: e.tensor_scalar(out=out, in0=in0, scalar1=s1, scalar2=None, op0=op0), reads, writes)
        else:
            S.add(eng, lambda e: e.tensor_scalar(out=out, in0=in0, scalar1=s1, scalar2=s2, op0=op0, op1=op1),
                  reads, writes)

    def stt(out, in0, scalar, in1, op0, op1, reads, writes):
        S.add("dve", lambda e: e.scalar_tensor_tensor(out=out, in0=in0, scalar=scalar, in1=in1, op0=op0, op1=op1),
              reads, writes)

    def copy(eng, out, in_, reads, writes):
        if eng == "act":
            act(out, in_, AF.Copy, reads, writes)
        else:
            S.add(eng, lambda e: e.tensor_copy(out=out, in_=in_), reads, writes)

    def dump(name, ap, buf):
        if dumps is None or name not in dumps:
            return
        shape = list(ap.shape)
        d = nc.dram_tensor("dbg_" + name, shape, ap.dtype, kind="ExternalOutput").ap()
        ch = new_chan("dbg_" + name)
        b = Buf("dbgd_" + name)
        dma("pool", d, ap, [buf], [b], ch)
        dump_list.append(b)

    NBANK = 8
    banks = [es.enter_context(nc.psum_tensor("bank%d" % i, [P, 512], F32)) for i in range(NBANK)]
    bank_bufs = [Buf("bank%d" % i) for i in range(NBANK)]
    rings = {"all": [0, list(range(NBANK))], "z": [0, [0, 1, 2]], "m": [0, [3, 4, 5, 6, 7]]}

    def psum(pool="all"):
        r = rings[pool]
        i = r[1][r[0] % len(r[1])]
        r[0] += 1
        return banks[i], bank_bufs[i]

    ident_f = sb("ident_f", [P, P])
    ident_b = sb("ident_b", [P, P], BF16)
    mask2 = sb("mask2", [P, P])
    m64 = sb("m64", [P, T])
    m32 = sb("m32", [P, T])
    ones_f = sb("ones_f", [P, P])
    ones_b = sb("ones_b", [P, P], BF16)
    colv = sb("colv", [P, NROWS])
    modv = sb("modv", [P, 32])
    scp = sb("scp", [P, 16])
    lbv = sb("lbv", [P, 24])
    flag = sb("flag", [P, 1])
    wmT = sb("wmT", [P, H, P], BF16)
    r_sb = sb("r_sb", [P, H, P])
    g1_bc = sb("g1_bc", [P, D])
    g2_bc = sb("g2_bc", [P, D])
    ln1w_bc = sb("ln1w_bc", [P, D])
    ln1b_bc = sb("ln1b_bc", [P, D])
    ln2w_bc = sb("ln2w_bc", [P, D])
    ln2b_bc = sb("ln2b_bc", [P, D])
    carry = sb("carry", [P, H, P])
    B_const = Buf("consts")
    B_carry = [Buf("carry%d" % h) for h in range(H)]

    conv_A = Buf("conv_A")
    conv_B = Buf("conv_B")
    ch_A = new_chan("conv_A")
    ch_B = new_chan("conv_B")

    def w3(w):
        return w.rearrange("(kc p) n -> p kc n", p=P)

    pieces = {}

    deferred = []

    def conv(name, dst, srcs, grpbuf, ch, kcn=8):
        for src, co in srcs:
            ncols = src.shape[2]
            args = ("pool", dst[:, 0:kcn, co:co + ncols], src, [], [grpbuf], ch, True)
            if grpbuf is conv_A:
                dma(*args)
            else:
                deferred.append(args)
        pieces[name] = (dst, grpbuf, kcn)

    def issue_deferred(n):
        for _ in range(min(n, len(deferred))):
            dma(*deferred.pop(0))

    NAMES_IN = ["q0", "q1", "f0", "f1", "i0", "i1", "g0", "g1", "u0", "u1", "v0", "v1", "ga0", "ga1", "gb0", "gb1"]
    first = ["f0", "f1", "i0", "i1"]
    order = first + [n for n in NAMES_IN if n not in first]
    for n in order:
        j = NAMES_IN.index(n)
        grp = (conv_A, ch_A) if n in first else (conv_B, ch_B)
        conv(n, wi_bf[j], [(w3(w_in)[:, :, j * 512:(j + 1) * 512], 0)], *grp)
    for j in range(2):
        conv("pa%d" % j, wpa_bf[j], [(w3(w_pa)[:, :, j * 512:(j + 1) * 512], 0)], conv_B, ch_B)
        conv("pb%d" % j, wpb_bf[j], [(w3(w_pb)[:, :, j * 512:(j + 1) * 512], 0)], conv_B, ch_B)
    for j in range(2):
        conv("o%d" % j, wo_bf[j], [(w3(w_o)[:, :, j * 512:(j + 1) * 512], 0)], conv_B, ch_B)
    for m in range(11):
        conv("fi%d" % m, wfi_bf[m],
             [(w3(w_fi)[:, :, 256 * m:256 * (m + 1)], 0),
              (w3(w_fi)[:, :, DFF + 256 * m:DFF + 256 * (m + 1)], 256)], conv_B, ch_B)
    for kg in range(3):
        k0, k1 = kg * 8, min(NFF, kg * 8 + 8)
        for half in range(2):
            conv("fo%d_%d" % (kg, half), wfo_bf[kg * 2 + half],
                 [(w3(w_fo)[:, k0:k1, half * 512:(half + 1) * 512], 0)], conv_B, ch_B, kcn=k1 - k0)

    ch_c = new_chan("consts")
    for dst, src in ((ident_f, identf_in), (ident_b, identb_in), (mask2, mask2_in), (m64, m64_in), (m32, m32_in),
                     (flag, flag_in)):
        dma("sp", dst[:], src, [], [B_const], ch_c, skip_self=True)
    for dst, src in ((ln1w_bc, ln1w_in), (ln1b_bc, ln1b_in), (ln2w_bc, ln2w_in), (ln2b_bc, ln2b_in)):
        dma("sp", dst[:], src[0:1, :].broadcast_to([P, D]), [], [B_const], ch_c, skip_self=True)

    with ExitStack() as ses:
        def ssb(name, shape, dtype=F32):
            return ses.enter_context(nc.sbuf_tensor("ss_" + name, list(shape), dtype))

        rows_sb = ssb("rows_sb", [NROWS, P])
        cond2 = ssb("cond2", [P, 8, 2])
        condB = ssb("condB", [P, 8, P])
        NWA = 6
        wa = [ssb("wa%d" % i, [P, 8, 512]) for i in range(NWA)]
        bada_bc = ssb("bada_bc", [P, 2, D])
        lnb_bc = ssb("lnb_bc", [P, D])
        ws_sb = ssb("ws_sb", [P, H, P])
        wmT_f = ssb("wmT_f", [P, H, P])
        bs_sb = ssb("bs_sb", [1, H * P])
        dl = ssb("dl", [P, 8])
        B_rows = Buf("rows")
        B_set = Buf("setup_small")
        B_wa = [Buf("wa%d" % i) for i in range(NWA)]
        ch_wa = [new_chan("wa%d" % i) for i in range(NWA)]
        B_condB = Buf("condB")
        B_gm = Buf("gm")

        dma("sp", rows_sb[:], rows_in, [], [B_const], ch_c, skip_self=True)
        dma("sp", bada_bc[:, 0, :], b_ada[0:1, 2 * D:3 * D].broadcast_to([P, D]), [], [B_const], ch_c, skip_self=True)
        dma("sp", bada_bc[:, 1, :], b_ada[0:1, 5 * D:6 * D].broadcast_to([P, D]), [], [B_const], ch_c, skip_self=True)
        dma("sp", lnb_bc[:], lnb_in[0:1, :].broadcast_to([P, D]), [], [B_const], ch_c, skip_self=True)
        dma("sp", ws_sb[:], ws_in.rearrange("g t s -> t g s"), [], [B_const], ch_c, skip_self=True)
        dma("sp", bs_sb[:], bs_in, [], [B_const], ch_c, skip_self=True)

        S.add("dve", lambda e: e.memset(ones_f[:], 1.0), [], [B_const])
        S.add("dve", lambda e: e.memset(ones_b[:], 1.0), [], [B_const])
        S.add("dve", lambda e: e.memset(carry[:], 0.0), [], B_carry)
        for i in range(NBANK):
            S.add("dve", (lambda i: lambda e: e.memset(banks[i][:], 0.0))(i), [], [bank_bufs[i]])

        pb, pbuf = psum()
        tr(pb[:, 0:NROWS], rows_sb[:], ident_f[0:NROWS, 0:NROWS], [B_const], [pbuf])
        copy("dve", colv[:], pb[:, 0:NROWS], [pbuf], [B_set])
        act(cond2[:, :, 0], colv[:, 0:8], AF.Silu, [B_set], [B_condB])
        act(cond2[:, :, 1], colv[:, 0:8], AF.Silu, [B_set], [B_condB])
        for kc in range(8):
            ts("dve", condB[:, kc, :], ones_f[:], cond2[:, kc, 0:1], None, ALU.mult, None, [B_condB, B_const],
               [B_condB])
        tt("dve", dl[:], colv[:, 32:40], colv[:, 40:48], ALU.subtract, [B_set], [B_set])
        act(lbv[:, 0:8], dl[:], AF.Sigmoid, [B_set], [B_set])
        ts("dve", lbv[:, 8:16], lbv[:, 0:8], -1.0, 1.0, ALU.mult, ALU.add, [B_set], [B_set])
        ts("dve", lbv[:, 16:24], lbv[:, 0:8], -1.0, None, ALU.add, None, [B_set], [B_set])

        modp, modpbuf = psum()
        modp_v = modp[:, 0:64].rearrange("p (a b) -> p a b", b=2)
        wa3 = w3(w_ada)
        col_of = {0: 0, 1: 8, 3: 16, 4: 24}
        for ct in range(12):
            gi = ct // 2
            sl = ct % NWA
            dma("sp", wa[sl][:], wa3[:, :, ct * 512:(ct + 1) * 512], [], [B_wa[sl]], ch_wa[sl])
            if gi in col_of:
                for jb in range(4):
                    col = col_of[gi] + (ct % 2) * 4 + jb
                    for kc in range(8):
                        mm(modp_v[:, col, :], wa[sl][:, kc, jb * P:(jb + 1) * P], cond2[:, kc, :], kc == 0, kc == 7,
                           [B_wa[sl], B_condB], [modpbuf])
            else:
                gb_, gbuf = psum()
                for kc in range(8):
                    mm(gb_[:], condB[:, kc, :], wa[sl][:, kc, :], kc == 0, kc == 7, [B_wa[sl], B_condB], [gbuf])
                gdst = g1_bc if gi == 2 else g2_bc
                half = ct % 2
                tt("dve", gdst[:, half * 512:(half + 1) * 512], gb_[:],
                   bada_bc[:, 0 if gi == 2 else 1, half * 512:(half + 1) * 512], ALU.add, [gbuf, B_const], [B_const])
        tt("dve", modv[:], modp_v[:, 0:32, 0], colv[:, 49:81], ALU.add, [modpbuf, B_set], [B_set])
        ts("dve", scp[:, 0:8], modv[:, 8:16], 1.0, None, ALU.add, None, [B_set], [B_set])
        ts("dve", scp[:, 8:16], modv[:, 24:32], 1.0, None, ALU.add, None, [B_set], [B_set])

        for g in range(H):
            pb, pbuf = psum()
            tr(pb[:, 0:P], ws_sb[:, g, :], ident_f[:], [B_const], [pbuf])
            copy("dve", wmT_f[:, g, :], pb[:, 0:P], [pbuf], [B_gm])
        S.add("dve", lambda e: e.memset(wmT_f[64:128, :, 0:64], 0.0), [], [B_gm])
        copy("dve", wmT[:], wmT_f[:], [B_gm], [B_const])
        for g in range(H):
            pb, pbuf = psum()
            mm(pb[:, 0:P], lnb_bc[:, g * P:(g + 1) * P], wmT_f[:, g, :], True, False, [B_const, B_gm], [pbuf])
            mm(pb[:, 0:P], ones_f[0:1, :], bs_sb[0:1, g * P:(g + 1) * P], False, True, [B_const], [pbuf])
            copy("dve", r_sb[:, g, :], pb[:, 0:P], [pbuf], [B_const])
        S.barrier(exclude=(ch_A.key, ch_B.key))

    sh1 = modv[:, 0:8]
    sc1p = scp[:, 0:8]
    sh2 = modv[:, 16:24]
    sc2p = scp[:, 8:16]
    lb = lbv[:, 0:8]
    oml = lbv[:, 8:16]
    noml = lbv[:, 16:24]
    normw = colv[:, 48:49]
    lnw_col = colv[:, 8:16]
    bga = colv[:, 16:24]
    bgb = colv[:, 24:32]

    h_sb = sb("h_sb", [P, NB, D])
    arena1 = sb("arena1", [P, 6 * 8 * T], BF16)

    def a1(i, a):
        return arena1[:, i * 8 * T:(i + 1) * 8 * T].rearrange("p (a b) -> p a b", a=a)

    uT = a1(0, 8)
    v_sb = a1(1, NB)
    vn_sb = a1(2, NB)
    uG = a1(3, H)
    hgT = a1(4, H)
    ubT = a1(5, H)
    u2T = a1(0, 8)
    actT = arena1[:, 8 * T:(8 + NFF) * T].rearrange("p (a b) -> p a b", a=NFF)
    sgall = arena1[:, 16 * T:32 * T].bitcast(F32).rearrange("p (a b) -> p a b", a=H)
    B_sgall = [Buf("sgall%d" % h) for h in range(H)]
    wslots = [sb("wslot%d" % i, [P, 8, 512], BF16) for i in range(NSLOT)]
    tbig = sb("tbig", [P, 4, T])
    tq, tsg, tgt, tlf = tbig[:, 0, :], tbig[:, 1, :], tbig[:, 2, :], tbig[:, 3, :]
    mixT = tbig[:, :, :].bitcast(BF16).rearrange("p a (c b) -> p (a c) b", b=T)
    tkk = sb("tkk", [P, T])
    tcum = sb("tcum", [P, T])
    tc32 = sb("tc32", [P, T])
    te1 = sb("te1", [P, T])
    te3 = sb("te3", [P, T])
    ten = te3
    trc = sb("trc", [P, T])
    td1 = sb("td1", [P, NCH, 32])
    qt2 = [sb("qt%d" % i, [P, T], BF16) for i in range(2)]
    qs2 = [sb("qs%d" % i, [P, T], BF16) for i in range(2)]
    kd2 = [sb("kd%d" % i, [P, T], BF16) for i in range(2)]
    K02 = [sb("K0_%d" % i, [P, NCH, 32], BF16) for i in range(2)]
    K12 = [sb("K1_%d" % i, [P, NCH, 64], BF16) for i in range(2)]
    dec2 = [sb("dec%d" % i, [P, NCH]) for i in range(2)]
    kdT_sb = sb("kdT_sb", [P, NB, P], BF16)
    A_sb = sb("A_sb", [P, NB, P], BF16)
    osq = sb("osq", [P, T], BF16)
    Sst = sb("Sst", [P, NCH + 1, P])
    Sbf = sb("Sbf", [P, NCH, P], BF16)
    tokb = [arena1[:, 32 * T + i * D:32 * T + (i + 1) * D] for i in range(NB)]
    tok = [arena1[:, 40 * T + i * 2 * D:40 * T + (i + 1) * 2 * D].bitcast(F32) for i in range(2)]
    st6 = [sb("st6_%d" % i, [P, 12]) for i in range(NB)]
    mv = [sb("mv%d" % i, [P, 4]) for i in range(NB)]
    ftmp = [sb("ftmp%d" % i, [P, T]) for i in range(2)]
    ftmp2 = [sb("ftmp2_%d" % i, [P, T]) for i in range(2)]
    neghalf = sb("neghalf", [P, 1])
    S.add("pool", lambda e: e.memset(neghalf[:], -0.5), [], [B_const])

    B_h = [Buf("h%d" % b) for b in range(NB)]
    ch_h = [new_chan("h%d" % b) for b in range(NB)]
    B_uT = [Buf("uT%d" % b) for b in range(NB)]
    B_v = [[Buf("v%d_%d" % (b, j)) for j in range(2)] for b in range(NB)]
    B_vn = [Buf("vn%d" % b) for b in range(NB)]
    B_uG = [Buf("uG%d" % g) for g in range(H)]
    B_hgT = [Buf("hgT%d" % h) for h in range(H)]
    B_ubT = [Buf("ubT%d" % g) for g in range(H)]
    B_mixT = [Buf("mixT%d" % o) for o in range(8)]
    B_u2T = [Buf("u2T%d" % b) for b in range(NB)]
    B_actT = [Buf("actT%d" % j) for j in range(NFF)]
    B_slot = [Buf("slot%d" % i) for i in range(NSLOT)]
    ch_slot = [new_chan("slot%d" % i) for i in range(NSLOT)]
    B_t = {n: Buf("t_" + n) for n in ("q", "sg", "gt", "lf", "kk", "cum", "c32", "e1", "e3", "en", "rc", "d1", "qt",
                                     "qs", "kd", "K0", "K1a", "K1b", "kdT", "A", "osq", "Sst", "Sbf")}
    B_t["en"] = B_t["e3"]
    B_p = [{n: Buf("p%d_%s" % (i, n)) for n in ("qt", "qs", "kd", "K0", "K1a", "K1b", "dec")} for i in range(2)]
    B_tok = [Buf("tok0"), Buf("tok1")]
    B_tokb = [Buf("tokb%d" % i) for i in range(NB)]
    B_mv = [Buf("mv%d" % i) for i in range(NB)]
    B_ftmp = [Buf("ftmp0"), Buf("ftmp1")]
    B_ftmp2 = [Buf("ftmp2_0"), Buf("ftmp2_1")]
    tokc = [0]
    ftc = [0]

    wmap = {}
    wnext = [0]

    def wload(name):
        if name in wmap:
            return
        i = wnext[0]
        wnext[0] = (i + 1) % NSLOT
        for k in [k for k, vv in wmap.items() if vv == i]:
            del wmap[k]
        wmap[name] = i
        src, gbuf, kcn = pieces[name]
        dma("sp", wslots[i][:, 0:kcn, :], src[:, 0:kcn, :], [gbuf], [B_slot[i]], ch_slot[i])

    wseq = [[]]

    def wget(name):
        wload(name)
        seq = wseq[0]
        if name in seq:
            i = seq.index(name)
            if i + 1 < len(seq):
                wload(seq[i + 1])
        i = wmap[name]
        return wslots[i], B_slot[i]

    def layernorm_stats_multi(items):
        for src_ap, src_bufs, k in items:
            S.add("dve", (lambda src_ap, k: lambda e: e.bn_stats(out=st6[k][:, 0:6], in_=src_ap[:, 0:512]))(src_ap, k),
                  src_bufs, [B_mv[k]])
            S.add("dve", (lambda src_ap, k: lambda e: e.bn_stats(out=st6[k][:, 6:12], in_=src_ap[:, 512:1024]))(src_ap, k),
                  src_bufs, [B_mv[k]])
        for src_ap, src_bufs, k in items:
            S.add("dve", (lambda k: lambda e: e.bn_aggr(out=mv[k][:, 0:2], in_=st6[k][:, 0:12]))(k), [B_mv[k]],
                  [B_mv[k]])
            ts("dve", mv[k][:, 2:3], mv[k][:, 1:2], LN_EPS, None, ALU.add, None, [B_mv[k]], [B_mv[k]])
        for src_ap, src_bufs, k in items:
            if RSTD_POW:
                tt("pool", mv[k][:, 2:3], mv[k][:, 2:3], neghalf[:], ALU.pow, [B_mv[k], B_const], [B_mv[k]])
            else:
                act(mv[k][:, 2:3], mv[k][:, 2:3], AF.Ln, [B_mv[k]], [B_mv[k]])
                act(mv[k][:, 2:3], mv[k][:, 2:3], AF.Exp, [B_mv[k]], [B_mv[k]], scale=-0.5)
        for src_ap, src_bufs, k in items:
            stt(mv[k][:, 3:4], mv[k][:, 0:1], -1.0, mv[k][:, 2:3], ALU.mult, ALU.mult, [B_mv[k]], [B_mv[k]])

    def layernorm_stats(src_ap, src_bufs, k):
        layernorm_stats_multi([(src_ap, src_bufs, k)])

    def to_featmajor(src_bf, src_bufs, dstT, dst_buf, blk, scale_cols, bias_cols):
        pb, pbuf = psum()
        pv = pb.bitcast(BF16)[:, :].rearrange("p (a b) -> p a b", a=8)
        for kc in range(8):
            tr(pv[:, kc, :], src_bf[:, kc * P:(kc + 1) * P], ident_b[:], src_bufs + [B_const], [pbuf])
        for kc in range(8):
            act(dstT[:, kc, blk * P:(blk + 1) * P], pv[:, kc, :], AF.Identity, [pbuf, B_const], [dst_buf],
                scale=scale_cols[:, kc:kc + 1], bias=bias_cols[:, kc:kc + 1])

    def load_x(src, ti):
        for b in range(NB):
            r0 = (ti * NB + b) * P
            dma("act", h_sb[:, b, :], src[r0:r0 + P, :], [], [B_h[b]], ch_h[b])

    def adaln_all(dstT, B_dst, scale_cols, bias_cols):
        layernorm_stats_multi([(h_sb[:, blk, :], [B_h[blk]], blk) for blk in range(NB)])
        for blk in range(NB):
            act(tokb[blk][:], h_sb[:, blk, :], AF.Identity, [B_h[blk], B_mv[blk]], [B_tokb[blk]],
                scale=mv[blk][:, 2:3], bias=mv[blk][:, 3:4])
        for j in range(4):
            pb, pbuf = psum()
            pv = pb.bitcast(BF16)[:, :].rearrange("p (k a b) -> p k a b", k=2, a=NB)
            for k2 in range(2):
                kc = 2 * j + k2
                for blk in range(NB):
                    tr(pv[:, k2, blk, :], tokb[blk][:, kc * P:(kc + 1) * P], ident_b[:], [B_tokb[blk], B_const],
                       [pbuf])
            for k2 in range(2):
                kc = 2 * j + k2
                act(dstT[:, kc, :], pv[:, k2, :, :].rearrange("p a b -> p (a b)"), AF.Identity, [pbuf, B_const],
                    B_dst, scale=scale_cols[:, kc:kc + 1], bias=bias_cols[:, kc:kc + 1])

    def proj_feat(wt, wbuf, off, xT, xbufs, pool="all"):
        pb, pbuf = psum(pool)
        for kc in range(8):
            mm(pb[:], wt[:, kc, off:off + P], xT[:, kc, :], kc == 0, kc == 7, [wbuf] + xbufs, [pbuf])
        return pb, pbuf

    def proj_tok(wt, wbuf, xT, xbuf, blk):
        pb, pbuf = psum()
        for kc in range(8):
            mm(pb[:], xT[:, kc, blk * P:(blk + 1) * P], wt[:, kc, :], kc == 0, kc == 7, [wbuf, xbuf], [pbuf])
        return pb, pbuf

    def make_v():
        for j in range(2):
            wt, wbuf = wget("i%d" % j)
            for blk in range(NB):
                pb, pbuf = proj_tok(wt, wbuf, uT, B_uT[blk], blk)
                copy("act" if blk % 2 == 0 else "dve", v_sb[:, blk, j * 512:(j + 1) * 512], pb[:], [pbuf],
                     [B_v[blk][j]])

    c3 = lambda t_: t_[:, :].rearrange("p (c s) -> p c s", s=64)

    def hgrn_head_z(h, full):
        j, off = h // 4, (h % 4) * P
        res = {}
        names = ("f", "q", "g") if full else ("f",)
        for nm in names:
            wt, wbuf = wget("%s%d" % (nm, j))
            res[nm] = proj_feat(wt, wbuf, off, uT, B_uT, pool="z")
            yield res

    def hgrn_head_ew(h, z, full, pre_sg=False):
        B = dict(B_t)
        p = h % 2
        Bp = B_p[p]
        qt, qs, kd, K0, K1, dec = qt2[p], qs2[p], kd2[p], K02[p], K12[p], dec2[p]
        tgt_ = ftmp2[p]
        tsg = tbig[:, 1, :]
        if pre_sg:
            tsg = sgall[:, h, :]
            B["sg"] = B_sgall[h]
        else:
            zf, zfb = z["f"]
            act(tsg[:], zf[:], AF.Sigmoid, [zfb], [B["sg"]])
            yield
        if full:
            zq, zqb = z["q"]
            zg, zgb = z["g"]
            act(tq[:], zq[:], AF.Silu, [zqb], [B["q"]])
            yield
            act(tgt_[:], zg[:], AF.Silu, [zgb], [B_ftmp2[p]])
            yield
        act(tlf[:], tsg[:], AF.Ln, [B["sg"], B_const], [B["lf"]], scale=oml[:, h:h + 1], bias=lb[:, h:h + 1])
        yield
        S.add("dve", lambda e: e.tensor_tensor_scan(out=tcum[:], data0=m64[:], data1=tlf[:], initial=0.0,
                                                    op0=ALU.mult, op1=ALU.add), [B["lf"], B_const], [B["cum"]])
        yield
        ts("dve", tkk[:], tsg[:], noml[:, h:h + 1], oml[:, h:h + 1], ALU.mult, ALU.add, [B["sg"], B_const], [B["kk"]])
        yield
        act(te1[:], tcum[:], AF.Exp, [B["cum"]], [B["e1"]])
        yield
        cum3 = c3(tcum)
        tt("dve", c3(trc), cum3[:, :, 63:64].broadcast_to([P, NCH, 64]), cum3, ALU.subtract, [B["cum"]], [B["rc"]])
        yield
        act(trc[:], trc[:], AF.Exp, [B["rc"]], [B["rc"]])
        yield
        if full:
            S.add("dve", lambda e: e.tensor_tensor_scan(out=tc32[:], data0=m32[:], data1=tlf[:], initial=0.0,
                                                        op0=ALU.mult, op1=ALU.add), [B["lf"], B_const], [B["c32"]])
            yield
        copy("pool", dec[:], c3(te1)[:, :, 63], [B["e1"]], [Bp["dec"]])
        yield
        tt("dve", kd[:], tkk[:], trc[:], ALU.mult, [B["kk"], B["rc"]], [Bp["kd"]])
        yield
        if full:
            act(te3[:], tc32[:], AF.Exp, [B["c32"]], [B["e3"]])
            yield
            tt("dve", qt[:], tq[:], te1[:], ALU.mult, [B["q"], B["e1"]], [Bp["qt"]])
            yield
            tt("dve", qs[:], tq[:], te3[:], ALU.mult, [B["q"], B["e3"]], [Bp["qs"]])
            yield
            act(ten[:], tc32[:], AF.Exp, [B["c32"]], [B["en"]], scale=-1.0)
            yield
            kk3, en3 = c3(tkk), c3(ten)
            tt("dve", td1[:], cum3[:, :, 31:32].broadcast_to([P, NCH, 32]), cum3[:, :, 0:32], ALU.subtract,
               [B["cum"]], [B["d1"]])
            yield
            act(td1[:], td1[:], AF.Exp, [B["d1"]], [B["d1"]])
            yield
            tt("dve", K0[:], kk3[:, :, 0:32], en3[:, :, 0:32], ALU.mult, [B["kk"], B["en"]], [Bp["K0"]])
            yield
            tt("dve", K1[:, :, 32:64], kk3[:, :, 32:64], en3[:, :, 32:64], ALU.mult, [B["kk"], B["en"]], [Bp["K1b"]])
            yield
            tt("dve", K1[:, :, 0:32], kk3[:, :, 0:32], td1[:], ALU.mult, [B["kk"], B["d1"]], [Bp["K1a"]])
            yield

    def hgrn_head_small(h, full):
        B = B_t
        p = h % 2
        Bp = B_p[p]
        qt, qs, kd, K0, K1, dec = qt2[p], qs2[p], kd2[p], K02[p], K12[p], dec2[p]
        tgt_, tX = ftmp2[p], ftmp[p]
        vb_all = [B_v[b][h // 4] for b in range(NB)]
        vcol = slice(h * P, (h + 1) * P)
        pk, pkb = psum("m")
        pkv = pk.bitcast(BF16)[:, 0:NB * P].rearrange("p (a b) -> p a b", a=NB)
        for blk in range(NB):
            tr(pkv[:, blk, :], kd[:, blk * P:(blk + 1) * P], ident_b[:], [Bp["kd"], B_const], [pkb])
        yield
        copy("act", kdT_sb[:], pkv, [pkb], [B["kdT"]])
        yield
        Ue, Ueb = psum("m")
        Uo, Uob = psum("m")
        Uv = (Ue[:, :].rearrange("p (a b) -> p a b", a=NB), Uo[:, :].rearrange("p (a b) -> p a b", a=NB))
        Ub = (Ueb, Uob)
        for c in range(NCH):
            blk, po = c // 2, (c % 2) * 64
            mm(Uv[c % 2][:, blk, :], kdT_sb[po:po + 64, blk, :], v_sb[po:po + 64, blk, vcol], True, True,
               [B["kdT"], vb_all[blk]], [Ub[c % 2]])
        yield
        copy("pool", Sst[:, 0, :], carry[:, h, :], [B_carry[h]], [B["Sst"]])
        yield
        for c in range(NCH):
            stt(Sst[:, c + 1, :], Sst[:, c, :], dec[:, c:c + 1], Uv[c % 2][:, c // 2, :], ALU.mult,
                ALU.add, [B["Sst"], Bp["dec"], Ub[c % 2]], [B["Sst"]])
            yield
        copy("pool", carry[:, h, :], Sst[:, NCH, :], [B["Sst"]], [B_carry[h]])
        yield
        if not full:
            return
        copy("act", Sbf[:], Sst[:, 0:NCH, :], [B["Sst"]], [B["Sbf"]])
        yield
        pa, pab = psum("m")
        pav = pa[:, :].rearrange("p (a b) -> p a b", a=NB)
        S.add("dve", lambda e: e.memset(pa[:], 0.0), [], [pab])
        yield
        for c in range(NCH):
            blk, po = c // 2, (c % 2) * 64
            mm(pav[po:po + 32, blk, po:po + 32], K0[:, c, :], qs[:, c * 64:c * 64 + 32], True, True,
               [Bp["K0"], Bp["qs"]], [pab])
            mm(pav[po:po + 64, blk, po + 32:po + 64], K1[:, c, :], qs[:, c * 64 + 32:c * 64 + 64], True, True,
               [Bp["K1a"], Bp["K1b"], Bp["qs"]], [pab])
        yield
        mask_b = bass.AP(mask2[:].tensor, mask2[:].offset, [list(mask2[:].ap[0]), [0, NB], [1, P]])
        tt("dve", A_sb[:], pav, mask_b, ALU.mult, [pab, B_const], [B["A"]])
        yield
        po_, pob = psum("m")
        for blk in range(NB):
            mm(po_[:, blk * P:(blk + 1) * P], v_sb[:, blk, vcol], A_sb[:, blk, :], True, False,
               [vb_all[blk], B["A"]], [pob])
            for c in (2 * blk, 2 * blk + 1):
                mm(po_[:, c * 64:(c + 1) * 64], Sbf[:, c, :], qt[:, c * 64:(c + 1) * 64], False, c % 2 == 1,
                   [B["Sbf"], Bp["qt"]], [pob])
        yield
        act(osq[:], po_[:], AF.Square, [pob], [B["osq"]])
        yield
        pss, pssb = psum("m")
        mm(pss[:], ones_b[:], osq[:], True, True, [B["osq"], B_const], [pssb])
        yield
        act(tX[:], pss[:], AF.Ln, [pssb], [B_ftmp[p]], scale=1.0 / P, bias=RMS_EPS)
        yield
        act(tX[:], tX[:], AF.Exp, [B_ftmp[p]], [B_ftmp[p]], scale=-0.5)
        yield
        tt("dve", tX[:], po_[:], tX[:], ALU.mult, [pob, B_ftmp[p]], [B_ftmp[p]])
        yield
        stt(hgT[:, h, :], tX[:], normw, tgt_[:], ALU.mult, ALU.mult, [B_ftmp[p], B_ftmp2[p], B_const],
            [B_hgT[h]] + B_tokb)
        yield

    def delayed(gen, n):
        for _ in range(n):
            yield
        yield from gen

    def run_merged(gens):
        active = [g for g in gens if g is not None]
        while active:
            for g in list(active):
                try:
                    next(g)
                except StopIteration:
                    active.remove(g)

    def hgrn_all(full):
        zres = {}

        def zgen(h):
            last = None
            for last in hgrn_head_z(h, full):
                yield
            zres[h] = last

        if not full:
            for h in range(H):
                run_merged([zgen(h)])
                zf, zfb = zres[h]["f"]
                act(sgall[:, h, :], zf[:], AF.Sigmoid, [zfb], [B_sgall[h]])
            for i in range(-1, H):
                gens = []
                if i + 1 < H:
                    gens.append(hgrn_head_ew(i + 1, None, False, pre_sg=True))
                if i >= 0:
                    gens.append(hgrn_head_small(i, False))
                run_merged(gens)
            return
        run_merged([zgen(0)])
        for i in range(-1, H):
            gens = []
            if i + 1 < H:
                gens.append(hgrn_head_ew(i + 1, zres[i + 1], full))
            if i >= 0:
                gens.append(hgrn_head_small(i, full))
            if i + 2 <= H - 1:
                gens.append(delayed(zgen(i + 2), 4 if full else 2))
            run_merged(gens)


    def gmlp():
        for g in range(H):
            wt, wbuf = wget("u%d" % (g // 4))
            pb, pbuf = proj_feat(wt, wbuf, (g % 4) * P, uT, B_uT)
            act(uG[:, g, :], pb[:], AF.Gelu, [pbuf], [B_uG[g]])
        for blk in range(NB):
            k = tokc[0] % 2
            tokc[0] += 1
            for j in range(2):
                wt, wbuf = wget("v%d" % j)
                pb, pbuf = proj_tok(wt, wbuf, uT, B_uT[blk], blk)
                act(tok[k][:, j * 512:(j + 1) * 512], pb[:], AF.Gelu, [pbuf], [B_tok[k]])
            layernorm_stats(tok[k], [B_tok[k]], k)
            ts("dve", vn_sb[:, blk, :], tok[k][:], mv[k][:, 0:1], mv[k][:, 2:3], ALU.subtract, ALU.mult,
               [B_tok[k], B_mv[k]], [B_vn[blk]])
        for g in range(H):
            pb, pbuf = psum()
            for blk in range(NB):
                mm(pb[:, blk * P:(blk + 1) * P], vn_sb[:, blk, g * P:(g + 1) * P], wmT[:, g, :], True, True,
                   [B_vn[blk], B_const], [pbuf])
            k = ftc[0] % 2
            ftc[0] += 1
            r_b = bass.AP(r_sb[:].tensor, r_sb[:, g, :].offset, [list(r_sb[:].ap[0]), [0, NB], [1, P]])
            stt(ftmp[k][:, :].rearrange("p (a b) -> p a b", a=NB), pb[:, :].rearrange("p (a b) -> p a b", a=NB),
                lnw_col[:, g:g + 1], r_b, ALU.mult, ALU.add, [pbuf, B_const], [B_ftmp[k]])
            tt("pool", ubT[:, g, :], ftmp[k][:], uG[:, g, :], ALU.mult, [B_ftmp[k], B_uG[g]],
               [B_ubT[g], B_tok[0], B_tok[1]])

    def mixer_out():
        for oc in range(8):
            j, off = oc // 4, (oc % 4) * P
            wa_, wab = wget("pa%d" % j)
            pya, pyab = proj_feat(wa_, wab, off, hgT, B_hgT)
            wb_, wbb = wget("pb%d" % j)
            pyb, pybb = proj_feat(wb_, wbb, off, ubT, B_ubT)
            wga, wgab = wget("ga%d" % j)
            pga, pgab = proj_feat(wga, wgab, off, uT, B_uT)
            wgb, wgbb = wget("gb%d" % j)
            pgb, pgbb = proj_feat(wgb, wgbb, off, uT, B_uT)
            k = ftc[0] % 2
            ftc[0] += 1
            act(ftmp[k][:], pga[:], AF.Sigmoid, [pgab, B_const], [B_ftmp[k]], bias=bga[:, oc:oc + 1])
            act(ftmp2[k][:], pgb[:], AF.Sigmoid, [pgbb, B_const], [B_ftmp2[k]], bias=bgb[:, oc:oc + 1])
            tt("dve", ftmp[k][:], ftmp[k][:], pya[:], ALU.mult, [B_ftmp[k], pyab], [B_ftmp[k]])
            tt("dve", ftmp2[k][:], ftmp2[k][:], pyb[:], ALU.mult, [B_ftmp2[k], pybb], [B_ftmp2[k]])
            tt("pool", mixT[:, oc, :], ftmp[k][:], ftmp2[k][:], ALU.add, [B_ftmp[k], B_ftmp2[k]],
               [B_mixT[oc], B_t["q"], B_t["sg"], B_t["gt"], B_t["lf"]])

    def post_ln_all(w_bc, b_bc):
        layernorm_stats_multi([(h_sb[:, blk, :], [B_h[blk]], blk) for blk in range(NB)])
        for blk in range(NB):
            act(h_sb[:, blk, :], h_sb[:, blk, :], AF.Identity, [B_h[blk], B_mv[blk]], [B_h[blk]],
                scale=mv[blk][:, 2:3], bias=mv[blk][:, 3:4])
        for blk in range(NB):
            tt("dve" if blk % 2 == 0 else "pool", h_sb[:, blk, :], h_sb[:, blk, :], w_bc[:], ALU.mult,
               [B_h[blk], B_const], [B_h[blk]])
        for blk in range(NB):
            tt("pool" if blk == 2 else "dve", h_sb[:, blk, :], h_sb[:, blk, :], b_bc[:], ALU.add,
               [B_h[blk], B_const], [B_h[blk]])

    def residual_update(pb, pbuf, blk, hs, g_bc):
        k = ftc[0] % 2
        ftc[0] += 1
        tt("dve", ftmp[k][:], pb[:], g_bc[:, hs], ALU.mult, [pbuf, B_const], [B_ftmp[k]])
        stt(h_sb[:, blk, hs], h_sb[:, blk, hs], ALPHA, ftmp[k][:], ALU.mult, ALU.add, [B_h[blk], B_ftmp[k]],
            [B_h[blk]])

    def mix_and_ln1():
        for blk in range(NB):
            for half in range(2):
                wt, wbuf = wget("o%d" % half)
                pb, pbuf = psum()
                for kc in range(8):
                    mm(pb[:], mixT[:, kc, blk * P:(blk + 1) * P], wt[:, kc, :], kc == 0, kc == 7,
                       [wbuf] + B_mixT, [pbuf])
                residual_update(pb, pbuf, blk, slice(half * 512, (half + 1) * 512), g1_bc)
        post_ln_all(ln1w_bc, ln1b_bc)
        adaln_all(u2T, B_u2T, sc2p, sh2)

    def ffn():
        for m in range(11):
            wt, wbuf = wget("fi%d" % m)
            for jj in range(2):
                j = 2 * m + jj
                pa_, pab_ = proj_feat(wt, wbuf, jj * P, u2T, B_u2T)
                pbb, pbbb = proj_feat(wt, wbuf, 256 + jj * P, u2T, B_u2T)
                k = ftc[0] % 2
                ftc[0] += 1
                act(ftmp2[k][:], pa_[:], AF.Silu, [pab_], [B_ftmp2[k]])
                tt("dve", actT[:, j, :], ftmp2[k][:], pbb[:], ALU.mult, [B_ftmp2[k], pbbb], [B_actT[j]])
        for half in range(2):
            hs = slice(half * 512, (half + 1) * 512)
            pbs = [psum() for _ in range(NB)]
            for kg in range(3):
                wt, wbuf = wget("fo%d_%d" % (kg, half))
                k0, k1 = kg * 8, min(NFF, kg * 8 + 8)
                for blk in range(NB):
                    for kc in range(k0, k1):
                        mm(pbs[blk][0][:], actT[:, kc, blk * P:(blk + 1) * P], wt[:, kc - k0, :], kc == 0,
                           kc == NFF - 1, [wbuf, B_actT[kc]], [pbs[blk][1]])
            for blk in range(NB):
                residual_update(pbs[blk][0], pbs[blk][1], blk, hs, g2_bc)

    B_out = [Buf("out%d" % b) for b in range(NB)]

    def ln2_and_store(ti):
        post_ln_all(ln2w_bc, ln2b_bc)
        for blk in range(NB):
            r0 = (ti * NB + blk) * P
            dma("act", out_d[r0:r0 + P, :], h_sb[:, blk, :], [B_h[blk]], [B_out[blk]], ch_h[blk])

    SEQ_WARM = ["i0", "i1", "f0", "f1"]
    SEQ_MAIN = (["i0", "i1", "q0", "f0", "g0", "q1", "f1", "g1", "u0", "u1", "v0", "v1",
                 "pa0", "pb0", "ga0", "gb0", "pa1", "pb1", "ga1", "gb1", "o0", "o1"]
                + ["fi%d" % m for m in range(11)]
                + ["fo%d_%d" % (kg, half) for half in range(2) for kg in range(3)])

    for ti in range(n_tiles):
        wseq[0] = SEQ_WARM + (SEQ_WARM[:1] if ti + 1 < n_tiles else SEQ_MAIN[:1])
        issue_deferred(len(deferred) if ti == n_tiles - 1 else -(-len(deferred) // (n_tiles - ti)))
        load_x(x_warm, ti)
        adaln_all(uT, B_uT, sc1p, sh1)
        make_v()
        hgrn_all(False)
    for h in range(H):
        ts("dve", carry[:, h, :], carry[:, h, :], flag[:, 0:1], None, ALU.mult, None, [B_carry[h], B_const],
           [B_carry[h]])
    dump("carry", carry[:], B_carry[H - 1])
    S.barrier(exclude=(ch_A.key, ch_B.key))

    for ti in range(n_tiles):
        wseq[0] = SEQ_MAIN + (SEQ_MAIN[:1] if ti + 1 < n_tiles else [])
        load_x(x_main, ti)
        adaln_all(uT, B_uT, sc1p, sh1)
        if ti == 0:
            dump("uT", uT, B_uT[NB - 1])
        make_v()
        if ti == 0:
            dump("v", v_sb, B_v[NB - 1][1])
        hgrn_all(True)
        if ti == 0:
            dump("hgT", hgT, B_hgT[H - 1])
        gmlp()
        if ti == 0:
            dump("ubT", ubT, B_ubT[H - 1])
        mixer_out()
        if ti == 0:
            dump("mixT", mixT, B_mixT[7])
        wload("o0")
        wload("o1")
        S.barrier()
        mix_and_ln1()
        if ti == 0:
            dump("u2T", u2T, B_u2T[NB - 1])
            dump("h1", h_sb[:], B_h[NB - 1])
        ffn()
        ln2_and_store(ti)
        S.barrier()

    S.add("sp", None, B_out + dump_list, [])
    S.emit(nc, sems)
    es.close()
    return nc


_CONST_CACHE = {}


def _consts():
    if not _CONST_CACHE:
        s = np.arange(P)
        mask2 = ((s[:, None] // 64 == s[None, :] // 64) & (s[:, None] <= s[None, :])).astype(np.float32)
        t = np.arange(T)
        m64 = np.broadcast_to((t % 64 != 0).astype(np.float32), (P, T)).copy()
        m32 = np.broadcast_to((t % 32 != 0).astype(np.float32), (P, T)).copy()
        _CONST_CACHE.update(
            ident_f=np.eye(P, dtype=np.float32),
            ident_b=np.eye(P, dtype=np.float32).astype(ml_dtypes.bfloat16),
            mask2=mask2, m64=m64, m32=m32)
    return _CONST_CACHE


def make_in_maps(inp, n_tiles, seq):
    f = lambda a: np.ascontiguousarray(np.asarray(a, dtype=np.float32))
    x = f(inp["x"])
    Tc = n_tiles * T
    assert seq == 2 * Tc
    shared = dict(
        w_ada=f(inp["w_ada"][0]), b_ada=f(inp["b_ada"][0]).reshape(1, -1), w_in=f(inp["w_in"][0]),
        w_proj_a=f(inp["w_proj_a"][0]), w_proj_b=f(inp["w_proj_b"][0]), w_out=f(inp["w_out"][0]),
        w_ffn_in=f(inp["w_ffn_in"][0]), w_ffn_out=f(inp["w_ffn_out"][0]),
        gmlp_ln_b=f(inp["gmlp_ln_b"][0]).reshape(1, -1), gmlp_ws=f(inp["gmlp_ws"][0]),
        gmlp_bs=f(inp["gmlp_bs"][0]).reshape(1, -1),
        ln1_w=f(inp["ln1_w"][0]).reshape(1, -1), ln1_b=f(inp["ln1_b"][0]).reshape(1, -1),
        ln2_w=f(inp["ln2_w"][0]).reshape(1, -1), ln2_b=f(inp["ln2_b"][0]).reshape(1, -1),
        **_consts())
    b_ada = f(inp["b_ada"][0])
    maps = []
    for core in range(N_CORES):
        b, half = core // 2, core % 2
        rows = np.concatenate([
            f(inp["c"])[b].reshape(8, P),
            f(inp["gmlp_ln_w"][0]).reshape(8, P),
            f(inp["b_gate"][0, 0]).reshape(8, P),
            f(inp["b_gate"][0, 1]).reshape(8, P),
            f(inp["hgrn_lb_logits"][0]).reshape(8, P),
            f(inp["hgrn_lb_logits"][1]).reshape(8, P),
            f(inp["hgrn_norm_w"][0]).reshape(1, P),
            b_ada[0:D].reshape(8, P), b_ada[D:2 * D].reshape(8, P),
            b_ada[3 * D:4 * D].reshape(8, P), b_ada[4 * D:5 * D].reshape(8, P)], axis=0)
        m = dict(shared)
        m["x_main"] = np.ascontiguousarray(x[b, half * Tc:(half + 1) * Tc])
        m["x_warm"] = np.ascontiguousarray(x[b, 0:Tc])
        m["rows"] = np.ascontiguousarray(rows)
        m["flag"] = np.full((P, 1), float(half), np.float32)
        maps.append(m)
    return maps


_NC_CACHE = {}


def run(inp, n_tiles, seq, dumps=None):
    key = (n_tiles, tuple(sorted(dumps)) if dumps else None)
    if key not in _NC_CACHE:
        _NC_CACHE[key] = build(n_tiles, dumps)
    nc = _NC_CACHE[key]
    maps = make_in_maps(inp, n_tiles, seq)
    res = run_bass_kernel_spmd(nc, maps, core_ids=list(range(N_CORES)))
    Tc = n_tiles * T
    out = np.empty((BATCH, seq, D), np.float32)
    for core in range(N_CORES):
        b, half = core // 2, core % 2
        out[b, half * Tc:(half + 1) * Tc] = res.results[core]["out"]
    return out, res


def kernel(**inputs):
    out, _ = run(inputs, SEQ // 2 // T, SEQ)
    return out
```

```python
from contextlib import ExitStack

import ml_dtypes
import numpy as np

import concourse.bass as bass
import concourse.mybir as mybir
from concourse.bass_utils import run_bass_kernel_spmd

F32 = mybir.dt.float32
BF16 = mybir.dt.bfloat16
AF = mybir.ActivationFunctionType
ALU = mybir.AluOpType

P = 128
D = 1024
T = 512
NB = T // P
NCH = T // 64
H = 8
DFF = 2816
NFF = DFF // P
ALPHA = 2.0 ** 0.25
LN_EPS = 1e-5
RMS_EPS = 1e-6
NROWS = 81
NSLOT = 5
RSTD_POW = True
MERGE_W = (1, 1)
N_CORES = 8
SEQ = 8192
BATCH = 4


class Buf:
    __slots__ = ("name", "w", "r")

    def __init__(self, name):
        self.name = name
        self.w = {}
        self.r = {}


class Chan:
    __slots__ = ("sem", "count", "key")

    def __init__(self, sem, key):
        self.sem = sem
        self.count = 0
        self.key = key


class Sched:
    def __init__(self):
        self.ops = []
        self.last = {}
        self.bar = {}

    def add(self, eng, fn, reads=(), writes=(), chan=None, skip_self=False):
        oid = len(self.ops)
        key = chan.key if chan is not None else eng
        deps = dict(self.bar)

        def need(evs):
            for k, o in evs.items():
                if skip_self and k == key:
                    continue
                if deps.get(k, -1) < o:
                    deps[k] = o

        for b in reads:
            need(b.w)
        for b in writes:
            need(b.w)
            need(b.r)
        val = None
        if chan is not None:
            chan.count += 1
            val = chan.count * 16
        self.ops.append(dict(eng=eng, fn=fn, deps=deps, chan=chan, key=key, marked=False, val=val))
        for b in reads:
            b.r[key] = oid
        for b in writes:
            b.w = {key: oid}
            b.r = {}
        self.last[key] = oid
        return oid

    def barrier(self, exclude=()):
        self.bar = {k: v for k, v in self.last.items() if k not in exclude}

    def emit(self, nc, sems):
        ops = self.ops
        for op in ops:
            for k, o in op["deps"].items():
                if k == "pe" and op["eng"] == "pe" and op["chan"] is None:
                    continue
                ops[o]["marked"] = True
        cnt = {}
        for op in ops:
            if op["chan"] is None and op["marked"]:
                cnt[op["key"]] = cnt.get(op["key"], 0) + 1
                op["val"] = cnt[op["key"]]
        by_eng = {}
        for op in ops:
            by_eng.setdefault(op["eng"], []).append(op)

        def body(engname):
            def f(e):
                waited = {}
                for op in by_eng.get(engname, []):
                    for k, o in sorted(op["deps"].items(), key=lambda kv: str(kv[0])):
                        if k == "pe" and engname == "pe" and op["chan"] is None:
                            continue
                        p = ops[o]
                        sem = p["chan"].sem if p["chan"] is not None else sems[p["key"]]
                        v = p["val"]
                        if waited.get(k, 0) >= v:
                            continue
                        e.wait_ge(sem, v)
                        waited[k] = v
                    if op["fn"] is None:
                        continue
                    ins = op["fn"](e)
                    if op["chan"] is not None:
                        ins.then_inc(op["chan"].sem, 16)
                    elif op["marked"]:
                        ins.then_inc(sems[engname], 1)

            return f

        with nc.Block() as block:
            block.tensor(body("pe"))
            block.scalar(body("act"))
            block.vector(body("dve"))
            block.gpsimd(body("pool"))
            block.sync(body("sp"))


def build(n_tiles, dumps=None):
    nc = bass.Bass("TRN2", target_bir_lowering=False)
    Tc = n_tiles * T
    es = ExitStack()
    S = Sched()
    dump_list = []

    def din(name, shape, dtype=F32):
        return nc.dram_tensor(name, list(shape), dtype, kind="ExternalInput").ap()

    x_main = din("x_main", [Tc, D])
    x_warm = din("x_warm", [Tc, D])
    rows_in = din("rows", [NROWS, P])
    flag_in = din("flag", [P, 1])
    w_ada = din("w_ada", [D, 6 * D])
    b_ada = din("b_ada", [1, 6 * D])
    w_in = din("w_in", [D, 8 * D])
    w_pa = din("w_proj_a", [D, D])
    w_pb = din("w_proj_b", [D, D])
    w_o = din("w_out", [D, D])
    w_fi = din("w_ffn_in", [D, 2 * DFF])
    w_fo = din("w_ffn_out", [DFF, D])
    lnb_in = din("gmlp_ln_b", [1, D])
    ws_in = din("gmlp_ws", [H, P, P])
    bs_in = din("gmlp_bs", [1, H * P])
    ln1w_in = din("ln1_w", [1, D])
    ln1b_in = din("ln1_b", [1, D])
    ln2w_in = din("ln2_w", [1, D])
    ln2b_in = din("ln2_b", [1, D])
    identf_in = din("ident_f", [P, P])
    identb_in = din("ident_b", [P, P], BF16)
    mask2_in = din("mask2", [P, P])
    m64_in = din("m64", [P, T])
    m32_in = din("m32", [P, T])
    out_d = nc.dram_tensor("out", [Tc, D], F32, kind="ExternalOutput").ap()

    wi_bf = nc.dram_tensor("wi_bf", [16, P, 8, 512], BF16).ap()
    wpa_bf = nc.dram_tensor("wpa_bf", [2, P, 8, 512], BF16).ap()
    wpb_bf = nc.dram_tensor("wpb_bf", [2, P, 8, 512], BF16).ap()
    wo_bf = nc.dram_tensor("wo_bf", [2, P, 8, 512], BF16).ap()
    wfi_bf = nc.dram_tensor("wfi_bf", [11, P, 8, 512], BF16).ap()
    wfo_bf = nc.dram_tensor("wfo_bf", [6, P, 8, 512], BF16).ap()

    nsem = [0]

    def new_sem(name):
        nsem[0] += 1
        return es.enter_context(nc.semaphore(name))

    sems = {k: new_sem("s_" + k) for k in ("pe", "act", "dve", "pool", "sp")}

    def new_chan(name):
        return Chan(new_sem("c_" + name), ("dma", name))

    def sb(name, shape, dtype=F32):
        return es.enter_context(nc.sbuf_tensor("sb_" + name, list(shape), dtype))

    def dma(q, out, in_, reads, writes, chan, skip_self=False):
        S.add(q, lambda e: e.dma_start(out=out, in_=in_), reads, writes, chan=chan, skip_self=skip_self)

    def mm(out, lhsT, rhs, start, stop, reads, writes):
        S.add("pe", lambda e: e.matmul(out, lhsT, rhs, start=start, stop=stop), reads, writes)

    def tr(out, in_, ident, reads, writes):
        S.add("pe", lambda e: e.transpose(out, in_, ident), reads, writes)

    def act(out, in_, func, reads, writes, scale=1.0, bias=0.0):
        S.add("act", lambda e: e.activation(out=out, in_=in_, func=func, bias=bias, scale=scale), reads, writes)

    def tt(eng, out, in0, in1, op, reads, writes):
        S.add(eng, lambda e: e.tensor_tensor(out=out, in0=in0, in1=in1, op=op), reads, writes)

    def ts(eng, out, in0, s1, s2, op0, op1, reads, writes):
        if op1 is None:
            S.add(eng, lambda e: e.tensor_scalar(out=out, in0=in0, scalar1=s1, scalar2=None, op0=op0), reads, writes)
        else:
            S.add(eng, lambda e: e.tensor_scalar(out=out, in0=in0, scalar1=s1, scalar2=s2, op0=op0, op1=op1),
                  reads, writes)

    def stt(out, in0, scalar, in1, op0, op1, reads, writes):
        S.add("dve", lambda e: e.scalar_tensor_tensor(out=out, in0=in0, scalar=scalar, in1=in1, op0=op0, op1=op1),
              reads, writes)

    def copy(eng, out, in_, reads, writes):
        if eng == "act":
            act(out, in_, AF.Copy, reads, writes)
        else:
            S.add(eng, lambda e: e.tensor_copy(out=out, in_=in_), reads, writes)

    def dump(name, ap, buf):
        if dumps is None or name not in dumps:
            return
        shape = list(ap.shape)
        d = nc.dram_tensor("dbg_" + name, shape, ap.dtype, kind="ExternalOutput").ap()
        ch = new_chan("dbg_" + name)
        b = Buf("dbgd_" + name)
        dma("pool", d, ap, [buf], [b], ch)
        dump_list.append(b)

    NBANK = 8
    banks = [es.enter_context(nc.psum_tensor("bank%d" % i, [P, 512], F32)) for i in range(NBANK)]
    bank_bufs = [Buf("bank%d" % i) for i in range(NBANK)]
    rings = {"all": [0, list(range(NBANK))], "z": [0, [0, 1, 2]], "m": [0, [3, 4, 5, 6, 7]]}

    def psum(pool="all"):
        r = rings[pool]
        i = r[1][r[0] % len(r[1])]
        r[0] += 1
        return banks[i], bank_bufs[i]

    ident_f = sb("ident_f", [P, P])
    ident_b = sb("ident_b", [P, P], BF16)
    mask2 = sb("mask2", [P, P])
    m64 = sb("m64", [P, T])
    m32 = sb("m32", [P, T])
    ones_f = sb("ones_f", [P, P])
    ones_b = sb("ones_b", [P, P], BF16)
    colv = sb("colv", [P, NROWS])
    modv = sb("modv", [P, 32])
    scp = sb("scp", [P, 16])
    lbv = sb("lbv", [P, 24])
    flag = sb("flag", [P, 1])
    wmT = sb("wmT", [P, H, P], BF16)
    r_sb = sb("r_sb", [P, H, P])
    g1_bc = sb("g1_bc", [P, D])
    g2_bc = sb("g2_bc", [P, D])
    ln1w_bc = sb("ln1w_bc", [P, D])
    ln1b_bc = sb("ln1b_bc", [P, D])
    ln2w_bc = sb("ln2w_bc", [P, D])
    ln2b_bc = sb("ln2b_bc", [P, D])
    carry = sb("carry", [P, H, P])
    B_const = Buf("consts")
    B_carry = [Buf("carry%d" % h) for h in range(H)]

    conv_A = Buf("conv_A")
    conv_B = Buf("conv_B")
    ch_A = new_chan("conv_A")
    ch_B = new_chan("conv_B")

    def w3(w):
        return w.rearrange("(kc p) n -> p kc n", p=P)

    pieces = {}

    deferred = []

    def conv(name, dst, srcs, grpbuf, ch, kcn=8):
        for src, co in srcs:
            ncols = src.shape[2]
            args = ("pool", dst[:, 0:kcn, co:co + ncols], src, [], [grpbuf], ch, True)
            if grpbuf is conv_A:
                dma(*args)
            else:
                deferred.append(args)
        pieces[name] = (dst, grpbuf, kcn)

    def issue_deferred(n):
        for _ in range(min(n, len(deferred))):
            dma(*deferred.pop(0))

    NAMES_IN = ["q0", "q1", "f0", "f1", "i0", "i1", "g0", "g1", "u0", "u1", "v0", "v1", "ga0", "ga1", "gb0", "gb1"]
    first = ["f0", "f1", "i0", "i1"]
    order = first + [n for n in NAMES_IN if n not in first]
    for n in order:
        j = NAMES_IN.index(n)
        grp = (conv_A, ch_A) if n in first else (conv_B, ch_B)
        conv(n, wi_bf[j], [(w3(w_in)[:, :, j * 512:(j + 1) * 512], 0)], *grp)
    for j in range(2):
        conv("pa%d" % j, wpa_bf[j], [(w3(w_pa)[:, :, j * 512:(j + 1) * 512], 0)], conv_B, ch_B)
        conv("pb%d" % j, wpb_bf[j], [(w3(w_pb)[:, :, j * 512:(j + 1) * 512], 0)], conv_B, ch_B)
    for j in range(2):
        conv("o%d" % j, wo_bf[j], [(w3(w_o)[:, :, j * 512:(j + 1) * 512], 0)], conv_B, ch_B)
    for m in range(11):
        conv("fi%d" % m, wfi_bf[m],
             [(w3(w_fi)[:, :, 256 * m:256 * (m + 1)], 0),
              (w3(w_fi)[:, :, DFF + 256 * m:DFF + 256 * (m + 1)], 256)], conv_B, ch_B)
    for kg in range(3):
        k0, k1 = kg * 8, min(NFF, kg * 8 + 8)
        for half in range(2):
            conv("fo%d_%d" % (kg, half), wfo_bf[kg * 2 + half],
                 [(w3(w_fo)[:, k0:k1, half * 512:(half + 1) * 512], 0)], conv_B, ch_B, kcn=k1 - k0)

    ch_c = new_chan("consts")
    for dst, src in ((ident_f, identf_in), (ident_b, identb_in), (mask2, mask2_in), (m64, m64_in), (m32, m32_in),
                     (flag, flag_in)):
        dma("sp", dst[:], src, [], [B_const], ch_c, skip_self=True)
    for dst, src in ((ln1w_bc, ln1w_in), (ln1b_bc, ln1b_in), (ln2w_bc, ln2w_in), (ln2b_bc, ln2b_in)):
        dma("sp", dst[:], src[0:1, :].broadcast_to([P, D]), [], [B_const], ch_c, skip_self=True)

    with ExitStack() as ses:
        def ssb(name, shape, dtype=F32):
            return ses.enter_context(nc.sbuf_tensor("ss_" + name, list(shape), dtype))

        rows_sb = ssb("rows_sb", [NROWS, P])
        cond2 = ssb("cond2", [P, 8, 2])
        condB = ssb("condB", [P, 8, P])
        NWA = 6
        wa = [ssb("wa%d" % i, [P, 8, 512]) for i in range(NWA)]
        bada_bc = ssb("bada_bc", [P, 2, D])
        lnb_bc = ssb("lnb_bc", [P, D])
        ws_sb = ssb("ws_sb", [P, H, P])
        wmT_f = ssb("wmT_f", [P, H, P])
        bs_sb = ssb("bs_sb", [1, H * P])
        dl = ssb("dl", [P, 8])
        B_rows = Buf("rows")
        B_set = Buf("setup_small")
        B_wa = [Buf("wa%d" % i) for i in range(NWA)]
        ch_wa = [new_chan("wa%d" % i) for i in range(NWA)]
        B_condB = Buf("condB")
        B_gm = Buf("gm")

        dma("sp", rows_sb[:], rows_in, [], [B_const], ch_c, skip_self=True)
        dma("sp", bada_bc[:, 0, :], b_ada[0:1, 2 * D:3 * D].broadcast_to([P, D]), [], [B_const], ch_c, skip_self=True)
        dma("sp", bada_bc[:, 1, :], b_ada[0:1, 5 * D:6 * D].broadcast_to([P, D]), [], [B_const], ch_c, skip_self=True)
        dma("sp", lnb_bc[:], lnb_in[0:1, :].broadcast_to([P, D]), [], [B_const], ch_c, skip_self=True)
        dma("sp", ws_sb[:], ws_in.rearrange("g t s -> t g s"), [], [B_const], ch_c, skip_self=True)
        dma("sp", bs_sb[:], bs_in, [], [B_const], ch_c, skip_self=True)

        S.add("dve", lambda e: e.memset(ones_f[:], 1.0), [], [B_const])
        S.add("dve", lambda e: e.memset(ones_b[:], 1.0), [], [B_const])
        S.add("dve", lambda e: e.memset(carry[:], 0.0), [], B_carry)
        for i in range(NBANK):
            S.add("dve", (lambda i: lambda e: e.memset(banks[i][:], 0.0))(i), [], [bank_bufs[i]])

        pb, pbuf = psum()
        tr(pb[:, 0:NROWS], rows_sb[:], ident_f[0:NROWS, 0:NROWS], [B_const], [pbuf])
        copy("dve", colv[:], pb[:, 0:NROWS], [pbuf], [B_set])
        act(cond2[:, :, 0], colv[:, 0:8], AF.Silu, [B_set], [B_condB])
        act(cond2[:, :, 1], colv[:, 0:8], AF.Silu, [B_set], [B_condB])
        for kc in range(8):
            ts("dve", condB[:, kc, :], ones_f[:], cond2[:, kc, 0:1], None, ALU.mult, None, [B_condB, B_const],
               [B_condB])
        tt("dve", dl[:], colv[:, 32:40], colv[:, 40:48], ALU.subtract, [B_set], [B_set])
        act(lbv[:, 0:8], dl[:], AF.Sigmoid, [B_set], [B_set])
        ts("dve", lbv[:, 8:16], lbv[:, 0:8], -1.0, 1.0, ALU.mult, ALU.add, [B_set], [B_set])
        ts("dve", lbv[:, 16:24], lbv[:, 0:8], -1.0, None, ALU.add, None, [B_set], [B_set])

        modp, modpbuf = psum()
        modp_v = modp[:, 0:64].rearrange("p (a b) -> p a b", b=2)
        wa3 = w3(w_ada)
        col_of = {0: 0, 1: 8, 3: 16, 4: 24}
        for ct in range(12):
            gi = ct // 2
            sl = ct % NWA
            dma("sp", wa[sl][:], wa3[:, :, ct * 512:(ct + 1) * 512], [], [B_wa[sl]], ch_wa[sl])
            if gi in col_of:
                for jb in range(4):
                    col = col_of[gi] + (ct % 2) * 4 + jb
                    for kc in range(8):
                        mm(modp_v[:, col, :], wa[sl][:, kc, jb * P:(jb + 1) * P], cond2[:, kc, :], kc == 0, kc == 7,
                           [B_wa[sl], B_condB], [modpbuf])
            else:
                gb_, gbuf = psum()
                for kc in range(8):
                    mm(gb_[:], condB[:, kc, :], wa[sl][:, kc, :], kc == 0, kc == 7, [B_wa[sl], B_condB], [gbuf])
                gdst = g1_bc if gi == 2 else g2_bc
                half = ct % 2
                tt("dve", gdst[:, half * 512:(half + 1) * 512], gb_[:],
                   bada_bc[:, 0 if gi == 2 else 1, half * 512:(half + 1) * 512], ALU.add, [gbuf, B_const], [B_const])
        tt("dve", modv[:], modp_v[:, 0:32, 0], colv[:, 49:81], ALU.add, [modpbuf, B_set], [B_set])
        ts("dve", scp[:, 0:8], modv[:, 8:16], 1.0, None, ALU.add, None, [B_set], [B_set])
        ts("dve", scp[:, 8:16], modv[:, 24:32], 1.0, None, ALU.add, None, [B_set], [B_set])

        for g in range(H):
            pb, pbuf = psum()
            tr(pb[:, 0:P], ws_sb[:, g, :], ident_f[:], [B_const], [pbuf])
            copy("dve", wmT_f[:, g, :], pb[:, 0:P], [pbuf], [B_gm])
        S.add("dve", lambda e: e.memset(wmT_f[64:128, :, 0:64], 0.0), [], [B_gm])
        copy("dve", wmT[:], wmT_f[:], [B_gm], [B_const])
        for g in range(H):
            pb, pbuf = psum()
            mm(pb[:, 0:P], lnb_bc[:, g * P:(g + 1) * P], wmT_f[:, g, :], True, False, [B_const, B_gm], [pbuf])
            mm(pb[:, 0:P], ones_f[0:1, :], bs_sb[0:1, g * P:(g + 1) * P], False, True, [B_const], [pbuf])
            copy("dve", r_sb[:, g, :], pb[:, 0:P], [pbuf], [B_const])
        S.barrier(exclude=(ch_A.key, ch_B.key))

    sh1 = modv[:, 0:8]
    sc1p = scp[:, 0:8]
    sh2 = modv[:, 16:24]
    sc2p = scp[:, 8:16]
    lb = lbv[:, 0:8]
    oml = lbv[:, 8:16]
    noml = lbv[:, 16:24]
    normw = colv[:, 48:49]
    lnw_col = colv[:, 8:16]
    bga = colv[:, 16:24]
    bgb = colv[:, 24:32]

    h_sb = sb("h_sb", [P, NB, D])
    arena1 = sb("arena1", [P, 6 * 8 * T], BF16)

    def a1(i, a):
        return arena1[:, i * 8 * T:(i + 1) * 8 * T].rearrange("p (a b) -> p a b", a=a)

    uT = a1(0, 8)
    v_sb = a1(1, NB)
    vn_sb = a1(2, NB)
    uG = a1(3, H)
    hgT = a1(4, H)
    ubT = a1(5, H)
    u2T = a1(0, 8)
    actT = arena1[:, 8 * T:(8 + NFF) * T].rearrange("p (a b) -> p a b", a=NFF)
    xpre = arena1[:, 32 * T:48 * T].bitcast(F32).rearrange("p (a b) -> p a b", a=NB)
    B_xpre = [Buf("xpre%d" % b) for b in range(NB)]
    tokb2 = [arena1[:, 16 * T + i * D:16 * T + (i + 1) * D] for i in range(NB)]
    B_tokb2 = [Buf("tokb2_%d" % i) for i in range(NB)]
    sgall = arena1[:, 16 * T:32 * T].bitcast(F32).rearrange("p (a b) -> p a b", a=H)
    B_sgall = [Buf("sgall%d" % h) for h in range(H)]
    wslots = [sb("wslot%d" % i, [P, 8, 512], BF16) for i in range(NSLOT)]
    tbig = sb("tbig", [P, 4, T])
    tq, tsg, tgt, tlf = tbig[:, 0, :], tbig[:, 1, :], tbig[:, 2, :], tbig[:, 3, :]
    mixT = tbig[:, :, :].bitcast(BF16).rearrange("p a (c b) -> p (a c) b", b=T)
    tkk = sb("tkk", [P, T])
    tcum = sb("tcum", [P, T])
    tc32 = sb("tc32", [P, T])
    te1 = sb("te1", [P, T])
    te3 = sb("te3", [P, T])
    ten = te3
    trc = sb("trc", [P, T])
    td1 = sb("td1", [P, NCH, 32])
    qt2 = [sb("qt%d" % i, [P, T], BF16) for i in range(2)]
    qs2 = [sb("qs%d" % i, [P, T], BF16) for i in range(2)]
    kd2 = [sb("kd%d" % i, [P, T], BF16) for i in range(2)]
    K02 = [sb("K0_%d" % i, [P, NCH, 32], BF16) for i in range(2)]
    K12 = [sb("K1_%d" % i, [P, NCH, 64], BF16) for i in range(2)]
    dec2 = [sb("dec%d" % i, [P, NCH]) for i in range(2)]
    kdT_sb = sb("kdT_sb", [P, NB, P], BF16)
    A_sb = sb("A_sb", [P, NB, P], BF16)
    osq = sb("osq", [P, T], BF16)
    Sst = sb("Sst", [P, NCH + 1, P])
    Sbf = sb("Sbf", [P, NCH, P], BF16)
    tokb = [arena1[:, 32 * T + i * D:32 * T + (i + 1) * D] for i in range(NB)]
    tok = [arena1[:, 40 * T + i * 2 * D:40 * T + (i + 1) * 2 * D].bitcast(F32) for i in range(2)]
    st6 = [sb("st6_%d" % i, [P, 12]) for i in range(NB)]
    mv = [sb("mv%d" % i, [P, 4]) for i in range(NB)]
    ftmp = [sb("ftmp%d" % i, [P, T]) for i in range(2)]
    ftmp2 = [sb("ftmp2_%d" % i, [P, T]) for i in range(2)]
    neghalf = sb("neghalf", [P, 1])
    S.add("pool", lambda e: e.memset(neghalf[:], -0.5), [], [B_const])

    B_h = [Buf("h%d" % b) for b in range(NB)]
    ch_h = [new_chan("h%d" % b) for b in range(NB)]
    B_uT = [Buf("uT%d" % b) for b in range(NB)]
    B_v = [[Buf("v%d_%d" % (b, j)) for j in range(2)] for b in range(NB)]
    B_vn = [Buf("vn%d" % b) for b in range(NB)]
    B_uG = [Buf("uG%d" % g) for g in range(H)]
    B_hgT = [Buf("hgT%d" % h) for h in range(H)]
    B_ubT = [Buf("ubT%d" % g) for g in range(H)]
    B_mixT = [Buf("mixT%d" % o) for o in range(8)]
    B_u2T = [Buf("u2T%d" % b) for b in range(NB)]
    B_actT = [Buf("actT%d" % j) for j in range(NFF)]
    ch_xpre = [new_chan("xpre%d" % b) for b in range(NB)]
    B_slot = [Buf("slot%d" % i) for i in range(NSLOT)]
    ch_slot = [new_chan("slot%d" % i) for i in range(NSLOT)]
    B_t = {n: Buf("t_" + n) for n in ("q", "sg", "gt", "lf", "kk", "cum", "c32", "e1", "e3", "en", "rc", "d1", "qt",
                                     "qs", "kd", "K0", "K1a", "K1b", "kdT", "A", "osq", "Sst", "Sbf")}
    B_t["en"] = B_t["e3"]
    B_p = [{n: Buf("p%d_%s" % (i, n)) for n in ("qt", "qs", "kd", "K0", "K1a", "K1b", "dec")} for i in range(2)]
    B_tok = [Buf("tok0"), Buf("tok1")]
    B_tokb = [Buf("tokb%d" % i) for i in range(NB)]
    tokb_main, B_tokb_main = tokb, B_tokb
    B_mv = [Buf("mv%d" % i) for i in range(NB)]
    B_ftmp = [Buf("ftmp0"), Buf("ftmp1")]
    B_ftmp2 = [Buf("ftmp2_0"), Buf("ftmp2_1")]
    tokc = [0]
    ftc = [0]

    wmap = {}
    wnext = [0]

    def wload(name):
        if name in wmap:
            return
        i = wnext[0]
        wnext[0] = (i + 1) % NSLOT
        for k in [k for k, vv in wmap.items() if vv == i]:
            del wmap[k]
        wmap[name] = i
        src, gbuf, kcn = pieces[name]
        dma("sp", wslots[i][:, 0:kcn, :], src[:, 0:kcn, :], [gbuf], [B_slot[i]], ch_slot[i])

    wseq = [[]]

    def wget(name):
        wload(name)
        seq = wseq[0]
        if name in seq:
            i = seq.index(name)
            if i + 1 < len(seq):
                wload(seq[i + 1])
        i = wmap[name]
        return wslots[i], B_slot[i]

    def layernorm_stats_multi(items):
        for src_ap, src_bufs, k in items:
            S.add("dve", (lambda src_ap, k: lambda e: e.bn_stats(out=st6[k][:, 0:6], in_=src_ap[:, 0:512]))(src_ap, k),
                  src_bufs, [B_mv[k]])
            S.add("dve", (lambda src_ap, k: lambda e: e.bn_stats(out=st6[k][:, 6:12], in_=src_ap[:, 512:1024]))(src_ap, k),
                  src_bufs, [B_mv[k]])
        for src_ap, src_bufs, k in items:
            S.add("dve", (lambda k: lambda e: e.bn_aggr(out=mv[k][:, 0:2], in_=st6[k][:, 0:12]))(k), [B_mv[k]],
                  [B_mv[k]])
            ts("dve", mv[k][:, 2:3], mv[k][:, 1:2], LN_EPS, None, ALU.add, None, [B_mv[k]], [B_mv[k]])
        for src_ap, src_bufs, k in items:
            if RSTD_POW:
                tt("pool", mv[k][:, 2:3], mv[k][:, 2:3], neghalf[:], ALU.pow, [B_mv[k], B_const], [B_mv[k]])
            else:
                act(mv[k][:, 2:3], mv[k][:, 2:3], AF.Ln, [B_mv[k]], [B_mv[k]])
                act(mv[k][:, 2:3], mv[k][:, 2:3], AF.Exp, [B_mv[k]], [B_mv[k]], scale=-0.5)
        for src_ap, src_bufs, k in items:
            stt(mv[k][:, 3:4], mv[k][:, 0:1], -1.0, mv[k][:, 2:3], ALU.mult, ALU.mult, [B_mv[k]], [B_mv[k]])

    def layernorm_stats(src_ap, src_bufs, k):
        layernorm_stats_multi([(src_ap, src_bufs, k)])

    def to_featmajor(src_bf, src_bufs, dstT, dst_buf, blk, scale_cols, bias_cols):
        pb, pbuf = psum()
        pv = pb.bitcast(BF16)[:, :].rearrange("p (a b) -> p a b", a=8)
        for kc in range(8):
            tr(pv[:, kc, :], src_bf[:, kc * P:(kc + 1) * P], ident_b[:], src_bufs + [B_const], [pbuf])
        for kc in range(8):
            act(dstT[:, kc, blk * P:(blk + 1) * P], pv[:, kc, :], AF.Identity, [pbuf, B_const], [dst_buf],
                scale=scale_cols[:, kc:kc + 1], bias=bias_cols[:, kc:kc + 1])

    def load_x(src, ti):
        for b in range(NB):
            r0 = (ti * NB + b) * P
            dma("act", h_sb[:, b, :], src[r0:r0 + P, :], [], [B_h[b]], ch_h[b])

    def adaln_all(dstT, B_dst, scale_cols, bias_cols, src=None, tb=None):
        if src is None:
            src = [(h_sb[:, blk, :], [B_h[blk]]) for blk in range(NB)]
        tokb, B_tokb = tb if tb is not None else (tokb_main, B_tokb_main)
        layernorm_stats_multi([(src[blk][0], src[blk][1], blk) for blk in range(NB)])
        for blk in range(NB):
            act(tokb[blk][:], src[blk][0], AF.Identity, src[blk][1] + [B_mv[blk]], [B_tokb[blk]],
                scale=mv[blk][:, 2:3], bias=mv[blk][:, 3:4])
        for j in range(4):
            pb, pbuf = psum()
            pv = pb.bitcast(BF16)[:, :].rearrange("p (k a b) -> p k a b", k=2, a=NB)
            for k2 in range(2):
                kc = 2 * j + k2
                for blk in range(NB):
                    tr(pv[:, k2, blk, :], tokb[blk][:, kc * P:(kc + 1) * P], ident_b[:], [B_tokb[blk], B_const],
                       [pbuf])
            for k2 in range(2):
                kc = 2 * j + k2
                act(dstT[:, kc, :], pv[:, k2, :, :].rearrange("p a b -> p (a b)"), AF.Identity, [pbuf, B_const],
                    B_dst, scale=scale_cols[:, kc:kc + 1], bias=bias_cols[:, kc:kc + 1])

    def proj_feat(wt, wbuf, off, xT, xbufs, pool="all"):
        pb, pbuf = psum(pool)
        for kc in range(8):
            mm(pb[:], wt[:, kc, off:off + P], xT[:, kc, :], kc == 0, kc == 7, [wbuf] + xbufs, [pbuf])
        return pb, pbuf

    def proj_tok(wt, wbuf, xT, xbuf, blk):
        pb, pbuf = psum()
        for kc in range(8):
            mm(pb[:], xT[:, kc, blk * P:(blk + 1) * P], wt[:, kc, :], kc == 0, kc == 7, [wbuf, xbuf], [pbuf])
        return pb, pbuf

    def make_v():
        for j in range(2):
            wt, wbuf = wget("i%d" % j)
            for blk in range(NB):
                pb, pbuf = proj_tok(wt, wbuf, uT, B_uT[blk], blk)
                copy("act" if blk % 2 == 0 else "dve", v_sb[:, blk, j * 512:(j + 1) * 512], pb[:], [pbuf],
                     [B_v[blk][j]])

    c3 = lambda t_: t_[:, :].rearrange("p (c s) -> p c s", s=64)

    def hgrn_head_z(h, full):
        j, off = h // 4, (h % 4) * P
        res = {}
        names = ("f", "q", "g") if full else ("f",)
        for nm in names:
            wt, wbuf = wget("%s%d" % (nm, j))
            res[nm] = proj_feat(wt, wbuf, off, uT, B_uT, pool="z")
            yield res

    def hgrn_head_ew(h, z, full, pre_sg=False):
        B = dict(B_t)
        p = h % 2
        Bp = B_p[p]
        qt, qs, kd, K0, K1, dec = qt2[p], qs2[p], kd2[p], K02[p], K12[p], dec2[p]
        tgt_ = ftmp2[p]
        tsg = tbig[:, 1, :]
        if pre_sg:
            tsg = sgall[:, h, :]
            B["sg"] = B_sgall[h]
        else:
            zf, zfb = z["f"]
            act(tsg[:], zf[:], AF.Sigmoid, [zfb], [B["sg"]])
            yield
        if full:
            zq, zqb = z["q"]
            zg, zgb = z["g"]
            act(tq[:], zq[:], AF.Silu, [zqb], [B["q"]])
            yield
            act(tgt_[:], zg[:], AF.Silu, [zgb], [B_ftmp2[p]])
            yield
        act(tlf[:], tsg[:], AF.Ln, [B["sg"], B_const], [B["lf"]], scale=oml[:, h:h + 1], bias=lb[:, h:h + 1])
        yield
        S.add("dve", lambda e: e.tensor_tensor_scan(out=tcum[:], data0=m64[:], data1=tlf[:], initial=0.0,
                                                    op0=ALU.mult, op1=ALU.add), [B["lf"], B_const], [B["cum"]])
        yield
        ts("dve", tkk[:], tsg[:], noml[:, h:h + 1], oml[:, h:h + 1], ALU.mult, ALU.add, [B["sg"], B_const], [B["kk"]])
        yield
        act(te1[:], tcum[:], AF.Exp, [B["cum"]], [B["e1"]])
        yield
        cum3 = c3(tcum)
        tt("dve", c3(trc), cum3[:, :, 63:64].broadcast_to([P, NCH, 64]), cum3, ALU.subtract, [B["cum"]], [B["rc"]])
        yield
        act(trc[:], trc[:], AF.Exp, [B["rc"]], [B["rc"]])
        yield
        if full:
            S.add("dve", lambda e: e.tensor_tensor_scan(out=tc32[:], data0=m32[:], data1=tlf[:], initial=0.0,
                                                        op0=ALU.mult, op1=ALU.add), [B["lf"], B_const], [B["c32"]])
            yield
        copy("pool", dec[:], c3(te1)[:, :, 63], [B["e1"]], [Bp["dec"]])
        yield
        tt("dve", kd[:], tkk[:], trc[:], ALU.mult, [B["kk"], B["rc"]], [Bp["kd"]])
        yield
        if full:
            act(te3[:], tc32[:], AF.Exp, [B["c32"]], [B["e3"]])
            yield
            tt("dve", qt[:], tq[:], te1[:], ALU.mult, [B["q"], B["e1"]], [Bp["qt"]])
            yield
            tt("dve", qs[:], tq[:], te3[:], ALU.mult, [B["q"], B["e3"]], [Bp["qs"]])
            yield
            act(ten[:], tc32[:], AF.Exp, [B["c32"]], [B["en"]], scale=-1.0)
            yield
            kk3, en3 = c3(tkk), c3(ten)
            tt("dve", td1[:], cum3[:, :, 31:32].broadcast_to([P, NCH, 32]), cum3[:, :, 0:32], ALU.subtract,
               [B["cum"]], [B["d1"]])
            yield
            act(td1[:], td1[:], AF.Exp, [B["d1"]], [B["d1"]])
            yield
            tt("dve", K0[:], kk3[:, :, 0:32], en3[:, :, 0:32], ALU.mult, [B["kk"], B["en"]], [Bp["K0"]])
            yield
            tt("dve", K1[:, :, 32:64], kk3[:, :, 32:64], en3[:, :, 32:64], ALU.mult, [B["kk"], B["en"]], [Bp["K1b"]])
            yield
            tt("dve", K1[:, :, 0:32], kk3[:, :, 0:32], td1[:], ALU.mult, [B["kk"], B["d1"]], [Bp["K1a"]])
            yield

    def hgrn_head_small(h, full):
        B = B_t
        p = h % 2
        Bp = B_p[p]
        qt, qs, kd, K0, K1, dec = qt2[p], qs2[p], kd2[p], K02[p], K12[p], dec2[p]
        tgt_, tX = ftmp2[p], ftmp[p]
        vb_all = [B_v[b][h // 4] for b in range(NB)]
        vcol = slice(h * P, (h + 1) * P)
        pk, pkb = psum("m")
        pkv = pk.bitcast(BF16)[:, 0:NB * P].rearrange("p (a b) -> p a b", a=NB)
        for blk in range(NB):
            tr(pkv[:, blk, :], kd[:, blk * P:(blk + 1) * P], ident_b[:], [Bp["kd"], B_const], [pkb])
        yield
        copy("act", kdT_sb[:], pkv, [pkb], [B["kdT"]])
        yield
        Ue, Ueb = psum("m")
        Uo, Uob = psum("m")
        Uv = (Ue[:, :].rearrange("p (a b) -> p a b", a=NB), Uo[:, :].rearrange("p (a b) -> p a b", a=NB))
        Ub = (Ueb, Uob)
        for c in range(NCH):
            blk, po = c // 2, (c % 2) * 64
            mm(Uv[c % 2][:, blk, :], kdT_sb[po:po + 64, blk, :], v_sb[po:po + 64, blk, vcol], True, True,
               [B["kdT"], vb_all[blk]], [Ub[c % 2]])
        yield
        copy("pool", Sst[:, 0, :], carry[:, h, :], [B_carry[h]], [B["Sst"]])
        yield
        for c in range(NCH):
            stt(Sst[:, c + 1, :], Sst[:, c, :], dec[:, c:c + 1], Uv[c % 2][:, c // 2, :], ALU.mult,
                ALU.add, [B["Sst"], Bp["dec"], Ub[c % 2]], [B["Sst"]])
            yield
        copy("pool", carry[:, h, :], Sst[:, NCH, :], [B["Sst"]], [B_carry[h]])
        yield
        if not full:
            return
        copy("act", Sbf[:], Sst[:, 0:NCH, :], [B["Sst"]], [B["Sbf"]])
        yield
        pa, pab = psum("m")
        pav = pa[:, :].rearrange("p (a b) -> p a b", a=NB)
        S.add("dve", lambda e: e.memset(pa[:], 0.0), [], [pab])
        yield
        for c in range(NCH):
            blk, po = c // 2, (c % 2) * 64
            mm(pav[po:po + 32, blk, po:po + 32], K0[:, c, :], qs[:, c * 64:c * 64 + 32], True, True,
               [Bp["K0"], Bp["qs"]], [pab])
            mm(pav[po:po + 64, blk, po + 32:po + 64], K1[:, c, :], qs[:, c * 64 + 32:c * 64 + 64], True, True,
               [Bp["K1a"], Bp["K1b"], Bp["qs"]], [pab])
        yield
        mask_b = bass.AP(mask2[:].tensor, mask2[:].offset, [list(mask2[:].ap[0]), [0, NB], [1, P]])
        tt("dve", A_sb[:], pav, mask_b, ALU.mult, [pab, B_const], [B["A"]])
        yield
        po_, pob = psum("m")
        for blk in range(NB):
            mm(po_[:, blk * P:(blk + 1) * P], v_sb[:, blk, vcol], A_sb[:, blk, :], True, False,
               [vb_all[blk], B["A"]], [pob])
            for c in (2 * blk, 2 * blk + 1):
                mm(po_[:, c * 64:(c + 1) * 64], Sbf[:, c, :], qt[:, c * 64:(c + 1) * 64], False, c % 2 == 1,
                   [B["Sbf"], Bp["qt"]], [pob])
        yield
        act(osq[:], po_[:], AF.Square, [pob], [B["osq"]])
        yield
        pss, pssb = psum("m")
        mm(pss[:], ones_b[:], osq[:], True, True, [B["osq"], B_const], [pssb])
        yield
        act(tX[:], pss[:], AF.Ln, [pssb], [B_ftmp[p]], scale=1.0 / P, bias=RMS_EPS)
        yield
        act(tX[:], tX[:], AF.Exp, [B_ftmp[p]], [B_ftmp[p]], scale=-0.5)
        yield
        tt("dve", tX[:], po_[:], tX[:], ALU.mult, [pob, B_ftmp[p]], [B_ftmp[p]])
        yield
        stt(hgT[:, h, :], tX[:], normw, tgt_[:], ALU.mult, ALU.mult, [B_ftmp[p], B_ftmp2[p], B_const],
            [B_hgT[h]] + B_tokb + B_xpre)
        yield

    def delayed(gen, n):
        for _ in range(n):
            yield
        yield from gen

    def run_merged(gens, weights=None):
        active = [(g, (weights[i] if weights else 1)) for i, g in enumerate(gens) if g is not None]
        while active:
            for g, w in list(active):
                try:
                    for _ in range(w):
                        next(g)
                except StopIteration:
                    active.remove((g, w))

    def hgrn_all(full):
        zres = {}

        def zgen(h):
            last = None
            for last in hgrn_head_z(h, full):
                yield
            zres[h] = last

        if not full:
            for h in range(H):
                run_merged([zgen(h)])
                zf, zfb = zres[h]["f"]
                act(sgall[:, h, :], zf[:], AF.Sigmoid, [zfb], [B_sgall[h]])
            for i in range(-1, H):
                gens, wts = [], []
                if i + 1 < H:
                    gens.append(hgrn_head_ew(i + 1, None, False, pre_sg=True))
                    wts.append(MERGE_W[0])
                if i >= 0:
                    gens.append(hgrn_head_small(i, False))
                    wts.append(MERGE_W[1])
                run_merged(gens, wts)
            return
        run_merged([zgen(0)])
        for i in range(-1, H):
            gens, wts = [], []
            if i + 1 < H:
                gens.append(hgrn_head_ew(i + 1, zres[i + 1], full))
                wts.append(MERGE_W[0])
            if i >= 0:
                gens.append(hgrn_head_small(i, full))
                wts.append(MERGE_W[1])
            if i + 2 <= H - 1:
                gens.append(delayed(zgen(i + 2), 4 if full else 2))
                wts.append(1)
            run_merged(gens, wts)


    def gmlp():
        for g in range(H):
            wt, wbuf = wget("u%d" % (g // 4))
            pb, pbuf = proj_feat(wt, wbuf, (g % 4) * P, uT, B_uT)
            act(uG[:, g, :], pb[:], AF.Gelu, [pbuf], [B_uG[g]])
        for blk in range(NB):
            k = tokc[0] % 2
            tokc[0] += 1
            for j in range(2):
                wt, wbuf = wget("v%d" % j)
                pb, pbuf = proj_tok(wt, wbuf, uT, B_uT[blk], blk)
                act(tok[k][:, j * 512:(j + 1) * 512], pb[:], AF.Gelu, [pbuf], [B_tok[k]] + B_xpre)
            layernorm_stats(tok[k], [B_tok[k]], k)
            ts("dve", vn_sb[:, blk, :], tok[k][:], mv[k][:, 0:1], mv[k][:, 2:3], ALU.subtract, ALU.mult,
               [B_tok[k], B_mv[k]], [B_vn[blk]] + B_tokb2)
        for g in range(H):
            pb, pbuf = psum()
            for blk in range(NB):
                mm(pb[:, blk * P:(blk + 1) * P], vn_sb[:, blk, g * P:(g + 1) * P], wmT[:, g, :], True, True,
                   [B_vn[blk], B_const], [pbuf])
            k = ftc[0] % 2
            ftc[0] += 1
            r_b = bass.AP(r_sb[:].tensor, r_sb[:, g, :].offset, [list(r_sb[:].ap[0]), [0, NB], [1, P]])
            stt(ftmp[k][:, :].rearrange("p (a b) -> p a b", a=NB), pb[:, :].rearrange("p (a b) -> p a b", a=NB),
                lnw_col[:, g:g + 1], r_b, ALU.mult, ALU.add, [pbuf, B_const], [B_ftmp[k]])
            tt("pool", ubT[:, g, :], ftmp[k][:], uG[:, g, :], ALU.mult, [B_ftmp[k], B_uG[g]],
               [B_ubT[g], B_tok[0], B_tok[1]])

    def mixer_out():
        for oc in range(8):
            j, off = oc // 4, (oc % 4) * P
            wa_, wab = wget("pa%d" % j)
            pya, pyab = proj_feat(wa_, wab, off, hgT, B_hgT)
            wb_, wbb = wget("pb%d" % j)
            pyb, pybb = proj_feat(wb_, wbb, off, ubT, B_ubT)
            wga, wgab = wget("ga%d" % j)
            pga, pgab = proj_feat(wga, wgab, off, uT, B_uT)
            wgb, wgbb = wget("gb%d" % j)
            pgb, pgbb = proj_feat(wgb, wgbb, off, uT, B_uT)
            k = ftc[0] % 2
            ftc[0] += 1
            act(ftmp[k][:], pga[:], AF.Sigmoid, [pgab, B_const], [B_ftmp[k]], bias=bga[:, oc:oc + 1])
            act(ftmp2[k][:], pgb[:], AF.Sigmoid, [pgbb, B_const], [B_ftmp2[k]], bias=bgb[:, oc:oc + 1])
            tt("dve", ftmp[k][:], ftmp[k][:], pya[:], ALU.mult, [B_ftmp[k], pyab], [B_ftmp[k]])
            tt("dve", ftmp2[k][:], ftmp2[k][:], pyb[:], ALU.mult, [B_ftmp2[k], pybb], [B_ftmp2[k]])
            tt("pool", mixT[:, oc, :], ftmp[k][:], ftmp2[k][:], ALU.add, [B_ftmp[k], B_ftmp2[k]],
               [B_mixT[oc], B_t["q"], B_t["sg"], B_t["gt"], B_t["lf"]])

    def post_ln_all(w_bc, b_bc):
        layernorm_stats_multi([(h_sb[:, blk, :], [B_h[blk]], blk) for blk in range(NB)])
        for blk in range(NB):
            act(h_sb[:, blk, :], h_sb[:, blk, :], AF.Identity, [B_h[blk], B_mv[blk]], [B_h[blk]],
                scale=mv[blk][:, 2:3], bias=mv[blk][:, 3:4])
        for blk in range(NB):
            tt("dve" if blk % 2 == 0 else "pool", h_sb[:, blk, :], h_sb[:, blk, :], w_bc[:], ALU.mult,
               [B_h[blk], B_const], [B_h[blk]])
        for blk in range(NB):
            tt("pool" if blk == 2 else "dve", h_sb[:, blk, :], h_sb[:, blk, :], b_bc[:], ALU.add,
               [B_h[blk], B_const], [B_h[blk]])

    def residual_update(pb, pbuf, blk, hs, g_bc):
        k = ftc[0] % 2
        ftc[0] += 1
        tt("dve", ftmp[k][:], pb[:], g_bc[:, hs], ALU.mult, [pbuf, B_const], [B_ftmp[k]])
        stt(h_sb[:, blk, hs], h_sb[:, blk, hs], ALPHA, ftmp[k][:], ALU.mult, ALU.add, [B_h[blk], B_ftmp[k]],
            [B_h[blk]])

    def mix_and_ln1():
        for blk in range(NB):
            for half in range(2):
                wt, wbuf = wget("o%d" % half)
                pb, pbuf = psum()
                for kc in range(8):
                    mm(pb[:], mixT[:, kc, blk * P:(blk + 1) * P], wt[:, kc, :], kc == 0, kc == 7,
                       [wbuf] + B_mixT, [pbuf])
                residual_update(pb, pbuf, blk, slice(half * 512, (half + 1) * 512), g1_bc)
        post_ln_all(ln1w_bc, ln1b_bc)
        adaln_all(u2T, B_u2T, sc2p, sh2)

    def prefetch_x(ti):
        for b in range(NB):
            r0 = (ti * NB + b) * P
            dma("act", xpre[:, b, :], x_main[r0:r0 + P, :], [], [B_xpre[b]] + B_tokb + B_tok, ch_xpre[b])

    def ffn(next_ti=None):
        if next_ti is not None:
            prefetch_x(next_ti)
        for m in range(11):
            wt, wbuf = wget("fi%d" % m)
            for jj in range(2):
                j = 2 * m + jj
                pa_, pab_ = proj_feat(wt, wbuf, jj * P, u2T, B_u2T)
                pbb, pbbb = proj_feat(wt, wbuf, 256 + jj * P, u2T, B_u2T)
                k = ftc[0] % 2
                ftc[0] += 1
                act(ftmp2[k][:], pa_[:], AF.Silu, [pab_], [B_ftmp2[k]])
                tt("dve", actT[:, j, :], ftmp2[k][:], pbb[:], ALU.mult, [B_ftmp2[k], pbbb], [B_actT[j]])
        for half in range(2):
            hs = slice(half * 512, (half + 1) * 512)
            pbs = [psum() for _ in range(NB)]
            for kg in range(3):
                wt, wbuf = wget("fo%d_%d" % (kg, half))
                k0, k1 = kg * 8, min(NFF, kg * 8 + 8)
                for blk in range(NB):
                    for kc in range(k0, k1):
                        mm(pbs[blk][0][:], actT[:, kc, blk * P:(blk + 1) * P], wt[:, kc - k0, :], kc == 0,
                           kc == NFF - 1, [wbuf, B_actT[kc]], [pbs[blk][1]])
            for blk in range(NB):
                residual_update(pbs[blk][0], pbs[blk][1], blk, hs, g2_bc)

    B_out = [Buf("out%d" % b) for b in range(NB)]

    def ln2_and_store(ti):
        post_ln_all(ln2w_bc, ln2b_bc)
        for blk in range(NB):
            r0 = (ti * NB + blk) * P
            dma("act", out_d[r0:r0 + P, :], h_sb[:, blk, :], [B_h[blk]], [B_out[blk]], ch_h[blk])

    SEQ_WARM = ["i0", "i1", "f0", "f1"]
    SEQ_MAIN = (["i0", "i1", "q0", "f0", "g0", "q1", "f1", "g1", "u0", "u1", "v0", "v1",
                 "pa0", "pb0", "ga0", "gb0", "pa1", "pb1", "ga1", "gb1", "o0", "o1"]
                + ["fi%d" % m for m in range(11)]
                + ["fo%d_%d" % (kg, half) for half in range(2) for kg in range(3)])

    for ti in range(n_tiles):
        wseq[0] = SEQ_WARM + (SEQ_WARM[:1] if ti + 1 < n_tiles else SEQ_MAIN[:1])
        issue_deferred(len(deferred) if ti == n_tiles - 1 else -(-len(deferred) // (n_tiles - ti)))
        load_x(x_warm, ti)
        adaln_all(uT, B_uT, sc1p, sh1)
        make_v()
        hgrn_all(False)
    for h in range(H):
        ts("dve", carry[:, h, :], carry[:, h, :], flag[:, 0:1], None, ALU.mult, None, [B_carry[h], B_const],
           [B_carry[h]])
    dump("carry", carry[:], B_carry[H - 1])
    S.barrier(exclude=(ch_A.key, ch_B.key))

    for ti in range(n_tiles):
        wseq[0] = SEQ_MAIN + (SEQ_MAIN[:1] if ti + 1 < n_tiles else [])
        if ti == 0:
            load_x(x_main, ti)
            adaln_all(uT, B_uT, sc1p, sh1)
        else:
            adaln_all(uT, B_uT, sc1p, sh1, src=[(xpre[:, b, :], [B_xpre[b]]) for b in range(NB)],
                      tb=(tokb2, B_tokb2))
            for b in range(NB):
                dma("pool", h_sb[:, b, :], xpre[:, b, :], [B_xpre[b]], [B_h[b]], ch_h[b])
        if ti == 0:
            dump("uT", uT, B_uT[NB - 1])
        make_v()
        if ti == 0:
            dump("v", v_sb, B_v[NB - 1][1])
        hgrn_all(True)
        if ti == 0:
            dump("hgT", hgT, B_hgT[H - 1])
        gmlp()
        if ti == 0:
            dump("ubT", ubT, B_ubT[H - 1])
        mixer_out()
        if ti == 0:
            dump("mixT", mixT, B_mixT[7])
        wload("o0")
        wload("o1")
        S.barrier()
        mix_and_ln1()
        if ti == 0:
            dump("u2T", u2T, B_u2T[NB - 1])
            dump("h1", h_sb[:], B_h[NB - 1])
        ffn(ti + 1 if ti + 1 < n_tiles else None)
        ln2_and_store(ti)
        S.barrier()

    S.add("sp", None, B_out + dump_list, [])
    S.emit(nc, sems)
    es.close()
    return nc


_CONST_CACHE = {}


def _consts():
    if not _CONST_CACHE:
        s = np.arange(P)
        mask2 = ((s[:, None] // 64 == s[None, :] // 64) & (s[:, None] <= s[None, :])).astype(np.float32)
        t = np.arange(T)
        m64 = np.broadcast_to((t % 64 != 0).astype(np.float32), (P, T)).copy()
        m32 = np.broadcast_to((t % 32 != 0).astype(np.float32), (P, T)).copy()
        _CONST_CACHE.update(
            ident_f=np.eye(P, dtype=np.float32),
            ident_b=np.eye(P, dtype=np.float32).astype(ml_dtypes.bfloat16),
            mask2=mask2, m64=m64, m32=m32)
    return _CONST_CACHE


def make_in_maps(inp, n_tiles, seq):
    f = lambda a: np.ascontiguousarray(np.asarray(a, dtype=np.float32))
    x = f(inp["x"])
    Tc = n_tiles * T
    assert seq == 2 * Tc
    shared = dict(
        w_ada=f(inp["w_ada"][0]), b_ada=f(inp["b_ada"][0]).reshape(1, -1), w_in=f(inp["w_in"][0]),
        w_proj_a=f(inp["w_proj_a"][0]), w_proj_b=f(inp["w_proj_b"][0]), w_out=f(inp["w_out"][0]),
        w_ffn_in=f(inp["w_ffn_in"][0]), w_ffn_out=f(inp["w_ffn_out"][0]),
        gmlp_ln_b=f(inp["gmlp_ln_b"][0]).reshape(1, -1), gmlp_ws=f(inp["gmlp_ws"][0]),
        gmlp_bs=f(inp["gmlp_bs"][0]).reshape(1, -1),
        ln1_w=f(inp["ln1_w"][0]).reshape(1, -1), ln1_b=f(inp["ln1_b"][0]).reshape(1, -1),
        ln2_w=f(inp["ln2_w"][0]).reshape(1, -1), ln2_b=f(inp["ln2_b"][0]).reshape(1, -1),
        **_consts())
    b_ada = f(inp["b_ada"][0])
    maps = []
    for core in range(N_CORES):
        b, half = core // 2, core % 2
        rows = np.concatenate([
            f(inp["c"])[b].reshape(8, P),
            f(inp["gmlp_ln_w"][0]).reshape(8, P),
            f(inp["b_gate"][0, 0]).reshape(8, P),
            f(inp["b_gate"][0, 1]).reshape(8, P),
            f(inp["hgrn_lb_logits"][0]).reshape(8, P),
            f(inp["hgrn_lb_logits"][1]).reshape(8, P),
            f(inp["hgrn_norm_w"][0]).reshape(1, P),
            b_ada[0:D].reshape(8, P), b_ada[D:2 * D].reshape(8, P),
            b_ada[3 * D:4 * D].reshape(8, P), b_ada[4 * D:5 * D].reshape(8, P)], axis=0)
        m = dict(shared)
        m["x_main"] = np.ascontiguousarray(x[b, half * Tc:(half + 1) * Tc])
        m["x_warm"] = np.ascontiguousarray(x[b, 0:Tc])
        m["rows"] = np.ascontiguousarray(rows)
        m["flag"] = np.full((P, 1), float(half), np.float32)
        maps.append(m)
    return maps


_NC_CACHE = {}


def run(inp, n_tiles, seq, dumps=None):
    key = (n_tiles, tuple(sorted(dumps)) if dumps else None)
    if key not in _NC_CACHE:
        _NC_CACHE[key] = build(n_tiles, dumps)
    nc = _NC_CACHE[key]
    maps = make_in_maps(inp, n_tiles, seq)
    res = run_bass_kernel_spmd(nc, maps, core_ids=list(range(N_CORES)))
    Tc = n_tiles * T
    out = np.empty((BATCH, seq, D), np.float32)
    for core in range(N_CORES):
        b, half = core // 2, core % 2
        out[b, half * Tc:(half + 1) * Tc] = res.results[core]["out"]
    return out, res


def kernel(**inputs):
    out, _ = run(inputs, SEQ // 2 // T, SEQ)
    return out
```

```python
from contextlib import ExitStack

import ml_dtypes
import numpy as np

import concourse.bass as bass
import concourse.mybir as mybir
from concourse.bass_utils import run_bass_kernel_spmd

F32 = mybir.dt.float32
BF16 = mybir.dt.bfloat16
AF = mybir.ActivationFunctionType
ALU = mybir.AluOpType

P = 128
D = 1024
T = 512
NB = T // P
NCH = T // 64
H = 8
DFF = 2816
NFF = DFF // P
ALPHA = 2.0 ** 0.25
LN_EPS = 1e-5
RMS_EPS = 1e-6
NROWS = 81
NSLOT = 5
RSTD_POW = True
MERGE_W = (1, 1)
N_CORES = 8
SEQ = 8192
BATCH = 4


class Buf:
    __slots__ = ("name", "w", "r")

    def __init__(self, name):
        self.name = name
        self.w = {}
        self.r = {}


class Chan:
    __slots__ = ("sem", "count", "key")

    def __init__(self, sem, key):
        self.sem = sem
        self.count = 0
        self.key = key


class Sched:
    def __init__(self):
        self.ops = []
        self.last = {}
        self.bar = {}

    def add(self, eng, fn, reads=(), writes=(), chan=None, skip_self=False):
        oid = len(self.ops)
        key = chan.key if chan is not None else eng
        deps = dict(self.bar)

        def need(evs):
            for k, o in evs.items():
                if skip_self and k == key:
                    continue
                if deps.get(k, -1) < o:
                    deps[k] = o

        for b in reads:
            need(b.w)
        for b in writes:
            need(b.w)
            need(b.r)
        val = None
        if chan is not None:
            chan.count += 1
            val = chan.count * 16
        self.ops.append(dict(eng=eng, fn=fn, deps=deps, chan=chan, key=key, marked=False, val=val))
        for b in reads:
            b.r[key] = oid
        for b in writes:
            b.w = {key: oid}
            b.r = {}
        self.last[key] = oid
        return oid

    def barrier(self, exclude=()):
        self.bar = {k: v for k, v in self.last.items() if k not in exclude}

    def emit(self, nc, sems):
        ops = self.ops
        for op in ops:
            for k, o in op["deps"].items():
                if k == "pe" and op["eng"] == "pe" and op["chan"] is None:
                    continue
                ops[o]["marked"] = True
        cnt = {}
        for op in ops:
            if op["chan"] is None and op["marked"]:
                cnt[op["key"]] = cnt.get(op["key"], 0) + 1
                op["val"] = cnt[op["key"]]
        by_eng = {}
        for op in ops:
            by_eng.setdefault(op["eng"], []).append(op)

        def body(engname):
            def f(e):
                waited = {}
                for op in by_eng.get(engname, []):
                    for k, o in sorted(op["deps"].items(), key=lambda kv: str(kv[0])):
                        if k == "pe" and engname == "pe" and op["chan"] is None:
                            continue
                        p = ops[o]
                        sem = p["chan"].sem if p["chan"] is not None else sems[p["key"]]
                        v = p["val"]
                        if waited.get(k, 0) >= v:
                            continue
                        e.wait_ge(sem, v)
                        waited[k] = v
                    if op["fn"] is None:
                        continue
                    ins = op["fn"](e)
                    if op["chan"] is not None:
                        ins.then_inc(op["chan"].sem, 16)
                    elif op["marked"]:
                        ins.then_inc(sems[engname], 1)

            return f

        with nc.Block() as block:
            block.tensor(body("pe"))
            block.scalar(body("act"))
            block.vector(body("dve"))
            block.gpsimd(body("pool"))
            block.sync(body("sp"))


def build(n_tiles, dumps=None):
    nc = bass.Bass("TRN2", target_bir_lowering=False)
    Tc = n_tiles * T
    es = ExitStack()
    S = Sched()
    dump_list = []

    def din(name, shape, dtype=F32):
        return nc.dram_tensor(name, list(shape), dtype, kind="ExternalInput").ap()

    x_main = din("x_main", [Tc, D])
    x_warm = din("x_warm", [Tc, D])
    rows_in = din("rows", [NROWS, P])
    flag_in = din("flag", [P, 1])
    w_ada = din("w_ada", [D, 6 * D])
    b_ada = din("b_ada", [1, 6 * D])
    w_in = din("w_in", [D, 8 * D])
    w_pa = din("w_proj_a", [D, D])
    w_pb = din("w_proj_b", [D, D])
    w_o = din("w_out", [D, D])
    w_fi = din("w_ffn_in", [D, 2 * DFF])
    w_fo = din("w_ffn_out", [DFF, D])
    lnb_in = din("gmlp_ln_b", [1, D])
    ws_in = din("gmlp_ws", [H, P, P])
    bs_in = din("gmlp_bs", [1, H * P])
    ln1w_in = din("ln1_w", [1, D])
    ln1b_in = din("ln1_b", [1, D])
    ln2w_in = din("ln2_w", [1, D])
    ln2b_in = din("ln2_b", [1, D])
    identf_in = din("ident_f", [P, P])
    identb_in = din("ident_b", [P, P], BF16)
    mask2_in = din("mask2", [P, P])
    m64_in = din("m64", [P, T])
    m32_in = din("m32", [P, T])
    out_d = nc.dram_tensor("out", [Tc, D], F32, kind="ExternalOutput").ap()

    wi_bf = nc.dram_tensor("wi_bf", [16, P, 8, 512], BF16).ap()
    wpa_bf = nc.dram_tensor("wpa_bf", [2, P, 8, 512], BF16).ap()
    wpb_bf = nc.dram_tensor("wpb_bf", [2, P, 8, 512], BF16).ap()
    wo_bf = nc.dram_tensor("wo_bf", [2, P, 8, 512], BF16).ap()
    wfi_bf = nc.dram_tensor("wfi_bf", [11, P, 8, 512], BF16).ap()
    wfo_bf = nc.dram_tensor("wfo_bf", [6, P, 8, 512], BF16).ap()

    nsem = [0]

    def new_sem(name):
        nsem[0] += 1
        return es.enter_context(nc.semaphore(name))

    sems = {k: new_sem("s_" + k) for k in ("pe", "act", "dve", "pool", "sp")}

    def new_chan(name):
        return Chan(new_sem("c_" + name), ("dma", name))

    def sb(name, shape, dtype=F32):
        return es.enter_context(nc.sbuf_tensor("sb_" + name, list(shape), dtype))

    def dma(q, out, in_, reads, writes, chan, skip_self=False):
        S.add(q, lambda e: e.dma_start(out=out, in_=in_), reads, writes, chan=chan, skip_self=skip_self)

    def mm(out, lhsT, rhs, start, stop, reads, writes):
        S.add("pe", lambda e: e.matmul(out, lhsT, rhs, start=start, stop=stop), reads, writes)

    def tr(out, in_, ident, reads, writes):
        S.add("pe", lambda e: e.transpose(out, in_, ident), reads, writes)

    def act(out, in_, func, reads, writes, scale=1.0, bias=0.0):
        S.add("act", lambda e: e.activation(out=out, in_=in_, func=func, bias=bias, scale=scale), reads, writes)

    def tt(eng, out, in0, in1, op, reads, writes):
        S.add(eng, lambda e: e.tensor_tensor(out=out, in0=in0, in1=in1, op=op), reads, writes)

    def ts(eng, out, in0, s1, s2, op0, op1, reads, writes):
        if op1 is None:
            S.add(eng, lambda e: e.tensor_scalar(out=out, in0=in0, scalar1=s1, scalar2=None, op0=op0), reads, writes)
        else:
            S.add(eng, lambda e: e.tensor_scalar(out=out, in0=in0, scalar1=s1, scalar2=s2, op0=op0, op1=op1),
                  reads, writes)

    def stt(out, in0, scalar, in1, op0, op1, reads, writes):
        S.add("dve", lambda e: e.scalar_tensor_tensor(out=out, in0=in0, scalar=scalar, in1=in1, op0=op0, op1=op1),
              reads, writes)

    def copy(eng, out, in_, reads, writes):
        if eng == "act":
            act(out, in_, AF.Copy, reads, writes)
        else:
            S.add(eng, lambda e: e.tensor_copy(out=out, in_=in_), reads, writes)

    def dump(name, ap, buf):
        if dumps is None or name not in dumps:
            return
        shape = list(ap.shape)
        d = nc.dram_tensor("dbg_" + name, shape, ap.dtype, kind="ExternalOutput").ap()
        ch = new_chan("dbg_" + name)
        b = Buf("dbgd_" + name)
        dma("pool", d, ap, [buf], [b], ch)
        dump_list.append(b)

    NBANK = 8
    banks = [es.enter_context(nc.psum_tensor("bank%d" % i, [P, 512], F32)) for i in range(NBANK)]
    bank_bufs = [Buf("bank%d" % i) for i in range(NBANK)]
    rings = {"all": [0, list(range(NBANK))], "z": [0, [0, 1, 2]], "m": [0, [3, 4, 5, 6, 7]]}

    def psum(pool="all"):
        r = rings[pool]
        i = r[1][r[0] % len(r[1])]
        r[0] += 1
        return banks[i], bank_bufs[i]

    ident_f = sb("ident_f", [P, P])
    ident_b = sb("ident_b", [P, P], BF16)
    mask2 = sb("mask2", [P, P])
    m64 = sb("m64", [P, T])
    m32 = sb("m32", [P, T])
    ones_f = sb("ones_f", [P, P])
    ones_b = sb("ones_b", [P, P], BF16)
    colv = sb("colv", [P, NROWS])
    modv = sb("modv", [P, 32])
    scp = sb("scp", [P, 16])
    lbv = sb("lbv", [P, 24])
    flag = sb("flag", [P, 1])
    wmT = sb("wmT", [P, H, P], BF16)
    r_sb = sb("r_sb", [P, H, P])
    g1_bc = sb("g1_bc", [P, D])
    g2_bc = sb("g2_bc", [P, D])
    ln1w_bc = sb("ln1w_bc", [P, D])
    ln1b_bc = sb("ln1b_bc", [P, D])
    ln2w_bc = sb("ln2w_bc", [P, D])
    ln2b_bc = sb("ln2b_bc", [P, D])
    carry = sb("carry", [P, H, P])
    B_const = Buf("consts")
    B_carry = [Buf("carry%d" % h) for h in range(H)]

    conv_A = Buf("conv_A")
    conv_B = Buf("conv_B")
    ch_A = new_chan("conv_A")
    ch_B = new_chan("conv_B")

    def w3(w):
        return w.rearrange("(kc p) n -> p kc n", p=P)

    pieces = {}

    deferred = []

    def conv(name, dst, srcs, grpbuf, ch, kcn=8):
        for src, co in srcs:
            ncols = src.shape[2]
            args = ("pool", dst[:, 0:kcn, co:co + ncols], src, [], [grpbuf], ch, True)
            if grpbuf is conv_A:
                dma(*args)
            else:
                deferred.append(args)
        pieces[name] = (dst, grpbuf, kcn)

    def issue_deferred(n):
        for _ in range(min(n, len(deferred))):
            dma(*deferred.pop(0))

    NAMES_IN = ["q0", "q1", "f0", "f1", "i0", "i1", "g0", "g1", "u0", "u1", "v0", "v1", "ga0", "ga1", "gb0", "gb1"]
    first = ["f0", "f1", "i0", "i1"]
    order = first + [n for n in NAMES_IN if n not in first]
    for n in order:
        j = NAMES_IN.index(n)
        grp = (conv_A, ch_A) if n in first else (conv_B, ch_B)
        conv(n, wi_bf[j], [(w3(w_in)[:, :, j * 512:(j + 1) * 512], 0)], *grp)
    for j in range(2):
        conv("pa%d" % j, wpa_bf[j], [(w3(w_pa)[:, :, j * 512:(j + 1) * 512], 0)], conv_B, ch_B)
        conv("pb%d" % j, wpb_bf[j], [(w3(w_pb)[:, :, j * 512:(j + 1) * 512], 0)], conv_B, ch_B)
    for j in range(2):
        conv("o%d" % j, wo_bf[j], [(w3(w_o)[:, :, j * 512:(j + 1) * 512], 0)], conv_B, ch_B)
    for m in range(11):
        conv("fi%d" % m, wfi_bf[m],
             [(w3(w_fi)[:, :, 256 * m:256 * (m + 1)], 0),
              (w3(w_fi)[:, :, DFF + 256 * m:DFF + 256 * (m + 1)], 256)], conv_B, ch_B)
    for kg in range(3):
        k0, k1 = kg * 8, min(NFF, kg * 8 + 8)
        for half in range(2):
            conv("fo%d_%d" % (kg, half), wfo_bf[kg * 2 + half],
                 [(w3(w_fo)[:, k0:k1, half * 512:(half + 1) * 512], 0)], conv_B, ch_B, kcn=k1 - k0)

    ch_c = new_chan("consts")
    ch_r = new_chan("consts_early")
    B_early = Buf("consts_early")
    dma("sp", ident_f[:], identf_in, [], [B_early], ch_r, skip_self=True)
    for dst, src in ((ident_b, identb_in), (mask2, mask2_in), (m64, m64_in), (m32, m32_in),
                     (flag, flag_in)):
        dma("sp", dst[:], src, [], [B_const], ch_c, skip_self=True)
    for dst, src in ((ln1w_bc, ln1w_in), (ln1b_bc, ln1b_in), (ln2w_bc, ln2w_in), (ln2b_bc, ln2b_in)):
        dma("sp", dst[:], src[0:1, :].broadcast_to([P, D]), [], [B_const], ch_c, skip_self=True)

    with ExitStack() as ses:
        def ssb(name, shape, dtype=F32):
            return ses.enter_context(nc.sbuf_tensor("ss_" + name, list(shape), dtype))

        rows_sb = ssb("rows_sb", [NROWS, P])
        cond2 = ssb("cond2", [P, 8, 2])
        condB = ssb("condB", [P, 8, P])
        NWA = 6
        wa = [ssb("wa%d" % i, [P, 8, 512]) for i in range(NWA)]
        bada_bc = ssb("bada_bc", [P, 2, D])
        lnb_bc = ssb("lnb_bc", [P, D])
        ws_sb = ssb("ws_sb", [P, H, P])
        wmT_f = ssb("wmT_f", [P, H, P])
        bs_sb = ssb("bs_sb", [1, H * P])
        dl = ssb("dl", [P, 8])
        B_rows = Buf("rows")
        B_set = Buf("setup_small")
        B_wa = [Buf("wa%d" % i) for i in range(NWA)]
        ch_wa = [new_chan("wa%d" % i) for i in range(NWA)]
        B_condB = Buf("condB")
        B_gm = Buf("gm")

        dma("sp", rows_sb[:], rows_in, [], [B_early], ch_r, skip_self=True)
        dma("sp", bada_bc[:, 0, :], b_ada[0:1, 2 * D:3 * D].broadcast_to([P, D]), [], [B_const], ch_c, skip_self=True)
        dma("sp", bada_bc[:, 1, :], b_ada[0:1, 5 * D:6 * D].broadcast_to([P, D]), [], [B_const], ch_c, skip_self=True)
        dma("sp", lnb_bc[:], lnb_in[0:1, :].broadcast_to([P, D]), [], [B_const], ch_c, skip_self=True)
        dma("sp", ws_sb[:], ws_in.rearrange("g t s -> t g s"), [], [B_const], ch_c, skip_self=True)
        dma("sp", bs_sb[:], bs_in, [], [B_const], ch_c, skip_self=True)

        B_ones = Buf("ones")
        S.add("dve", lambda e: e.memset(ones_f[:], 1.0), [], [B_ones])
        S.add("dve", lambda e: e.memset(ones_b[:], 1.0), [], [B_ones])
        S.add("dve", lambda e: e.memset(carry[:], 0.0), [], B_carry)
        for i in range(NBANK):
            S.add("dve", (lambda i: lambda e: e.memset(banks[i][:], 0.0))(i), [], [bank_bufs[i]])

        pb, pbuf = psum()
        tr(pb[:, 0:NROWS], rows_sb[:], ident_f[0:NROWS, 0:NROWS], [B_early], [pbuf])
        copy("dve", colv[:], pb[:, 0:NROWS], [pbuf], [B_set])
        act(cond2[:, :, 0], colv[:, 0:8], AF.Silu, [B_set], [B_condB])
        act(cond2[:, :, 1], colv[:, 0:8], AF.Silu, [B_set], [B_condB])
        for kc in range(8):
            ts("dve", condB[:, kc, :], ones_f[:], cond2[:, kc, 0:1], None, ALU.mult, None, [B_condB, B_ones],
               [B_condB])
        tt("dve", dl[:], colv[:, 32:40], colv[:, 40:48], ALU.subtract, [B_set], [B_set])
        act(lbv[:, 0:8], dl[:], AF.Sigmoid, [B_set], [B_set])
        ts("dve", lbv[:, 8:16], lbv[:, 0:8], -1.0, 1.0, ALU.mult, ALU.add, [B_set], [B_set])
        ts("dve", lbv[:, 16:24], lbv[:, 0:8], -1.0, None, ALU.add, None, [B_set], [B_set])

        modp, modpbuf = psum()
        modp_v = modp[:, 0:64].rearrange("p (a b) -> p a b", b=2)
        wa3 = w3(w_ada)
        col_of = {0: 0, 1: 8, 3: 16, 4: 24}
        for ct in range(12):
            gi = ct // 2
            sl = ct % NWA
            dma("sp", wa[sl][:], wa3[:, :, ct * 512:(ct + 1) * 512], [], [B_wa[sl]], ch_wa[sl])
            if gi in col_of:
                for jb in range(4):
                    col = col_of[gi] + (ct % 2) * 4 + jb
                    for kc in range(8):
                        mm(modp_v[:, col, :], wa[sl][:, kc, jb * P:(jb + 1) * P], cond2[:, kc, :], kc == 0, kc == 7,
                           [B_wa[sl], B_condB], [modpbuf])
            else:
                gb_, gbuf = psum()
                for kc in range(8):
                    mm(gb_[:], condB[:, kc, :], wa[sl][:, kc, :], kc == 0, kc == 7, [B_wa[sl], B_condB], [gbuf])
                gdst = g1_bc if gi == 2 else g2_bc
                half = ct % 2
                tt("dve", gdst[:, half * 512:(half + 1) * 512], gb_[:],
                   bada_bc[:, 0 if gi == 2 else 1, half * 512:(half + 1) * 512], ALU.add, [gbuf, B_const], [B_const])
        tt("dve", modv[:], modp_v[:, 0:32, 0], colv[:, 49:81], ALU.add, [modpbuf, B_set], [B_set])
        ts("dve", scp[:, 0:8], modv[:, 8:16], 1.0, None, ALU.add, None, [B_set], [B_set])
        ts("dve", scp[:, 8:16], modv[:, 24:32], 1.0, None, ALU.add, None, [B_set], [B_set])

        for g in range(H):
            pb, pbuf = psum()
            tr(pb[:, 0:P], ws_sb[:, g, :], ident_f[:], [B_const], [pbuf])
            copy("dve", wmT_f[:, g, :], pb[:, 0:P], [pbuf], [B_gm])
        S.add("dve", lambda e: e.memset(wmT_f[64:128, :, 0:64], 0.0), [], [B_gm])
        copy("dve", wmT[:], wmT_f[:], [B_gm], [B_const])
        for g in range(H):
            pb, pbuf = psum()
            mm(pb[:, 0:P], lnb_bc[:, g * P:(g + 1) * P], wmT_f[:, g, :], True, False, [B_const, B_gm], [pbuf])
            mm(pb[:, 0:P], ones_f[0:1, :], bs_sb[0:1, g * P:(g + 1) * P], False, True, [B_const, B_ones], [pbuf])
            copy("dve", r_sb[:, g, :], pb[:, 0:P], [pbuf], [B_const])
        S.barrier(exclude=(ch_A.key, ch_B.key))

    sh1 = modv[:, 0:8]
    sc1p = scp[:, 0:8]
    sh2 = modv[:, 16:24]
    sc2p = scp[:, 8:16]
    lb = lbv[:, 0:8]
    oml = lbv[:, 8:16]
    noml = lbv[:, 16:24]
    normw = colv[:, 48:49]
    lnw_col = colv[:, 8:16]
    bga = colv[:, 16:24]
    bgb = colv[:, 24:32]

    h_sb = sb("h_sb", [P, NB, D])
    arena1 = sb("arena1", [P, 6 * 8 * T], BF16)

    def a1(i, a):
        return arena1[:, i * 8 * T:(i + 1) * 8 * T].rearrange("p (a b) -> p a b", a=a)

    uT = a1(0, 8)
    v_sb = a1(1, NB)
    vn_sb = a1(2, NB)
    uG = a1(3, H)
    hgT = a1(4, H)
    ubT = a1(5, H)
    u2T = a1(0, 8)
    actT = arena1[:, 8 * T:(8 + NFF) * T].rearrange("p (a b) -> p a b", a=NFF)
    xpre = arena1[:, 32 * T:48 * T].bitcast(F32).rearrange("p (a b) -> p a b", a=NB)
    B_xpre = [Buf("xpre%d" % b) for b in range(NB)]
    tokb2 = [arena1[:, 16 * T + i * D:16 * T + (i + 1) * D] for i in range(NB)]
    B_tokb2 = [Buf("tokb2_%d" % i) for i in range(NB)]
    sgall = arena1[:, 16 * T:32 * T].bitcast(F32).rearrange("p (a b) -> p a b", a=H)
    B_sgall = [Buf("sgall%d" % h) for h in range(H)]
    wslots = [sb("wslot%d" % i, [P, 8, 512], BF16) for i in range(NSLOT)]
    tbig = sb("tbig", [P, 4, T])
    tq, tsg, tgt, tlf = tbig[:, 0, :], tbig[:, 1, :], tbig[:, 2, :], tbig[:, 3, :]
    mixT = tbig[:, :, :].bitcast(BF16).rearrange("p a (c b) -> p (a c) b", b=T)
    tkk = sb("tkk", [P, T])
    tcum = sb("tcum", [P, T])
    tc32 = sb("tc32", [P, T])
    te1 = sb("te1", [P, T])
    te3 = sb("te3", [P, T])
    ten = te3
    trc = sb("trc", [P, T])
    td1 = sb("td1", [P, NCH, 32])
    qt2 = [sb("qt%d" % i, [P, T], BF16) for i in range(2)]
    qs2 = [sb("qs%d" % i, [P, T], BF16) for i in range(2)]
    kd2 = [sb("kd%d" % i, [P, T], BF16) for i in range(2)]
    K02 = [sb("K0_%d" % i, [P, NCH, 32], BF16) for i in range(2)]
    K12 = [sb("K1_%d" % i, [P, NCH, 64], BF16) for i in range(2)]
    dec2 = [sb("dec%d" % i, [P, NCH]) for i in range(2)]
    kdT_sb = sb("kdT_sb", [P, NB, P], BF16)
    A_sb = sb("A_sb", [P, NB, P], BF16)
    osq = sb("osq", [P, T], BF16)
    Sst = sb("Sst", [P, NCH + 1, P])
    Sbf = sb("Sbf", [P, NCH, P], BF16)
    tokb = [arena1[:, 32 * T + i * D:32 * T + (i + 1) * D] for i in range(NB)]
    tok = [arena1[:, 40 * T + i * 2 * D:40 * T + (i + 1) * 2 * D].bitcast(F32) for i in range(2)]
    st6 = [sb("st6_%d" % i, [P, 12]) for i in range(2 * NB)]
    mv = [sb("mv%d" % i, [P, 4]) for i in range(2 * NB)]
    ftmp = [sb("ftmp%d" % i, [P, T]) for i in range(2)]
    ftmp2 = [sb("ftmp2_%d" % i, [P, T]) for i in range(2)]
    neghalf = sb("neghalf", [P, 1])
    S.add("pool", lambda e: e.memset(neghalf[:], -0.5), [], [B_const])

    B_h = [Buf("h%d" % b) for b in range(NB)]
    ch_h = [new_chan("h%d" % b) for b in range(NB)]
    B_uT = [Buf("uT%d" % b) for b in range(NB)]
    B_v = [[Buf("v%d_%d" % (b, j)) for j in range(2)] for b in range(NB)]
    B_vn = [Buf("vn%d" % b) for b in range(NB)]
    B_uG = [Buf("uG%d" % g) for g in range(H)]
    B_hgT = [Buf("hgT%d" % h) for h in range(H)]
    B_ubT = [Buf("ubT%d" % g) for g in range(H)]
    B_mixT = [Buf("mixT%d" % o) for o in range(8)]
    B_u2T = [Buf("u2T%d" % b) for b in range(NB)]
    B_actT = [Buf("actT%d" % j) for j in range(NFF)]
    ch_xpre = [new_chan("xpre%d" % b) for b in range(NB)]
    B_slot = [Buf("slot%d" % i) for i in range(NSLOT)]
    ch_slot = [new_chan("slot%d" % i) for i in range(NSLOT)]
    B_t = {n: Buf("t_" + n) for n in ("q", "sg", "gt", "lf", "kk", "cum", "c32", "e1", "e3", "en", "rc", "d1", "qt",
                                     "qs", "kd", "K0", "K1a", "K1b", "kdT", "A", "osq", "Sst", "Sbf")}
    B_t["en"] = B_t["e3"]
    B_p = [{n: Buf("p%d_%s" % (i, n)) for n in ("qt", "qs", "kd", "K0", "K1a", "K1b", "dec")} for i in range(2)]
    B_tok = [Buf("tok0"), Buf("tok1")]
    B_tokb = [Buf("tokb%d" % i) for i in range(NB)]
    tokb_main, B_tokb_main = tokb, B_tokb
    B_mv = [Buf("mv%d" % i) for i in range(2 * NB)]
    B_ftmp = [Buf("ftmp0"), Buf("ftmp1")]
    B_ftmp2 = [Buf("ftmp2_0"), Buf("ftmp2_1")]
    tokc = [0]
    ftc = [0]

    wmap = {}
    wnext = [0]

    def wload(name):
        if name in wmap:
            return
        i = wnext[0]
        wnext[0] = (i + 1) % NSLOT
        for k in [k for k, vv in wmap.items() if vv == i]:
            del wmap[k]
        wmap[name] = i
        src, gbuf, kcn = pieces[name]
        dma("sp", wslots[i][:, 0:kcn, :], src[:, 0:kcn, :], [gbuf], [B_slot[i]], ch_slot[i])

    wseq = [[]]

    def wget(name):
        wload(name)
        seq = wseq[0]
        if name in seq:
            i = seq.index(name)
            if i + 1 < len(seq):
                wload(seq[i + 1])
        i = wmap[name]
        return wslots[i], B_slot[i]

    def g_layernorm_stats_multi(items):
        for src_ap, src_bufs, k in items:
            S.add("dve", (lambda src_ap, k: lambda e: e.bn_stats(out=st6[k][:, 0:6], in_=src_ap[:, 0:512]))(src_ap, k),
                  src_bufs, [B_mv[k]])
            S.add("dve", (lambda src_ap, k: lambda e: e.bn_stats(out=st6[k][:, 6:12], in_=src_ap[:, 512:1024]))(src_ap, k),
                  src_bufs, [B_mv[k]])
        yield
        for src_ap, src_bufs, k in items:
            S.add("dve", (lambda k: lambda e: e.bn_aggr(out=mv[k][:, 0:2], in_=st6[k][:, 0:12]))(k), [B_mv[k]],
                  [B_mv[k]])
            ts("dve", mv[k][:, 2:3], mv[k][:, 1:2], LN_EPS, None, ALU.add, None, [B_mv[k]], [B_mv[k]])
        yield
        for src_ap, src_bufs, k in items:
            if RSTD_POW:
                tt("pool", mv[k][:, 2:3], mv[k][:, 2:3], neghalf[:], ALU.pow, [B_mv[k], B_const], [B_mv[k]])
            else:
                act(mv[k][:, 2:3], mv[k][:, 2:3], AF.Ln, [B_mv[k]], [B_mv[k]])
                act(mv[k][:, 2:3], mv[k][:, 2:3], AF.Exp, [B_mv[k]], [B_mv[k]], scale=-0.5)
        yield
        for src_ap, src_bufs, k in items:
            stt(mv[k][:, 3:4], mv[k][:, 0:1], -1.0, mv[k][:, 2:3], ALU.mult, ALU.mult, [B_mv[k]], [B_mv[k]])
        yield

    def layernorm_stats_multi(items):
        for _ in g_layernorm_stats_multi(items):
            pass

    def layernorm_stats(src_ap, src_bufs, k):
        layernorm_stats_multi([(src_ap, src_bufs, k)])

    def to_featmajor(src_bf, src_bufs, dstT, dst_buf, blk, scale_cols, bias_cols):
        pb, pbuf = psum()
        pv = pb.bitcast(BF16)[:, :].rearrange("p (a b) -> p a b", a=8)
        for kc in range(8):
            tr(pv[:, kc, :], src_bf[:, kc * P:(kc + 1) * P], ident_b[:], src_bufs + [B_const], [pbuf])
        for kc in range(8):
            act(dstT[:, kc, blk * P:(blk + 1) * P], pv[:, kc, :], AF.Identity, [pbuf, B_const], [dst_buf],
                scale=scale_cols[:, kc:kc + 1], bias=bias_cols[:, kc:kc + 1])

    def load_x(src, ti):
        for b in range(NB):
            r0 = (ti * NB + b) * P
            dma("act", h_sb[:, b, :], src[r0:r0 + P, :], [], [B_h[b]], ch_h[b])

    def g_adaln_all(dstT, B_dst, scale_cols, bias_cols, src=None, tb=None, koff=0):
        if src is None:
            src = [(h_sb[:, blk, :], [B_h[blk]]) for blk in range(NB)]
        tokb, B_tokb = tb if tb is not None else (tokb_main, B_tokb_main)
        yield from g_layernorm_stats_multi([(src[blk][0], src[blk][1], blk + koff) for blk in range(NB)])
        for blk in range(NB):
            act(tokb[blk][:], src[blk][0], AF.Identity, src[blk][1] + [B_mv[blk + koff]], [B_tokb[blk]],
                scale=mv[blk + koff][:, 2:3], bias=mv[blk + koff][:, 3:4])
            yield
        for j in range(4):
            pb, pbuf = psum()
            pv = pb.bitcast(BF16)[:, :].rearrange("p (k a b) -> p k a b", k=2, a=NB)
            for k2 in range(2):
                kc = 2 * j + k2
                for blk in range(NB):
                    tr(pv[:, k2, blk, :], tokb[blk][:, kc * P:(kc + 1) * P], ident_b[:], [B_tokb[blk], B_const],
                       [pbuf])
                yield
            for k2 in range(2):
                kc = 2 * j + k2
                act(dstT[:, kc, :], pv[:, k2, :, :].rearrange("p a b -> p (a b)"), AF.Identity, [pbuf, B_const],
                    B_dst, scale=scale_cols[:, kc:kc + 1], bias=bias_cols[:, kc:kc + 1])
                yield

    def adaln_all(*a_, **k_):
        for _ in g_adaln_all(*a_, **k_):
            pass

    def proj_feat(wt, wbuf, off, xT, xbufs, pool="all"):
        pb, pbuf = psum(pool)
        for kc in range(8):
            mm(pb[:], wt[:, kc, off:off + P], xT[:, kc, :], kc == 0, kc == 7, [wbuf] + xbufs, [pbuf])
        return pb, pbuf

    def proj_tok(wt, wbuf, xT, xbuf, blk):
        pb, pbuf = psum()
        for kc in range(8):
            mm(pb[:], xT[:, kc, blk * P:(blk + 1) * P], wt[:, kc, :], kc == 0, kc == 7, [wbuf, xbuf], [pbuf])
        return pb, pbuf

    def make_v():
        for j in range(2):
            wt, wbuf = wget("i%d" % j)
            for blk in range(NB):
                pb, pbuf = proj_tok(wt, wbuf, uT, B_uT[blk], blk)
                copy("act" if blk % 2 == 0 else "dve", v_sb[:, blk, j * 512:(j + 1) * 512], pb[:], [pbuf],
                     [B_v[blk][j]])

    c3 = lambda t_: t_[:, :].rearrange("p (c s) -> p c s", s=64)

    def hgrn_head_z(h, full):
        j, off = h // 4, (h % 4) * P
        res = {}
        names = ("f", "q", "g") if full else ("f",)
        for nm in names:
            wt, wbuf = wget("%s%d" % (nm, j))
            res[nm] = proj_feat(wt, wbuf, off, uT, B_uT, pool="z")
            yield res

    def hgrn_head_ew(h, z, full, pre_sg=False):
        B = dict(B_t)
        p = h % 2
        Bp = B_p[p]
        qt, qs, kd, K0, K1, dec = qt2[p], qs2[p], kd2[p], K02[p], K12[p], dec2[p]
        tgt_ = ftmp2[p]
        tsg = tbig[:, 1, :]
        if pre_sg:
            tsg = sgall[:, h, :]
            B["sg"] = B_sgall[h]
        else:
            zf, zfb = z["f"]
            act(tsg[:], zf[:], AF.Sigmoid, [zfb], [B["sg"]])
            yield
        if full:
            zq, zqb = z["q"]
            zg, zgb = z["g"]
            act(tq[:], zq[:], AF.Silu, [zqb], [B["q"]])
            yield
            act(tgt_[:], zg[:], AF.Silu, [zgb], [B_ftmp2[p]])
            yield
        act(tlf[:], tsg[:], AF.Ln, [B["sg"], B_const], [B["lf"]], scale=oml[:, h:h + 1], bias=lb[:, h:h + 1])
        yield
        S.add("dve", lambda e: e.tensor_tensor_scan(out=tcum[:], data0=m64[:], data1=tlf[:], initial=0.0,
                                                    op0=ALU.mult, op1=ALU.add), [B["lf"], B_const], [B["cum"]])
        yield
        ts("dve", tkk[:], tsg[:], noml[:, h:h + 1], oml[:, h:h + 1], ALU.mult, ALU.add, [B["sg"], B_const], [B["kk"]])
        yield
        act(te1[:], tcum[:], AF.Exp, [B["cum"]], [B["e1"]])
        yield
        cum3 = c3(tcum)
        tt("dve", c3(trc), cum3[:, :, 63:64].broadcast_to([P, NCH, 64]), cum3, ALU.subtract, [B["cum"]], [B["rc"]])
        yield
        act(trc[:], trc[:], AF.Exp, [B["rc"]], [B["rc"]])
        yield
        if full:
            S.add("dve", lambda e: e.tensor_tensor_scan(out=tc32[:], data0=m32[:], data1=tlf[:], initial=0.0,
                                                        op0=ALU.mult, op1=ALU.add), [B["lf"], B_const], [B["c32"]])
            yield
        copy("pool", dec[:], c3(te1)[:, :, 63], [B["e1"]], [Bp["dec"]])
        yield
        tt("dve", kd[:], tkk[:], trc[:], ALU.mult, [B["kk"], B["rc"]], [Bp["kd"]])
        yield
        if full:
            act(te3[:], tc32[:], AF.Exp, [B["c32"]], [B["e3"]])
            yield
            tt("dve", qt[:], tq[:], te1[:], ALU.mult, [B["q"], B["e1"]], [Bp["qt"]])
            yield
            tt("dve", qs[:], tq[:], te3[:], ALU.mult, [B["q"], B["e3"]], [Bp["qs"]])
            yield
            act(ten[:], tc32[:], AF.Exp, [B["c32"]], [B["en"]], scale=-1.0)
            yield
            kk3, en3 = c3(tkk), c3(ten)
            tt("dve", td1[:], cum3[:, :, 31:32].broadcast_to([P, NCH, 32]), cum3[:, :, 0:32], ALU.subtract,
               [B["cum"]], [B["d1"]])
            yield
            act(td1[:], td1[:], AF.Exp, [B["d1"]], [B["d1"]])
            yield
            tt("dve", K0[:], kk3[:, :, 0:32], en3[:, :, 0:32], ALU.mult, [B["kk"], B["en"]], [Bp["K0"]])
            yield
            tt("dve", K1[:, :, 32:64], kk3[:, :, 32:64], en3[:, :, 32:64], ALU.mult, [B["kk"], B["en"]], [Bp["K1b"]])
            yield
            tt("dve", K1[:, :, 0:32], kk3[:, :, 0:32], td1[:], ALU.mult, [B["kk"], B["d1"]], [Bp["K1a"]])
            yield

    def hgrn_head_small(h, full):
        B = B_t
        p = h % 2
        Bp = B_p[p]
        qt, qs, kd, K0, K1, dec = qt2[p], qs2[p], kd2[p], K02[p], K12[p], dec2[p]
        tgt_, tX = ftmp2[p], ftmp[p]
        vb_all = [B_v[b][h // 4] for b in range(NB)]
        vcol = slice(h * P, (h + 1) * P)
        pk, pkb = psum("m")
        pkv = pk.bitcast(BF16)[:, 0:NB * P].rearrange("p (a b) -> p a b", a=NB)
        for blk in range(NB):
            tr(pkv[:, blk, :], kd[:, blk * P:(blk + 1) * P], ident_b[:], [Bp["kd"], B_const], [pkb])
        yield
        copy("act", kdT_sb[:], pkv, [pkb], [B["kdT"]])
        yield
        Ue, Ueb = psum("m")
        Uo, Uob = psum("m")
        Uv = (Ue[:, :].rearrange("p (a b) -> p a b", a=NB), Uo[:, :].rearrange("p (a b) -> p a b", a=NB))
        Ub = (Ueb, Uob)
        for c in range(NCH):
            blk, po = c // 2, (c % 2) * 64
            mm(Uv[c % 2][:, blk, :], kdT_sb[po:po + 64, blk, :], v_sb[po:po + 64, blk, vcol], True, True,
               [B["kdT"], vb_all[blk]], [Ub[c % 2]])
        yield
        copy("pool", Sst[:, 0, :], carry[:, h, :], [B_carry[h]], [B["Sst"]])
        yield
        for c in range(NCH):
            stt(Sst[:, c + 1, :], Sst[:, c, :], dec[:, c:c + 1], Uv[c % 2][:, c // 2, :], ALU.mult,
                ALU.add, [B["Sst"], Bp["dec"], Ub[c % 2]], [B["Sst"]])
            yield
        copy("pool", carry[:, h, :], Sst[:, NCH, :], [B["Sst"]], [B_carry[h]])
        yield
        if not full:
            return
        copy("act", Sbf[:], Sst[:, 0:NCH, :], [B["Sst"]], [B["Sbf"]])
        yield
        pa, pab = psum("m")
        pav = pa[:, :].rearrange("p (a b) -> p a b", a=NB)
        S.add("dve", lambda e: e.memset(pa[:], 0.0), [], [pab])
        yield
        for c in range(NCH):
            blk, po = c // 2, (c % 2) * 64
            mm(pav[po:po + 32, blk, po:po + 32], K0[:, c, :], qs[:, c * 64:c * 64 + 32], True, True,
               [Bp["K0"], Bp["qs"]], [pab])
            mm(pav[po:po + 64, blk, po + 32:po + 64], K1[:, c, :], qs[:, c * 64 + 32:c * 64 + 64], True, True,
               [Bp["K1a"], Bp["K1b"], Bp["qs"]], [pab])
        yield
        mask_b = bass.AP(mask2[:].tensor, mask2[:].offset, [list(mask2[:].ap[0]), [0, NB], [1, P]])
        tt("dve", A_sb[:], pav, mask_b, ALU.mult, [pab, B_const], [B["A"]])
        yield
        po_, pob = psum("m")
        for blk in range(NB):
            mm(po_[:, blk * P:(blk + 1) * P], v_sb[:, blk, vcol], A_sb[:, blk, :], True, False,
               [vb_all[blk], B["A"]], [pob])
            for c in (2 * blk, 2 * blk + 1):
                mm(po_[:, c * 64:(c + 1) * 64], Sbf[:, c, :], qt[:, c * 64:(c + 1) * 64], False, c % 2 == 1,
                   [B["Sbf"], Bp["qt"]], [pob])
        yield
        act(osq[:], po_[:], AF.Square, [pob], [B["osq"]])
        yield
        pss, pssb = psum("m")
        mm(pss[:], ones_b[:], osq[:], True, True, [B["osq"], B_const], [pssb])
        yield
        act(tX[:], pss[:], AF.Ln, [pssb], [B_ftmp[p]], scale=1.0 / P, bias=RMS_EPS)
        yield
        act(tX[:], tX[:], AF.Exp, [B_ftmp[p]], [B_ftmp[p]], scale=-0.5)
        yield
        tt("dve", tX[:], po_[:], tX[:], ALU.mult, [pob, B_ftmp[p]], [B_ftmp[p]])
        yield
        stt(hgT[:, h, :], tX[:], normw, tgt_[:], ALU.mult, ALU.mult, [B_ftmp[p], B_ftmp2[p], B_const],
            [B_hgT[h]] + B_tokb + B_xpre)
        yield

    def delayed(gen, n):
        for _ in range(n):
            yield
        yield from gen

    def run_merged(gens, weights=None):
        active = [(g, (weights[i] if weights else 1)) for i, g in enumerate(gens) if g is not None]
        while active:
            for g, w in list(active):
                try:
                    for _ in range(w):
                        next(g)
                except StopIteration:
                    active.remove((g, w))

    def hgrn_all(full):
        zres = {}

        def zgen(h):
            last = None
            for last in hgrn_head_z(h, full):
                yield
            zres[h] = last

        if not full:
            for h in range(H):
                run_merged([zgen(h)])
                zf, zfb = zres[h]["f"]
                act(sgall[:, h, :], zf[:], AF.Sigmoid, [zfb], [B_sgall[h]])
            for i in range(-1, H):
                gens, wts = [], []
                if i + 1 < H:
                    gens.append(hgrn_head_ew(i + 1, None, False, pre_sg=True))
                    wts.append(MERGE_W[0])
                if i >= 0:
                    gens.append(hgrn_head_small(i, False))
                    wts.append(MERGE_W[1])
                run_merged(gens, wts)
            return
        run_merged([zgen(0)])
        for i in range(-1, H):
            gens, wts = [], []
            if i + 1 < H:
                gens.append(hgrn_head_ew(i + 1, zres[i + 1], full))
                wts.append(MERGE_W[0])
            if i >= 0:
                gens.append(hgrn_head_small(i, full))
                wts.append(MERGE_W[1])
            if i + 2 <= H - 1:
                gens.append(delayed(zgen(i + 2), 4 if full else 2))
                wts.append(1)
            run_merged(gens, wts)


    def gmlp():
        for g in range(H):
            wt, wbuf = wget("u%d" % (g // 4))
            pb, pbuf = proj_feat(wt, wbuf, (g % 4) * P, uT, B_uT)
            act(uG[:, g, :], pb[:], AF.Gelu, [pbuf], [B_uG[g]])
        for blk in range(NB):
            k = tokc[0] % 2
            tokc[0] += 1
            for j in range(2):
                wt, wbuf = wget("v%d" % j)
                pb, pbuf = proj_tok(wt, wbuf, uT, B_uT[blk], blk)
                act(tok[k][:, j * 512:(j + 1) * 512], pb[:], AF.Gelu, [pbuf], [B_tok[k]] + B_xpre)
            layernorm_stats(tok[k], [B_tok[k]], k)
            ts("dve", vn_sb[:, blk, :], tok[k][:], mv[k][:, 0:1], mv[k][:, 2:3], ALU.subtract, ALU.mult,
               [B_tok[k], B_mv[k]], [B_vn[blk]] + B_tokb2)
        for g in range(H):
            pb, pbuf = psum()
            for blk in range(NB):
                mm(pb[:, blk * P:(blk + 1) * P], vn_sb[:, blk, g * P:(g + 1) * P], wmT[:, g, :], True, True,
                   [B_vn[blk], B_const], [pbuf])
            k = ftc[0] % 2
            ftc[0] += 1
            r_b = bass.AP(r_sb[:].tensor, r_sb[:, g, :].offset, [list(r_sb[:].ap[0]), [0, NB], [1, P]])
            stt(ftmp[k][:, :].rearrange("p (a b) -> p a b", a=NB), pb[:, :].rearrange("p (a b) -> p a b", a=NB),
                lnw_col[:, g:g + 1], r_b, ALU.mult, ALU.add, [pbuf, B_const], [B_ftmp[k]])
            tt("pool", ubT[:, g, :], ftmp[k][:], uG[:, g, :], ALU.mult, [B_ftmp[k], B_uG[g]],
               [B_ubT[g], B_tok[0], B_tok[1]])

    def mixer_out():
        for oc in range(8):
            j, off = oc // 4, (oc % 4) * P
            wa_, wab = wget("pa%d" % j)
            pya, pyab = proj_feat(wa_, wab, off, hgT, B_hgT)
            wb_, wbb = wget("pb%d" % j)
            pyb, pybb = proj_feat(wb_, wbb, off, ubT, B_ubT)
            wga, wgab = wget("ga%d" % j)
            pga, pgab = proj_feat(wga, wgab, off, uT, B_uT)
            wgb, wgbb = wget("gb%d" % j)
            pgb, pgbb = proj_feat(wgb, wgbb, off, uT, B_uT)
            k = ftc[0] % 2
            ftc[0] += 1
            act(ftmp[k][:], pga[:], AF.Sigmoid, [pgab, B_const], [B_ftmp[k]], bias=bga[:, oc:oc + 1])
            act(ftmp2[k][:], pgb[:], AF.Sigmoid, [pgbb, B_const], [B_ftmp2[k]], bias=bgb[:, oc:oc + 1])
            tt("dve", ftmp[k][:], ftmp[k][:], pya[:], ALU.mult, [B_ftmp[k], pyab], [B_ftmp[k]])
            tt("dve", ftmp2[k][:], ftmp2[k][:], pyb[:], ALU.mult, [B_ftmp2[k], pybb], [B_ftmp2[k]])
            tt("pool", mixT[:, oc, :], ftmp[k][:], ftmp2[k][:], ALU.add, [B_ftmp[k], B_ftmp2[k]],
               [B_mixT[oc], B_t["q"], B_t["sg"], B_t["gt"], B_t["lf"]])

    def g_post_ln_all(w_bc, b_bc):
        yield from g_layernorm_stats_multi([(h_sb[:, blk, :], [B_h[blk]], blk) for blk in range(NB)])
        for blk in range(NB):
            act(h_sb[:, blk, :], h_sb[:, blk, :], AF.Identity, [B_h[blk], B_mv[blk]], [B_h[blk]],
                scale=mv[blk][:, 2:3], bias=mv[blk][:, 3:4])
            yield
        for blk in range(NB):
            tt("dve" if blk % 2 == 0 else "pool", h_sb[:, blk, :], h_sb[:, blk, :], w_bc[:], ALU.mult,
               [B_h[blk], B_const], [B_h[blk]])
            yield
        for blk in range(NB):
            tt("pool" if blk == 2 else "dve", h_sb[:, blk, :], h_sb[:, blk, :], b_bc[:], ALU.add,
               [B_h[blk], B_const], [B_h[blk]])
            yield

    def post_ln_all(*a_):
        for _ in g_post_ln_all(*a_):
            pass

    def residual_update(pb, pbuf, blk, hs, g_bc):
        k = ftc[0] % 2
        ftc[0] += 1
        tt("dve", ftmp[k][:], pb[:], g_bc[:, hs], ALU.mult, [pbuf, B_const], [B_ftmp[k]])
        stt(h_sb[:, blk, hs], h_sb[:, blk, hs], ALPHA, ftmp[k][:], ALU.mult, ALU.add, [B_h[blk], B_ftmp[k]],
            [B_h[blk]])

    def mix_and_ln1():
        for blk in range(NB):
            for half in range(2):
                wt, wbuf = wget("o%d" % half)
                pb, pbuf = psum()
                for kc in range(8):
                    mm(pb[:], mixT[:, kc, blk * P:(blk + 1) * P], wt[:, kc, :], kc == 0, kc == 7,
                       [wbuf] + B_mixT, [pbuf])
                residual_update(pb, pbuf, blk, slice(half * 512, (half + 1) * 512), g1_bc)
        post_ln_all(ln1w_bc, ln1b_bc)
        adaln_all(u2T, B_u2T, sc2p, sh2)

    def prefetch_x(ti):
        for b in range(NB):
            r0 = (ti * NB + b) * P
            dma("act", xpre[:, b, :], x_main[r0:r0 + P, :], [], [B_xpre[b]] + B_tokb + B_tok, ch_xpre[b])

    def ffn(next_ti=None):
        if next_ti is not None:
            prefetch_x(next_ti)
        for m in range(11):
            wt, wbuf = wget("fi%d" % m)
            for jj in range(2):
                j = 2 * m + jj
                pa_, pab_ = proj_feat(wt, wbuf, jj * P, u2T, B_u2T)
                pbb, pbbb = proj_feat(wt, wbuf, 256 + jj * P, u2T, B_u2T)
                k = ftc[0] % 2
                ftc[0] += 1
                act(ftmp2[k][:], pa_[:], AF.Silu, [pab_], [B_ftmp2[k]])
                tt("dve", actT[:, j, :], ftmp2[k][:], pbb[:], ALU.mult, [B_ftmp2[k], pbbb], [B_actT[j]])
        for half in range(2):
            hs = slice(half * 512, (half + 1) * 512)
            pbs = [psum() for _ in range(NB)]
            for kg in range(3):
                wt, wbuf = wget("fo%d_%d" % (kg, half))
                k0, k1 = kg * 8, min(NFF, kg * 8 + 8)
                for blk in range(NB):
                    for kc in range(k0, k1):
                        mm(pbs[blk][0][:], actT[:, kc, blk * P:(blk + 1) * P], wt[:, kc - k0, :], kc == 0,
                           kc == NFF - 1, [wbuf, B_actT[kc]], [pbs[blk][1]])
            for blk in range(NB):
                residual_update(pbs[blk][0], pbs[blk][1], blk, hs, g2_bc)

    B_out = [Buf("out%d" % b) for b in range(NB)]

    def g_ln2_and_store(ti):
        yield from g_post_ln_all(ln2w_bc, ln2b_bc)
        for blk in range(NB):
            r0 = (ti * NB + blk) * P
            dma("sp", out_d[r0:r0 + P, :], h_sb[:, blk, :], [B_h[blk]], [B_out[blk]], ch_h[blk])
            yield

    SEQ_WARM = ["i0", "i1", "f0", "f1"]
    SEQ_MAIN = (["i0", "i1", "q0", "f0", "g0", "q1", "f1", "g1", "u0", "u1", "v0", "v1",
                 "pa0", "pb0", "ga0", "gb0", "pa1", "pb1", "ga1", "gb1", "o0", "o1"]
                + ["fi%d" % m for m in range(11)]
                + ["fo%d_%d" % (kg, half) for half in range(2) for kg in range(3)])

    for ti in range(n_tiles):
        wseq[0] = SEQ_WARM + (SEQ_WARM[:1] if ti + 1 < n_tiles else SEQ_MAIN[:1])
        issue_deferred(len(deferred) if ti == n_tiles - 1 else -(-len(deferred) // (n_tiles - ti)))
        load_x(x_warm, ti)
        adaln_all(uT, B_uT, sc1p, sh1)
        make_v()
        hgrn_all(False)
    for h in range(H):
        ts("dve", carry[:, h, :], carry[:, h, :], flag[:, 0:1], None, ALU.mult, None, [B_carry[h], B_const],
           [B_carry[h]])
    dump("carry", carry[:], B_carry[H - 1])
    S.barrier(exclude=(ch_A.key, ch_B.key))

    for ti in range(n_tiles):
        wseq[0] = SEQ_MAIN + (SEQ_MAIN[:1] if ti + 1 < n_tiles else [])
        if ti == 0:
            load_x(x_main, ti)
            adaln_all(uT, B_uT, sc1p, sh1)
        if ti == 0:
            dump("uT", uT, B_uT[NB - 1])
        make_v()
        if ti == 0:
            dump("v", v_sb, B_v[NB - 1][1])
        hgrn_all(True)
        if ti == 0:
            dump("hgT", hgT, B_hgT[H - 1])
        gmlp()
        if ti == 0:
            dump("ubT", ubT, B_ubT[H - 1])
        mixer_out()
        if ti == 0:
            dump("mixT", mixT, B_mixT[7])
        wload("o0")
        wload("o1")
        S.barrier()
        mix_and_ln1()
        if ti == 0:
            dump("u2T", u2T, B_u2T[NB - 1])
            dump("h1", h_sb[:], B_h[NB - 1])
        ffn(ti + 1 if ti + 1 < n_tiles else None)
        S.barrier()
        if ti + 1 < n_tiles:
            run_merged([g_ln2_and_store(ti),
                        g_adaln_all(uT, B_uT, sc1p, sh1, src=[(xpre[:, b, :], [B_xpre[b]]) for b in range(NB)],
                                    tb=(tokb2, B_tokb2), koff=NB)])
            for b in range(NB):
                dma("pool", h_sb[:, b, :], xpre[:, b, :], [B_xpre[b]], [B_h[b]], ch_h[b])
        else:
            run_merged([g_ln2_and_store(ti)])

    S.add("sp", None, B_out + dump_list, [])
    S.emit(nc, sems)
    es.close()
    return nc


_CONST_CACHE = {}


def _consts():
    if not _CONST_CACHE:
        s = np.arange(P)
        mask2 = ((s[:, None] // 64 == s[None, :] // 64) & (s[:, None] <= s[None, :])).astype(np.float32)
        t = np.arange(T)
        m64 = np.broadcast_to((t % 64 != 0).astype(np.float32), (P, T)).copy()
        m32 = np.broadcast_to((t % 32 != 0).astype(np.float32), (P, T)).copy()
        _CONST_CACHE.update(
            ident_f=np.eye(P, dtype=np.float32),
            ident_b=np.eye(P, dtype=np.float32).astype(ml_dtypes.bfloat16),
            mask2=mask2, m64=m64, m32=m32)
    return _CONST_CACHE


def make_in_maps(inp, n_tiles, seq):
    f = lambda a: np.ascontiguousarray(np.asarray(a, dtype=np.float32))
    x = f(inp["x"])
    Tc = n_tiles * T
    assert seq == 2 * Tc
    shared = dict(
        w_ada=f(inp["w_ada"][0]), b_ada=f(inp["b_ada"][0]).reshape(1, -1), w_in=f(inp["w_in"][0]),
        w_proj_a=f(inp["w_proj_a"][0]), w_proj_b=f(inp["w_proj_b"][0]), w_out=f(inp["w_out"][0]),
        w_ffn_in=f(inp["w_ffn_in"][0]), w_ffn_out=f(inp["w_ffn_out"][0]),
        gmlp_ln_b=f(inp["gmlp_ln_b"][0]).reshape(1, -1), gmlp_ws=f(inp["gmlp_ws"][0]),
        gmlp_bs=f(inp["gmlp_bs"][0]).reshape(1, -1),
        ln1_w=f(inp["ln1_w"][0]).reshape(1, -1), ln1_b=f(inp["ln1_b"][0]).reshape(1, -1),
        ln2_w=f(inp["ln2_w"][0]).reshape(1, -1), ln2_b=f(inp["ln2_b"][0]).reshape(1, -1),
        **_consts())
    b_ada = f(inp["b_ada"][0])
    maps = []
    for core in range(N_CORES):
        b, half = core // 2, core % 2
        rows = np.concatenate([
            f(inp["c"])[b].reshape(8, P),
            f(inp["gmlp_ln_w"][0]).reshape(8, P),
            f(inp["b_gate"][0, 0]).reshape(8, P),
            f(inp["b_gate"][0, 1]).reshape(8, P),
            f(inp["hgrn_lb_logits"][0]).reshape(8, P),
            f(inp["hgrn_lb_logits"][1]).reshape(8, P),
            f(inp["hgrn_norm_w"][0]).reshape(1, P),
            b_ada[0:D].reshape(8, P), b_ada[D:2 * D].reshape(8, P),
            b_ada[3 * D:4 * D].reshape(8, P), b_ada[4 * D:5 * D].reshape(8, P)], axis=0)
        m = dict(shared)
        m["x_main"] = np.ascontiguousarray(x[b, half * Tc:(half + 1) * Tc])
        m["x_warm"] = np.ascontiguousarray(x[b, 0:Tc])
        m["rows"] = np.ascontiguousarray(rows)
        m["flag"] = np.full((P, 1), float(half), np.float32)
        maps.append(m)
    return maps


_NC_CACHE = {}


def run(inp, n_tiles, seq, dumps=None):
    key = (n_tiles, tuple(sorted(dumps)) if dumps else None)
    if key not in _NC_CACHE:
        _NC_CACHE[key] = build(n_tiles, dumps)
    nc = _NC_CACHE[key]
    maps = make_in_maps(inp, n_tiles, seq)
    res = run_bass_kernel_spmd(nc, maps, core_ids=list(range(N_CORES)))
    Tc = n_tiles * T
    out = np.empty((BATCH, seq, D), np.float32)
    for core in range(N_CORES):
        b, half = core // 2, core % 2
        out[b, half * Tc:(half + 1) * Tc] = res.results[core]["out"]
    return out, res


def kernel(**inputs):
    out, _ = run(inputs, SEQ // 2 // T, SEQ)
    return out
```

```python
from contextlib import ExitStack

import ml_dtypes
import numpy as np

import concourse.bass as bass
import concourse.mybir as mybir
from concourse.bass_utils import run_bass_kernel_spmd

F32 = mybir.dt.float32
BF16 = mybir.dt.bfloat16
AF = mybir.ActivationFunctionType
ALU = mybir.AluOpType

P = 128
D = 1024
T = 512
NB = T // P
NCH = T // 64
H = 8
DFF = 2816
NFF = DFF // P
ALPHA = 2.0 ** 0.25
LN_EPS = 1e-5
RMS_EPS = 1e-6
NROWS = 81
NSLOT = 5
RSTD_POW = True
MERGE_W = (1, 1)
N_CORES = 8
SEQ = 8192
BATCH = 4


class Buf:
    __slots__ = ("name", "w", "r")

    def __init__(self, name):
        self.name = name
        self.w = {}
        self.r = {}


class Chan:
    __slots__ = ("sem", "count", "key")

    def __init__(self, sem, key):
        self.sem = sem
        self.count = 0
        self.key = key


class Sched:
    def __init__(self):
        self.ops = []
        self.last = {}
        self.bar = {}

    def add(self, eng, fn, reads=(), writes=(), chan=None, skip_self=False):
        oid = len(self.ops)
        key = chan.key if chan is not None else eng
        deps = dict(self.bar)

        def need(evs):
            for k, o in evs.items():
                if skip_self and k == key:
                    continue
                if deps.get(k, -1) < o:
                    deps[k] = o

        for b in reads:
            need(b.w)
        for b in writes:
            need(b.w)
            need(b.r)
        val = None
        if chan is not None:
            chan.count += 1
            val = chan.count * 16
        self.ops.append(dict(eng=eng, fn=fn, deps=deps, chan=chan, key=key, marked=False, val=val))
        for b in reads:
            b.r[key] = oid
        for b in writes:
            b.w = {key: oid}
            b.r = {}
        self.last[key] = oid
        return oid

    def barrier(self, exclude=()):
        self.bar = {k: v for k, v in self.last.items() if k not in exclude}

    def emit(self, nc, sems):
        ops = self.ops
        for op in ops:
            for k, o in op["deps"].items():
                if k == "pe" and op["eng"] == "pe" and op["chan"] is None:
                    continue
                ops[o]["marked"] = True
        cnt = {}
        for op in ops:
            if op["chan"] is None and op["marked"]:
                cnt[op["key"]] = cnt.get(op["key"], 0) + 1
                op["val"] = cnt[op["key"]]
        by_eng = {}
        for op in ops:
            by_eng.setdefault(op["eng"], []).append(op)

        def body(engname):
            def f(e):
                waited = {}
                for op in by_eng.get(engname, []):
                    for k, o in sorted(op["deps"].items(), key=lambda kv: str(kv[0])):
                        if k == "pe" and engname == "pe" and op["chan"] is None:
                            continue
                        p = ops[o]
                        sem = p["chan"].sem if p["chan"] is not None else sems[p["key"]]
                        v = p["val"]
                        if waited.get(k, 0) >= v:
                            continue
                        e.wait_ge(sem, v)
                        waited[k] = v
                    if op["fn"] is None:
                        continue
                    ins = op["fn"](e)
                    if op["chan"] is not None:
                        ins.then_inc(op["chan"].sem, 16)
                    elif op["marked"]:
                        ins.then_inc(sems[engname], 1)

            return f

        with nc.Block() as block:
            block.tensor(body("pe"))
            block.scalar(body("act"))
            block.vector(body("dve"))
            block.gpsimd(body("pool"))
            block.sync(body("sp"))


def build(n_tiles, dumps=None):
    nc = bass.Bass("TRN2", target_bir_lowering=False)
    Tc = n_tiles * T
    es = ExitStack()
    S = Sched()
    dump_list = []

    def din(name, shape, dtype=F32):
        return nc.dram_tensor(name, list(shape), dtype, kind="ExternalInput").ap()

    x_main = din("x_main", [Tc, D])
    x_warm = din("x_warm", [Tc, D])
    rows_in = din("rows", [NROWS, P])
    flag_in = din("flag", [P, 1])
    w_ada = din("w_ada", [D, 6 * D])
    b_ada = din("b_ada", [1, 6 * D])
    w_in = din("w_in", [D, 8 * D])
    w_pa = din("w_proj_a", [D, D])
    w_pb = din("w_proj_b", [D, D])
    w_o = din("w_out", [D, D])
    w_fi = din("w_ffn_in", [D, 2 * DFF])
    w_fo = din("w_ffn_out", [DFF, D])
    lnb_in = din("gmlp_ln_b", [1, D])
    ws_in = din("gmlp_ws", [H, P, P])
    bs_in = din("gmlp_bs", [1, H * P])
    ln1w_in = din("ln1_w", [1, D])
    ln1b_in = din("ln1_b", [1, D])
    ln2w_in = din("ln2_w", [1, D])
    ln2b_in = din("ln2_b", [1, D])
    identf_in = din("ident_f", [P, P])
    identb_in = din("ident_b", [P, P], BF16)
    mask2_in = din("mask2", [P, P])
    m64_in = din("m64", [P, T])
    m32_in = din("m32", [P, T])
    out_d = nc.dram_tensor("out", [Tc, D], F32, kind="ExternalOutput").ap()

    wi_bf = nc.dram_tensor("wi_bf", [16, P, 8, 512], BF16).ap()
    wpa_bf = nc.dram_tensor("wpa_bf", [2, P, 8, 512], BF16).ap()
    wpb_bf = nc.dram_tensor("wpb_bf", [2, P, 8, 512], BF16).ap()
    wo_bf = nc.dram_tensor("wo_bf", [2, P, 8, 512], BF16).ap()
    wfi_bf = nc.dram_tensor("wfi_bf", [11, P, 8, 512], BF16).ap()
    wfo_bf = nc.dram_tensor("wfo_bf", [6, P, 8, 512], BF16).ap()

    nsem = [0]

    def new_sem(name):
        nsem[0] += 1
        return es.enter_context(nc.semaphore(name))

    sems = {k: new_sem("s_" + k) for k in ("pe", "act", "dve", "pool", "sp")}

    def new_chan(name):
        return Chan(new_sem("c_" + name), ("dma", name))

    def sb(name, shape, dtype=F32):
        return es.enter_context(nc.sbuf_tensor("sb_" + name, list(shape), dtype))

    def dma(q, out, in_, reads, writes, chan, skip_self=False):
        S.add(q, lambda e: e.dma_start(out=out, in_=in_), reads, writes, chan=chan, skip_self=skip_self)

    def mm(out, lhsT, rhs, start, stop, reads, writes):
        S.add("pe", lambda e: e.matmul(out, lhsT, rhs, start=start, stop=stop), reads, writes)

    def tr(out, in_, ident, reads, writes):
        S.add("pe", lambda e: e.transpose(out, in_, ident), reads, writes)

    def act(out, in_, func, reads, writes, scale=1.0, bias=0.0):
        S.add("act", lambda e: e.activation(out=out, in_=in_, func=func, bias=bias, scale=scale), reads, writes)

    def tt(eng, out, in0, in1, op, reads, writes):
        S.add(eng, lambda e: e.tensor_tensor(out=out, in0=in0, in1=in1, op=op), reads, writes)

    def ts(eng, out, in0, s1, s2, op0, op1, reads, writes):
        if op1 is None:
            S.add(eng, lambda e: e.tensor_scalar(out=out, in0=in0, scalar1=s1, scalar2=None, op0=op0), reads, writes)
        else:
            S.add(eng, lambda e: e.tensor_scalar(out=out, in0=in0, scalar1=s1, scalar2=s2, op0=op0, op1=op1),
                  reads, writes)

    def stt(out, in0, scalar, in1, op0, op1, reads, writes):
        S.add("dve", lambda e: e.scalar_tensor_tensor(out=out, in0=in0, scalar=scalar, in1=in1, op0=op0, op1=op1),
              reads, writes)

    def copy(eng, out, in_, reads, writes):
        if eng == "act":
            act(out, in_, AF.Copy, reads, writes)
        else:
            S.add(eng, lambda e: e.tensor_copy(out=out, in_=in_), reads, writes)

    def dump(name, ap, buf):
        if dumps is None or name not in dumps:
            return
        shape = list(ap.shape)
        d = nc.dram_tensor("dbg_" + name, shape, ap.dtype, kind="ExternalOutput").ap()
        ch = new_chan("dbg_" + name)
        b = Buf("dbgd_" + name)
        dma("pool", d, ap, [buf], [b], ch)
        dump_list.append(b)

    NBANK = 8
    banks = [es.enter_context(nc.psum_tensor("bank%d" % i, [P, 512], F32)) for i in range(NBANK)]
    bank_bufs = [Buf("bank%d" % i) for i in range(NBANK)]
    rings = {"all": [0, list(range(NBANK))], "z": [0, [0, 1, 2]], "m": [0, [3, 4, 5, 6, 7]]}

    def psum(pool="all"):
        r = rings[pool]
        i = r[1][r[0] % len(r[1])]
        r[0] += 1
        return banks[i], bank_bufs[i]

    ident_f = sb("ident_f", [P, P])
    ident_b = sb("ident_b", [P, P], BF16)
    mask2 = sb("mask2", [P, P])
    m64 = sb("m64", [P, T])
    m32 = sb("m32", [P, T])
    ones_f = sb("ones_f", [P, P])
    ones_b = sb("ones_b", [P, P], BF16)
    colv = sb("colv", [P, NROWS])
    modv = sb("modv", [P, 32])
    scp = sb("scp", [P, 16])
    lbv = sb("lbv", [P, 24])
    flag = sb("flag", [P, 1])
    wmT = sb("wmT", [P, H, P], BF16)
    r_sb = sb("r_sb", [P, H, P])
    g1_bc = sb("g1_bc", [P, D])
    g2_bc = sb("g2_bc", [P, D])
    ln1w_bc = sb("ln1w_bc", [P, D])
    ln1b_bc = sb("ln1b_bc", [P, D])
    ln2w_bc = sb("ln2w_bc", [P, D])
    ln2b_bc = sb("ln2b_bc", [P, D])
    carry = sb("carry", [P, H, P])
    B_const = Buf("consts")
    B_carry = [Buf("carry%d" % h) for h in range(H)]

    conv_A = Buf("conv_A")
    conv_B = Buf("conv_B")
    ch_A = new_chan("conv_A")
    ch_B = new_chan("conv_B")

    def w3(w):
        return w.rearrange("(kc p) n -> p kc n", p=P)

    pieces = {}

    deferred = []
    deferred_A = []

    def conv(name, dst, srcs, grpbuf, ch, kcn=8):
        for src, co in srcs:
            ncols = src.shape[2]
            args = ("pool", dst[:, 0:kcn, co:co + ncols], src, [], [grpbuf], ch, True)
            if grpbuf is conv_A:
                deferred_A.append(args)
            else:
                deferred.append(args)
        pieces[name] = (dst, grpbuf, kcn)

    def issue_deferred(n):
        for _ in range(min(n, len(deferred))):
            dma(*deferred.pop(0))

    NAMES_IN = ["q0", "q1", "f0", "f1", "i0", "i1", "g0", "g1", "u0", "u1", "v0", "v1", "ga0", "ga1", "gb0", "gb1"]
    first = ["f0", "f1", "i0", "i1"]
    order = first + [n for n in NAMES_IN if n not in first]
    for n in order:
        j = NAMES_IN.index(n)
        grp = (conv_A, ch_A) if n in first else (conv_B, ch_B)
        conv(n, wi_bf[j], [(w3(w_in)[:, :, j * 512:(j + 1) * 512], 0)], *grp)
    for j in range(2):
        conv("pa%d" % j, wpa_bf[j], [(w3(w_pa)[:, :, j * 512:(j + 1) * 512], 0)], conv_B, ch_B)
        conv("pb%d" % j, wpb_bf[j], [(w3(w_pb)[:, :, j * 512:(j + 1) * 512], 0)], conv_B, ch_B)
    for j in range(2):
        conv("o%d" % j, wo_bf[j], [(w3(w_o)[:, :, j * 512:(j + 1) * 512], 0)], conv_B, ch_B)
    for m in range(11):
        conv("fi%d" % m, wfi_bf[m],
             [(w3(w_fi)[:, :, 256 * m:256 * (m + 1)], 0),
              (w3(w_fi)[:, :, DFF + 256 * m:DFF + 256 * (m + 1)], 256)], conv_B, ch_B)
    for kg in range(3):
        k0, k1 = kg * 8, min(NFF, kg * 8 + 8)
        for half in range(2):
            conv("fo%d_%d" % (kg, half), wfo_bf[kg * 2 + half],
                 [(w3(w_fo)[:, k0:k1, half * 512:(half + 1) * 512], 0)], conv_B, ch_B, kcn=k1 - k0)

    ch_c = new_chan("consts")
    ch_r = new_chan("consts_early")
    B_early = Buf("consts_early")
    dma("sp", ident_f[:], identf_in, [], [B_early], ch_r, skip_self=True)
    for dst, src in ((ident_b, identb_in), (mask2, mask2_in), (m64, m64_in), (m32, m32_in),
                     (flag, flag_in)):
        dma("sp", dst[:], src, [], [B_const], ch_c, skip_self=True)
    for dst, src in ((ln1w_bc, ln1w_in), (ln1b_bc, ln1b_in), (ln2w_bc, ln2w_in), (ln2b_bc, ln2b_in)):
        dma("sp", dst[:], src[0:1, :].broadcast_to([P, D]), [], [B_const], ch_c, skip_self=True)

    with ExitStack() as ses:
        def ssb(name, shape, dtype=F32):
            return ses.enter_context(nc.sbuf_tensor("ss_" + name, list(shape), dtype))

        rows_sb = ssb("rows_sb", [NROWS, P])
        cond2 = ssb("cond2", [P, 8, 2])
        condB = ssb("condB", [P, 8, P])
        NWA = 6
        wa = [ssb("wa%d" % i, [P, 8, 512]) for i in range(NWA)]
        bada_bc = ssb("bada_bc", [P, 2, D])
        lnb_bc = ssb("lnb_bc", [P, D])
        ws_sb = ssb("ws_sb", [P, H, P])
        wmT_f = ssb("wmT_f", [P, H, P])
        bs_sb = ssb("bs_sb", [1, H * P])
        dl = ssb("dl", [P, 8])
        B_rows = Buf("rows")
        B_set = Buf("setup_small")
        B_wa = [Buf("wa%d" % i) for i in range(NWA)]
        ch_wa = [new_chan("wa%d" % i) for i in range(NWA)]
        B_condB = Buf("condB")
        B_gm = Buf("gm")

        dma("sp", rows_sb[:], rows_in, [], [B_early], ch_r, skip_self=True)
        dma("sp", bada_bc[:, 0, :], b_ada[0:1, 2 * D:3 * D].broadcast_to([P, D]), [], [B_const], ch_c, skip_self=True)
        dma("sp", bada_bc[:, 1, :], b_ada[0:1, 5 * D:6 * D].broadcast_to([P, D]), [], [B_const], ch_c, skip_self=True)
        dma("sp", lnb_bc[:], lnb_in[0:1, :].broadcast_to([P, D]), [], [B_const], ch_c, skip_self=True)
        dma("sp", ws_sb[:], ws_in.rearrange("g t s -> t g s"), [], [B_const], ch_c, skip_self=True)
        dma("sp", bs_sb[:], bs_in, [], [B_const], ch_c, skip_self=True)

        B_ones = Buf("ones")
        S.add("dve", lambda e: e.memset(ones_f[:], 1.0), [], [B_ones])
        S.add("dve", lambda e: e.memset(ones_b[:], 1.0), [], [B_ones])
        S.add("dve", lambda e: e.memset(carry[:], 0.0), [], B_carry)
        for i in range(NBANK):
            S.add("dve", (lambda i: lambda e: e.memset(banks[i][:], 0.0))(i), [], [bank_bufs[i]])

        pb, pbuf = psum()
        tr(pb[:, 0:NROWS], rows_sb[:], ident_f[0:NROWS, 0:NROWS], [B_early], [pbuf])
        copy("dve", colv[:], pb[:, 0:NROWS], [pbuf], [B_set])
        act(cond2[:, :, 0], colv[:, 0:8], AF.Silu, [B_set], [B_condB])
        act(cond2[:, :, 1], colv[:, 0:8], AF.Silu, [B_set], [B_condB])
        for kc in range(8):
            ts("dve", condB[:, kc, :], ones_f[:], cond2[:, kc, 0:1], None, ALU.mult, None, [B_condB, B_ones],
               [B_condB])
        tt("dve", dl[:], colv[:, 32:40], colv[:, 40:48], ALU.subtract, [B_set], [B_set])
        act(lbv[:, 0:8], dl[:], AF.Sigmoid, [B_set], [B_set])
        ts("dve", lbv[:, 8:16], lbv[:, 0:8], -1.0, 1.0, ALU.mult, ALU.add, [B_set], [B_set])
        ts("dve", lbv[:, 16:24], lbv[:, 0:8], -1.0, None, ALU.add, None, [B_set], [B_set])

        modp, modpbuf = psum()
        modp_v = modp[:, 0:64].rearrange("p (a b) -> p a b", b=2)
        wa3 = w3(w_ada)
        col_of = {0: 0, 1: 8, 3: 16, 4: 24}
        for ct in range(12):
            gi = ct // 2
            sl = ct % NWA
            dma("sp", wa[sl][:], wa3[:, :, ct * 512:(ct + 1) * 512], [], [B_wa[sl]], ch_wa[sl])
            if gi in col_of:
                for jb in range(4):
                    col = col_of[gi] + (ct % 2) * 4 + jb
                    for kc in range(8):
                        mm(modp_v[:, col, :], wa[sl][:, kc, jb * P:(jb + 1) * P], cond2[:, kc, :], kc == 0, kc == 7,
                           [B_wa[sl], B_condB], [modpbuf])
            else:
                gb_, gbuf = psum()
                for kc in range(8):
                    mm(gb_[:], condB[:, kc, :], wa[sl][:, kc, :], kc == 0, kc == 7, [B_wa[sl], B_condB], [gbuf])
                gdst = g1_bc if gi == 2 else g2_bc
                half = ct % 2
                tt("dve", gdst[:, half * 512:(half + 1) * 512], gb_[:],
                   bada_bc[:, 0 if gi == 2 else 1, half * 512:(half + 1) * 512], ALU.add, [gbuf, B_const], [B_const])
        for i_, args in enumerate(deferred_A):
            q_, o_, i__, r_, w_, c_, sk_ = args
            dma(q_, o_, i__, (B_wa if i_ == 0 else []), w_, c_, sk_)
        tt("dve", modv[:], modp_v[:, 0:32, 0], colv[:, 49:81], ALU.add, [modpbuf, B_set], [B_set])
        ts("dve", scp[:, 0:8], modv[:, 8:16], 1.0, None, ALU.add, None, [B_set], [B_set])
        ts("dve", scp[:, 8:16], modv[:, 24:32], 1.0, None, ALU.add, None, [B_set], [B_set])

        for g in range(H):
            pb, pbuf = psum()
            tr(pb[:, 0:P], ws_sb[:, g, :], ident_f[:], [B_const], [pbuf])
            copy("dve", wmT_f[:, g, :], pb[:, 0:P], [pbuf], [B_gm])
        S.add("dve", lambda e: e.memset(wmT_f[64:128, :, 0:64], 0.0), [], [B_gm])
        copy("dve", wmT[:], wmT_f[:], [B_gm], [B_const])
        for g in range(H):
            pb, pbuf = psum()
            mm(pb[:, 0:P], lnb_bc[:, g * P:(g + 1) * P], wmT_f[:, g, :], True, False, [B_const, B_gm], [pbuf])
            mm(pb[:, 0:P], ones_f[0:1, :], bs_sb[0:1, g * P:(g + 1) * P], False, True, [B_const, B_ones], [pbuf])
            copy("dve", r_sb[:, g, :], pb[:, 0:P], [pbuf], [B_const])
        S.barrier(exclude=(ch_A.key, ch_B.key))

    sh1 = modv[:, 0:8]
    sc1p = scp[:, 0:8]
    sh2 = modv[:, 16:24]
    sc2p = scp[:, 8:16]
    lb = lbv[:, 0:8]
    oml = lbv[:, 8:16]
    noml = lbv[:, 16:24]
    normw = colv[:, 48:49]
    lnw_col = colv[:, 8:16]
    bga = colv[:, 16:24]
    bgb = colv[:, 24:32]

    h_sb = sb("h_sb", [P, NB, D])
    arena1 = sb("arena1", [P, 6 * 8 * T], BF16)

    def a1(i, a):
        return arena1[:, i * 8 * T:(i + 1) * 8 * T].rearrange("p (a b) -> p a b", a=a)

    uT = a1(0, 8)
    v_sb = a1(1, NB)
    vn_sb = a1(2, NB)
    uG = a1(3, H)
    hgT = a1(4, H)
    ubT = a1(5, H)
    u2T = a1(0, 8)
    actT = arena1[:, 8 * T:(8 + NFF) * T].rearrange("p (a b) -> p a b", a=NFF)
    xpre = arena1[:, 32 * T:48 * T].bitcast(F32).rearrange("p (a b) -> p a b", a=NB)
    B_xpre = [Buf("xpre%d" % b) for b in range(NB)]
    tokb2 = [arena1[:, 16 * T + i * D:16 * T + (i + 1) * D] for i in range(NB)]
    B_tokb2 = [Buf("tokb2_%d" % i) for i in range(NB)]
    sgall = arena1[:, 16 * T:32 * T].bitcast(F32).rearrange("p (a b) -> p a b", a=H)
    B_sgall = [Buf("sgall%d" % h) for h in range(H)]
    wslots = [sb("wslot%d" % i, [P, 8, 512], BF16) for i in range(NSLOT)]
    tbig = sb("tbig", [P, 4, T])
    tq, tsg, tgt, tlf = tbig[:, 0, :], tbig[:, 1, :], tbig[:, 2, :], tbig[:, 3, :]
    mixT = tbig[:, :, :].bitcast(BF16).rearrange("p a (c b) -> p (a c) b", b=T)
    tkk = sb("tkk", [P, T])
    tcum = sb("tcum", [P, T])
    tc32 = sb("tc32", [P, T])
    te1 = sb("te1", [P, T])
    te3 = sb("te3", [P, T])
    ten = sb("ten", [P, T])
    trc = sb("trc", [P, T])
    td1 = sb("td1", [P, NCH, 32])
    qt2 = [sb("qt%d" % i, [P, T], BF16) for i in range(2)]
    qs2 = [sb("qs%d" % i, [P, T], BF16) for i in range(2)]
    kd2 = [sb("kd%d" % i, [P, T], BF16) for i in range(2)]
    K02 = [sb("K0_%d" % i, [P, NCH, 32], BF16) for i in range(2)]
    K12 = [sb("K1_%d" % i, [P, NCH, 64], BF16) for i in range(2)]
    dec2 = [sb("dec%d" % i, [P, NCH]) for i in range(2)]
    kdT_sb = sb("kdT_sb", [P, NB, P], BF16)
    A_sb = sb("A_sb", [P, NB, P], BF16)
    osq = sb("osq", [P, T], BF16)
    Sst = sb("Sst", [P, NCH + 1, P])
    Sbf = sb("Sbf", [P, NCH, P], BF16)
    tokb = [arena1[:, 32 * T + i * D:32 * T + (i + 1) * D] for i in range(NB)]
    tok = [arena1[:, 40 * T + i * 2 * D:40 * T + (i + 1) * 2 * D].bitcast(F32) for i in range(2)]
    st6 = [sb("st6_%d" % i, [P, 12]) for i in range(2 * NB)]
    mv = [sb("mv%d" % i, [P, 4]) for i in range(2 * NB)]
    ftmp = [sb("ftmp%d" % i, [P, T]) for i in range(2)]
    ftmp2 = [sb("ftmp2_%d" % i, [P, T]) for i in range(2)]
    neghalf = sb("neghalf", [P, 1])
    S.add("pool", lambda e: e.memset(neghalf[:], -0.5), [], [B_const])

    B_h = [Buf("h%d" % b) for b in range(NB)]
    ch_h = [new_chan("h%d" % b) for b in range(NB)]
    B_uT = [Buf("uT%d" % b) for b in range(NB)]
    B_v = [[Buf("v%d_%d" % (b, j)) for j in range(2)] for b in range(NB)]
    B_vn = [Buf("vn%d" % b) for b in range(NB)]
    B_uG = [Buf("uG%d" % g) for g in range(H)]
    B_hgT = [Buf("hgT%d" % h) for h in range(H)]
    B_ubT = [Buf("ubT%d" % g) for g in range(H)]
    B_mixT = [Buf("mixT%d" % o) for o in range(8)]
    B_u2T = [Buf("u2T%d" % b) for b in range(NB)]
    B_actT = [Buf("actT%d" % j) for j in range(NFF)]
    ch_xpre = [new_chan("xpre%d" % b) for b in range(NB)]
    B_slot = [Buf("slot%d" % i) for i in range(NSLOT)]
    ch_slot = [new_chan("slot%d" % i) for i in range(NSLOT)]
    B_t = {n: Buf("t_" + n) for n in ("q", "sg", "gt", "lf", "kk", "cum", "c32", "e1", "e3", "en", "rc", "d1", "qt",
                                     "qs", "kd", "K0", "K1a", "K1b", "kdT", "A", "osq", "Sst", "Sbf")}
    B_p = [{n: Buf("p%d_%s" % (i, n)) for n in ("qt", "qs", "kd", "K0", "K1a", "K1b", "dec")} for i in range(2)]
    B_tok = [Buf("tok0"), Buf("tok1")]
    B_tokb = [Buf("tokb%d" % i) for i in range(NB)]
    tokb_main, B_tokb_main = tokb, B_tokb
    B_mv = [Buf("mv%d" % i) for i in range(2 * NB)]
    B_ftmp = [Buf("ftmp0"), Buf("ftmp1")]
    B_ftmp2 = [Buf("ftmp2_0"), Buf("ftmp2_1")]
    tokc = [0]
    ftc = [0]

    wmap = {}
    wnext = [0]

    def wload(name):
        if name in wmap:
            return
        i = wnext[0]
        wnext[0] = (i + 1) % NSLOT
        for k in [k for k, vv in wmap.items() if vv == i]:
            del wmap[k]
        wmap[name] = i
        src, gbuf, kcn = pieces[name]
        dma("sp", wslots[i][:, 0:kcn, :], src[:, 0:kcn, :], [gbuf], [B_slot[i]], ch_slot[i])

    wseq = [[]]

    def wget(name):
        wload(name)
        seq = wseq[0]
        if name in seq:
            i = seq.index(name)
            if i + 1 < len(seq):
                wload(seq[i + 1])
        i = wmap[name]
        return wslots[i], B_slot[i]

    def g_layernorm_stats_multi(items):
        for src_ap, src_bufs, k in items:
            S.add("dve", (lambda src_ap, k: lambda e: e.bn_stats(out=st6[k][:, 0:6], in_=src_ap[:, 0:512]))(src_ap, k),
                  src_bufs, [B_mv[k]])
            S.add("dve", (lambda src_ap, k: lambda e: e.bn_stats(out=st6[k][:, 6:12], in_=src_ap[:, 512:1024]))(src_ap, k),
                  src_bufs, [B_mv[k]])
        yield
        for src_ap, src_bufs, k in items:
            S.add("dve", (lambda k: lambda e: e.bn_aggr(out=mv[k][:, 0:2], in_=st6[k][:, 0:12]))(k), [B_mv[k]],
                  [B_mv[k]])
            ts("dve", mv[k][:, 2:3], mv[k][:, 1:2], LN_EPS, None, ALU.add, None, [B_mv[k]], [B_mv[k]])
        yield
        for src_ap, src_bufs, k in items:
            if RSTD_POW:
                tt("pool", mv[k][:, 2:3], mv[k][:, 2:3], neghalf[:], ALU.pow, [B_mv[k], B_const], [B_mv[k]])
            else:
                act(mv[k][:, 2:3], mv[k][:, 2:3], AF.Ln, [B_mv[k]], [B_mv[k]])
                act(mv[k][:, 2:3], mv[k][:, 2:3], AF.Exp, [B_mv[k]], [B_mv[k]], scale=-0.5)
        yield
        for src_ap, src_bufs, k in items:
            stt(mv[k][:, 3:4], mv[k][:, 0:1], -1.0, mv[k][:, 2:3], ALU.mult, ALU.mult, [B_mv[k]], [B_mv[k]])
        yield

    def layernorm_stats_multi(items):
        for _ in g_layernorm_stats_multi(items):
            pass

    def layernorm_stats(src_ap, src_bufs, k):
        layernorm_stats_multi([(src_ap, src_bufs, k)])

    def to_featmajor(src_bf, src_bufs, dstT, dst_buf, blk, scale_cols, bias_cols):
        pb, pbuf = psum()
        pv = pb.bitcast(BF16)[:, :].rearrange("p (a b) -> p a b", a=8)
        for kc in range(8):
            tr(pv[:, kc, :], src_bf[:, kc * P:(kc + 1) * P], ident_b[:], src_bufs + [B_const], [pbuf])
        for kc in range(8):
            act(dstT[:, kc, blk * P:(blk + 1) * P], pv[:, kc, :], AF.Identity, [pbuf, B_const], [dst_buf],
                scale=scale_cols[:, kc:kc + 1], bias=bias_cols[:, kc:kc + 1])

    def load_x(src, ti):
        for b in range(NB):
            r0 = (ti * NB + b) * P
            dma("act", h_sb[:, b, :], src[r0:r0 + P, :], [], [B_h[b]], ch_h[b])

    def g_adaln_all(dstT, B_dst, scale_cols, bias_cols, src=None, tb=None, koff=0):
        if src is None:
            src = [(h_sb[:, blk, :], [B_h[blk]]) for blk in range(NB)]
        tokb, B_tokb = tb if tb is not None else (tokb_main, B_tokb_main)
        yield from g_layernorm_stats_multi([(src[blk][0], src[blk][1], blk + koff) for blk in range(NB)])
        for blk in range(NB):
            act(tokb[blk][:], src[blk][0], AF.Identity, src[blk][1] + [B_mv[blk + koff]], [B_tokb[blk]],
                scale=mv[blk + koff][:, 2:3], bias=mv[blk + koff][:, 3:4])
            yield
        for j in range(4):
            pb, pbuf = psum()
            pv = pb.bitcast(BF16)[:, :].rearrange("p (k a b) -> p k a b", k=2, a=NB)
            for k2 in range(2):
                kc = 2 * j + k2
                for blk in range(NB):
                    tr(pv[:, k2, blk, :], tokb[blk][:, kc * P:(kc + 1) * P], ident_b[:], [B_tokb[blk], B_const],
                       [pbuf])
                yield
            for k2 in range(2):
                kc = 2 * j + k2
                act(dstT[:, kc, :], pv[:, k2, :, :].rearrange("p a b -> p (a b)"), AF.Identity, [pbuf, B_const],
                    B_dst, scale=scale_cols[:, kc:kc + 1], bias=bias_cols[:, kc:kc + 1])
                yield

    def adaln_all(*a_, **k_):
        for _ in g_adaln_all(*a_, **k_):
            pass

    def proj_feat(wt, wbuf, off, xT, xbufs, pool="all"):
        pb, pbuf = psum(pool)
        for kc in range(8):
            mm(pb[:], wt[:, kc, off:off + P], xT[:, kc, :], kc == 0, kc == 7, [wbuf] + xbufs, [pbuf])
        return pb, pbuf

    def proj_tok(wt, wbuf, xT, xbuf, blk):
        pb, pbuf = psum()
        for kc in range(8):
            mm(pb[:], xT[:, kc, blk * P:(blk + 1) * P], wt[:, kc, :], kc == 0, kc == 7, [wbuf, xbuf], [pbuf])
        return pb, pbuf

    def make_v():
        for j in range(2):
            wt, wbuf = wget("i%d" % j)
            for blk in range(NB):
                pb, pbuf = proj_tok(wt, wbuf, uT, B_uT[blk], blk)
                copy("act" if blk % 2 == 0 else "dve", v_sb[:, blk, j * 512:(j + 1) * 512], pb[:], [pbuf],
                     [B_v[blk][j]])

    c3 = lambda t_: t_[:, :].rearrange("p (c s) -> p c s", s=64)

    def hgrn_head_z(h, full):
        j, off = h // 4, (h % 4) * P
        res = {}
        names = ("f", "q", "g") if full else ("f",)
        for nm in names:
            wt, wbuf = wget("%s%d" % (nm, j))
            res[nm] = proj_feat(wt, wbuf, off, uT, B_uT, pool="z")
            yield res

    def hgrn_head_ew(h, z, full, pre_sg=False):
        B = dict(B_t)
        p = h % 2
        Bp = B_p[p]
        qt, qs, kd, K0, K1, dec = qt2[p], qs2[p], kd2[p], K02[p], K12[p], dec2[p]
        tgt_ = ftmp2[p]
        tsg = tbig[:, 1, :]
        if pre_sg:
            tsg = sgall[:, h, :]
            B["sg"] = B_sgall[h]
        else:
            zf, zfb = z["f"]
            act(tsg[:], zf[:], AF.Sigmoid, [zfb], [B["sg"]])
            yield
        if full:
            zq, zqb = z["q"]
            zg, zgb = z["g"]
            act(tq[:], zq[:], AF.Silu, [zqb], [B["q"]])
            yield
            act(tgt_[:], zg[:], AF.Silu, [zgb], [B_ftmp2[p]])
            yield
        act(tlf[:], tsg[:], AF.Ln, [B["sg"], B_const], [B["lf"]], scale=oml[:, h:h + 1], bias=lb[:, h:h + 1])
        yield
        S.add("dve", lambda e: e.tensor_tensor_scan(out=tcum[:], data0=m64[:], data1=tlf[:], initial=0.0,
                                                    op0=ALU.mult, op1=ALU.add), [B["lf"], B_const], [B["cum"]])
        yield
        if full:
            S.add("dve", lambda e: e.tensor_tensor_scan(out=tc32[:], data0=m32[:], data1=tlf[:], initial=0.0,
                                                        op0=ALU.mult, op1=ALU.add), [B["lf"], B_const], [B["c32"]])
            yield
        ts("dve", tkk[:], tsg[:], noml[:, h:h + 1], oml[:, h:h + 1], ALU.mult, ALU.add, [B["sg"], B_const], [B["kk"]])
        yield
        act(te1[:], tcum[:], AF.Exp, [B["cum"]], [B["e1"]])
        yield
        if full:
            act(te3[:], tc32[:], AF.Exp, [B["c32"]], [B["e3"]])
            yield
            act(ten[:], tc32[:], AF.Exp, [B["c32"]], [B["en"]], scale=-1.0)
            yield
        cum3 = c3(tcum)
        tt("dve", c3(trc), cum3[:, :, 63:64].broadcast_to([P, NCH, 64]), cum3, ALU.subtract, [B["cum"]], [B["rc"]])
        yield
        if full:
            tt("dve", td1[:], cum3[:, :, 31:32].broadcast_to([P, NCH, 32]), cum3[:, :, 0:32], ALU.subtract,
               [B["cum"]], [B["d1"]])
            yield
        act(trc[:], trc[:], AF.Exp, [B["rc"]], [B["rc"]])
        yield
        if full:
            act(td1[:], td1[:], AF.Exp, [B["d1"]], [B["d1"]])
            yield
        copy("pool", dec[:], c3(te1)[:, :, 63], [B["e1"]], [Bp["dec"]])
        yield
        if full:
            kk3, en3 = c3(tkk), c3(ten)
            tt("dve", qt[:], tq[:], te1[:], ALU.mult, [B["q"], B["e1"]], [Bp["qt"]])
            yield
            tt("dve", qs[:], tq[:], te3[:], ALU.mult, [B["q"], B["e3"]], [Bp["qs"]])
            yield
            tt("dve", K0[:], kk3[:, :, 0:32], en3[:, :, 0:32], ALU.mult, [B["kk"], B["en"]], [Bp["K0"]])
            yield
            tt("dve", K1[:, :, 32:64], kk3[:, :, 32:64], en3[:, :, 32:64], ALU.mult, [B["kk"], B["en"]], [Bp["K1b"]])
            yield
        tt("dve", kd[:], tkk[:], trc[:], ALU.mult, [B["kk"], B["rc"]], [Bp["kd"]])
        yield
        if full:
            tt("dve", K1[:, :, 0:32], kk3[:, :, 0:32], td1[:], ALU.mult, [B["kk"], B["d1"]], [Bp["K1a"]])
            yield

    def hgrn_head_small(h, full):
        B = B_t
        p = h % 2
        Bp = B_p[p]
        qt, qs, kd, K0, K1, dec = qt2[p], qs2[p], kd2[p], K02[p], K12[p], dec2[p]
        tgt_, tX = ftmp2[p], ftmp[p]
        vb_all = [B_v[b][h // 4] for b in range(NB)]
        vcol = slice(h * P, (h + 1) * P)
        if full:
            pa, pab = psum("m")
            pav = pa[:, :].rearrange("p (a b) -> p a b", a=NB)
            S.add("dve", lambda e: e.memset(pa[:], 0.0), [], [pab])
            yield
            for c in range(NCH):
                blk, po = c // 2, (c % 2) * 64
                mm(pav[po:po + 32, blk, po:po + 32], K0[:, c, :], qs[:, c * 64:c * 64 + 32], True, True,
                   [Bp["K0"], Bp["qs"]], [pab])
                mm(pav[po:po + 64, blk, po + 32:po + 64], K1[:, c, :], qs[:, c * 64 + 32:c * 64 + 64], True, True,
                   [Bp["K1a"], Bp["K1b"], Bp["qs"]], [pab])
            yield
            mask_b = bass.AP(mask2[:].tensor, mask2[:].offset, [list(mask2[:].ap[0]), [0, NB], [1, P]])
            tt("dve", A_sb[:], pav, mask_b, ALU.mult, [pab, B_const], [B["A"]])
            yield
        pk, pkb = psum("m")
        pkv = pk.bitcast(BF16)[:, 0:NB * P].rearrange("p (a b) -> p a b", a=NB)
        for blk in range(NB):
            tr(pkv[:, blk, :], kd[:, blk * P:(blk + 1) * P], ident_b[:], [Bp["kd"], B_const], [pkb])
        yield
        copy("act", kdT_sb[:], pkv, [pkb], [B["kdT"]])
        yield
        Ue, Ueb = psum("m")
        Uo, Uob = psum("m")
        Uv = (Ue[:, :].rearrange("p (a b) -> p a b", a=NB), Uo[:, :].rearrange("p (a b) -> p a b", a=NB))
        Ub = (Ueb, Uob)
        for c in range(NCH):
            blk, po = c // 2, (c % 2) * 64
            mm(Uv[c % 2][:, blk, :], kdT_sb[po:po + 64, blk, :], v_sb[po:po + 64, blk, vcol], True, True,
               [B["kdT"], vb_all[blk]], [Ub[c % 2]])
        yield
        copy("pool", Sst[:, 0, :], carry[:, h, :], [B_carry[h]], [B["Sst"]])
        yield
        for c in range(NCH):
            stt(Sst[:, c + 1, :], Sst[:, c, :], dec[:, c:c + 1], Uv[c % 2][:, c // 2, :], ALU.mult,
                ALU.add, [B["Sst"], Bp["dec"], Ub[c % 2]], [B["Sst"]])
            yield
        copy("pool", carry[:, h, :], Sst[:, NCH, :], [B["Sst"]], [B_carry[h]])
        yield
        if not full:
            return
        copy("act", Sbf[:], Sst[:, 0:NCH, :], [B["Sst"]], [B["Sbf"]])
        yield
        po_, pob = psum("m")
        for blk in range(NB):
            mm(po_[:, blk * P:(blk + 1) * P], v_sb[:, blk, vcol], A_sb[:, blk, :], True, False,
               [vb_all[blk], B["A"]], [pob])
            for c in (2 * blk, 2 * blk + 1):
                mm(po_[:, c * 64:(c + 1) * 64], Sbf[:, c, :], qt[:, c * 64:(c + 1) * 64], False, c % 2 == 1,
                   [B["Sbf"], Bp["qt"]], [pob])
        yield
        act(osq[:], po_[:], AF.Square, [pob], [B["osq"]])
        yield
        pss, pssb = psum("m")
        mm(pss[:], ones_b[:], osq[:], True, True, [B["osq"], B_const], [pssb])
        yield
        act(tX[:], pss[:], AF.Ln, [pssb], [B_ftmp[p]], scale=1.0 / P, bias=RMS_EPS)
        yield
        act(tX[:], tX[:], AF.Exp, [B_ftmp[p]], [B_ftmp[p]], scale=-0.5)
        yield
        tt("dve", tX[:], po_[:], tX[:], ALU.mult, [pob, B_ftmp[p]], [B_ftmp[p]])
        yield
        stt(hgT[:, h, :], tX[:], normw, tgt_[:], ALU.mult, ALU.mult, [B_ftmp[p], B_ftmp2[p], B_const],
            [B_hgT[h]] + B_tokb + B_xpre)
        yield

    def delayed(gen, n):
        for _ in range(n):
            yield
        yield from gen

    def run_merged(gens, weights=None):
        active = [(g, (weights[i] if weights else 1)) for i, g in enumerate(gens) if g is not None]
        while active:
            for g, w in list(active):
                try:
                    for _ in range(w):
                        next(g)
                except StopIteration:
                    active.remove((g, w))

    def hgrn_all(full):
        zres = {}

        def zgen(h):
            last = None
            for last in hgrn_head_z(h, full):
                yield
            zres[h] = last

        if not full:
            for h in range(H):
                run_merged([zgen(h)])
                zf, zfb = zres[h]["f"]
                act(sgall[:, h, :], zf[:], AF.Sigmoid, [zfb], [B_sgall[h]])
            for i in range(-1, H):
                gens, wts = [], []
                if i + 1 < H:
                    gens.append(hgrn_head_ew(i + 1, None, False, pre_sg=True))
                    wts.append(MERGE_W[0])
                if i >= 0:
                    gens.append(hgrn_head_small(i, False))
                    wts.append(MERGE_W[1])
                run_merged(gens, wts)
            return
        run_merged([zgen(0)])
        for i in range(-1, H):
            gens, wts = [], []
            if i + 1 < H:
                gens.append(hgrn_head_ew(i + 1, zres[i + 1], full))
                wts.append(MERGE_W[0])
            if i >= 0:
                gens.append(hgrn_head_small(i, full))
                wts.append(MERGE_W[1])
            if i + 2 <= H - 1:
                gens.append(delayed(zgen(i + 2), 4 if full else 2))
                wts.append(1)
            run_merged(gens, wts)


    def gmlp():
        for g in range(H):
            wt, wbuf = wget("u%d" % (g // 4))
            pb, pbuf = proj_feat(wt, wbuf, (g % 4) * P, uT, B_uT)
            act(uG[:, g, :], pb[:], AF.Gelu, [pbuf], [B_uG[g]])
        for blk in range(NB):
            k = tokc[0] % 2
            tokc[0] += 1
            for j in range(2):
                wt, wbuf = wget("v%d" % j)
                pb, pbuf = proj_tok(wt, wbuf, uT, B_uT[blk], blk)
                act(tok[k][:, j * 512:(j + 1) * 512], pb[:], AF.Gelu, [pbuf], [B_tok[k]] + B_xpre)
            layernorm_stats(tok[k], [B_tok[k]], k)
            ts("dve", vn_sb[:, blk, :], tok[k][:], mv[k][:, 0:1], mv[k][:, 2:3], ALU.subtract, ALU.mult,
               [B_tok[k], B_mv[k]], [B_vn[blk]] + B_tokb2)
        for g in range(H):
            pb, pbuf = psum()
            for blk in range(NB):
                mm(pb[:, blk * P:(blk + 1) * P], vn_sb[:, blk, g * P:(g + 1) * P], wmT[:, g, :], True, True,
                   [B_vn[blk], B_const], [pbuf])
            k = ftc[0] % 2
            ftc[0] += 1
            r_b = bass.AP(r_sb[:].tensor, r_sb[:, g, :].offset, [list(r_sb[:].ap[0]), [0, NB], [1, P]])
            stt(ftmp[k][:, :].rearrange("p (a b) -> p a b", a=NB), pb[:, :].rearrange("p (a b) -> p a b", a=NB),
                lnw_col[:, g:g + 1], r_b, ALU.mult, ALU.add, [pbuf, B_const], [B_ftmp[k]])
            tt("pool", ubT[:, g, :], ftmp[k][:], uG[:, g, :], ALU.mult, [B_ftmp[k], B_uG[g]],
               [B_ubT[g], B_tok[0], B_tok[1]])

    def mixer_out():
        for oc in range(8):
            j, off = oc // 4, (oc % 4) * P
            wa_, wab = wget("pa%d" % j)
            pya, pyab = proj_feat(wa_, wab, off, hgT, B_hgT)
            wb_, wbb = wget("pb%d" % j)
            pyb, pybb = proj_feat(wb_, wbb, off, ubT, B_ubT)
            wga, wgab = wget("ga%d" % j)
            pga, pgab = proj_feat(wga, wgab, off, uT, B_uT)
            wgb, wgbb = wget("gb%d" % j)
            pgb, pgbb = proj_feat(wgb, wgbb, off, uT, B_uT)
            k = ftc[0] % 2
            ftc[0] += 1
            act(ftmp[k][:], pga[:], AF.Sigmoid, [pgab, B_const], [B_ftmp[k]], bias=bga[:, oc:oc + 1])
            act(ftmp2[k][:], pgb[:], AF.Sigmoid, [pgbb, B_const], [B_ftmp2[k]], bias=bgb[:, oc:oc + 1])
            tt("dve", ftmp[k][:], ftmp[k][:], pya[:], ALU.mult, [B_ftmp[k], pyab], [B_ftmp[k]])
            tt("dve", ftmp2[k][:], ftmp2[k][:], pyb[:], ALU.mult, [B_ftmp2[k], pybb], [B_ftmp2[k]])
            tt("pool", mixT[:, oc, :], ftmp[k][:], ftmp2[k][:], ALU.add, [B_ftmp[k], B_ftmp2[k]],
               [B_mixT[oc], B_t["q"], B_t["sg"], B_t["gt"], B_t["lf"]])

    def g_post_ln_all(w_bc, b_bc):
        yield from g_layernorm_stats_multi([(h_sb[:, blk, :], [B_h[blk]], blk) for blk in range(NB)])
        for blk in range(NB):
            act(h_sb[:, blk, :], h_sb[:, blk, :], AF.Identity, [B_h[blk], B_mv[blk]], [B_h[blk]],
                scale=mv[blk][:, 2:3], bias=mv[blk][:, 3:4])
            yield
        for blk in range(NB):
            tt("dve" if blk % 2 == 0 else "pool", h_sb[:, blk, :], h_sb[:, blk, :], w_bc[:], ALU.mult,
               [B_h[blk], B_const], [B_h[blk]])
            yield
        for blk in range(NB):
            tt("pool" if blk == 2 else "dve", h_sb[:, blk, :], h_sb[:, blk, :], b_bc[:], ALU.add,
               [B_h[blk], B_const], [B_h[blk]])
            yield

    def post_ln_all(*a_):
        for _ in g_post_ln_all(*a_):
            pass

    def residual_update(pb, pbuf, blk, hs, g_bc):
        k = ftc[0] % 2
        ftc[0] += 1
        tt("dve", ftmp[k][:], pb[:], g_bc[:, hs], ALU.mult, [pbuf, B_const], [B_ftmp[k]])
        stt(h_sb[:, blk, hs], h_sb[:, blk, hs], ALPHA, ftmp[k][:], ALU.mult, ALU.add, [B_h[blk], B_ftmp[k]],
            [B_h[blk]])

    def mix_and_ln1():
        for blk in range(NB):
            for half in range(2):
                wt, wbuf = wget("o%d" % half)
                pb, pbuf = psum()
                for kc in range(8):
                    mm(pb[:], mixT[:, kc, blk * P:(blk + 1) * P], wt[:, kc, :], kc == 0, kc == 7,
                       [wbuf] + B_mixT, [pbuf])
                residual_update(pb, pbuf, blk, slice(half * 512, (half + 1) * 512), g1_bc)
        post_ln_all(ln1w_bc, ln1b_bc)
        adaln_all(u2T, B_u2T, sc2p, sh2)

    def prefetch_x(ti):
        for b in range(NB):
            r0 = (ti * NB + b) * P
            dma("act", xpre[:, b, :], x_main[r0:r0 + P, :], [], [B_xpre[b]] + B_tokb + B_tok, ch_xpre[b])

    def ffn(next_ti=None):
        if next_ti is not None:
            prefetch_x(next_ti)
        for m in range(11):
            wt, wbuf = wget("fi%d" % m)
            for jj in range(2):
                j = 2 * m + jj
                pa_, pab_ = proj_feat(wt, wbuf, jj * P, u2T, B_u2T)
                pbb, pbbb = proj_feat(wt, wbuf, 256 + jj * P, u2T, B_u2T)
                k = ftc[0] % 2
                ftc[0] += 1
                act(ftmp2[k][:], pa_[:], AF.Silu, [pab_], [B_ftmp2[k]])
                tt("dve", actT[:, j, :], ftmp2[k][:], pbb[:], ALU.mult, [B_ftmp2[k], pbbb], [B_actT[j]])
        for half in range(2):
            hs = slice(half * 512, (half + 1) * 512)
            pbs = [psum() for _ in range(NB)]
            for kg in range(3):
                wt, wbuf = wget("fo%d_%d" % (kg, half))
                k0, k1 = kg * 8, min(NFF, kg * 8 + 8)
                for blk in range(NB):
                    for kc in range(k0, k1):
                        mm(pbs[blk][0][:], actT[:, kc, blk * P:(blk + 1) * P], wt[:, kc - k0, :], kc == 0,
                           kc == NFF - 1, [wbuf, B_actT[kc]], [pbs[blk][1]])
            for blk in range(NB):
                residual_update(pbs[blk][0], pbs[blk][1], blk, hs, g2_bc)

    B_out = [Buf("out%d" % b) for b in range(NB)]

    def g_ln2_and_store(ti):
        yield from g_post_ln_all(ln2w_bc, ln2b_bc)
        for blk in range(NB):
            r0 = (ti * NB + blk) * P
            dma("sp", out_d[r0:r0 + P, :], h_sb[:, blk, :], [B_h[blk]], [B_out[blk]], ch_h[blk])
            yield

    SEQ_WARM = ["i0", "i1", "f0", "f1"]
    SEQ_MAIN = (["i0", "i1", "q0", "f0", "g0", "q1", "f1", "g1", "u0", "u1", "v0", "v1",
                 "pa0", "pb0", "ga0", "gb0", "pa1", "pb1", "ga1", "gb1", "o0", "o1"]
                + ["fi%d" % m for m in range(11)]
                + ["fo%d_%d" % (kg, half) for half in range(2) for kg in range(3)])

    for ti in range(n_tiles):
        wseq[0] = SEQ_WARM + (SEQ_WARM[:1] if ti + 1 < n_tiles else SEQ_MAIN[:1])
        issue_deferred(len(deferred) if ti == n_tiles - 1 else -(-len(deferred) // (n_tiles - ti)))
        load_x(x_warm, ti)
        adaln_all(uT, B_uT, sc1p, sh1)
        make_v()
        hgrn_all(False)
    for h in range(H):
        ts("dve", carry[:, h, :], carry[:, h, :], flag[:, 0:1], None, ALU.mult, None, [B_carry[h], B_const],
           [B_carry[h]])
    dump("carry", carry[:], B_carry[H - 1])
    S.barrier(exclude=(ch_A.key, ch_B.key))

    for ti in range(n_tiles):
        wseq[0] = SEQ_MAIN + (SEQ_MAIN[:1] if ti + 1 < n_tiles else [])
        if ti == 0:
            load_x(x_main, ti)
            adaln_all(uT, B_uT, sc1p, sh1)
        if ti == 0:
            dump("uT", uT, B_uT[NB - 1])
        make_v()
        if ti == 0:
            dump("v", v_sb, B_v[NB - 1][1])
        hgrn_all(True)
        if ti == 0:
            dump("hgT", hgT, B_hgT[H - 1])
        gmlp()
        if ti == 0:
            dump("ubT", ubT, B_ubT[H - 1])
        mixer_out()
        if ti == 0:
            dump("mixT", mixT, B_mixT[7])
        wload("o0")
        wload("o1")
        S.barrier()
        mix_and_ln1()
        if ti == 0:
            dump("u2T", u2T, B_u2T[NB - 1])
            dump("h1", h_sb[:], B_h[NB - 1])
        ffn(ti + 1 if ti + 1 < n_tiles else None)
        S.barrier()
        if ti + 1 < n_tiles:
            run_merged([g_ln2_and_store(ti),
                        g_adaln_all(uT, B_uT, sc1p, sh1, src=[(xpre[:, b, :], [B_xpre[b]]) for b in range(NB)],
                                    tb=(tokb2, B_tokb2), koff=NB)])
            for b in range(NB):
                dma("pool", h_sb[:, b, :], xpre[:, b, :], [B_xpre[b]], [B_h[b]], ch_h[b])
        else:
            run_merged([g_ln2_and_store(ti)])

    S.add("sp", None, B_out + dump_list, [])
    S.emit(nc, sems)
    es.close()
    return nc


_CONST_CACHE = {}


def _consts():
    if not _CONST_CACHE:
        s = np.arange(P)
        mask2 = ((s[:, None] // 64 == s[None, :] // 64) & (s[:, None] <= s[None, :])).astype(np.float32)
        t = np.arange(T)
        m64 = np.broadcast_to((t % 64 != 0).astype(np.float32), (P, T)).copy()
        m32 = np.broadcast_to((t % 32 != 0).astype(np.float32), (P, T)).copy()
        _CONST_CACHE.update(
            ident_f=np.eye(P, dtype=np.float32),
            ident_b=np.eye(P, dtype=np.float32).astype(ml_dtypes.bfloat16),
            mask2=mask2, m64=m64, m32=m32)
    return _CONST_CACHE


def make_in_maps(inp, n_tiles, seq):
    f = lambda a: np.ascontiguousarray(np.asarray(a, dtype=np.float32))
    x = f(inp["x"])
    Tc = n_tiles * T
    assert seq == 2 * Tc
    shared = dict(
        w_ada=f(inp["w_ada"][0]), b_ada=f(inp["b_ada"][0]).reshape(1, -1), w_in=f(inp["w_in"][0]),
        w_proj_a=f(inp["w_proj_a"][0]), w_proj_b=f(inp["w_proj_b"][0]), w_out=f(inp["w_out"][0]),
        w_ffn_in=f(inp["w_ffn_in"][0]), w_ffn_out=f(inp["w_ffn_out"][0]),
        gmlp_ln_b=f(inp["gmlp_ln_b"][0]).reshape(1, -1), gmlp_ws=f(inp["gmlp_ws"][0]),
        gmlp_bs=f(inp["gmlp_bs"][0]).reshape(1, -1),
        ln1_w=f(inp["ln1_w"][0]).reshape(1, -1), ln1_b=f(inp["ln1_b"][0]).reshape(1, -1),
        ln2_w=f(inp["ln2_w"][0]).reshape(1, -1), ln2_b=f(inp["ln2_b"][0]).reshape(1, -1),
        **_consts())
    b_ada = f(inp["b_ada"][0])
    maps = []
    for core in range(N_CORES):
        b, half = core // 2, core % 2
        rows = np.concatenate([
            f(inp["c"])[b].reshape(8, P),
            f(inp["gmlp_ln_w"][0]).reshape(8, P),
            f(inp["b_gate"][0, 0]).reshape(8, P),
            f(inp["b_gate"][0, 1]).reshape(8, P),
            f(inp["hgrn_lb_logits"][0]).reshape(8, P),
            f(inp["hgrn_lb_logits"][1]).reshape(8, P),
            f(inp["hgrn_norm_w"][0]).reshape(1, P),
            b_ada[0:D].reshape(8, P), b_ada[D:2 * D].reshape(8, P),
            b_ada[3 * D:4 * D].reshape(8, P), b_ada[4 * D:5 * D].reshape(8, P)], axis=0)
        m = dict(shared)
        m["x_main"] = np.ascontiguousarray(x[b, half * Tc:(half + 1) * Tc])
        m["x_warm"] = np.ascontiguousarray(x[b, 0:Tc])
        m["rows"] = np.ascontiguousarray(rows)
        m["flag"] = np.full((P, 1), float(half), np.float32)
        maps.append(m)
    return maps


_NC_CACHE = {}


def run(inp, n_tiles, seq, dumps=None):
    key = (n_tiles, tuple(sorted(dumps)) if dumps else None)
    if key not in _NC_CACHE:
        _NC_CACHE[key] = build(n_tiles, dumps)
    nc = _NC_CACHE[key]
    maps = make_in_maps(inp, n_tiles, seq)
    res = run_bass_kernel_spmd(nc, maps, core_ids=list(range(N_CORES)))
    Tc = n_tiles * T
    out = np.empty((BATCH, seq, D), np.float32)
    for core in range(N_CORES):
        b, half = core // 2, core % 2
        out[b, half * Tc:(half + 1) * Tc] = res.results[core]["out"]
    return out, res


def kernel(**inputs):
    out, _ = run(inputs, SEQ // 2 // T, SEQ)
    return out
```

```python
from contextlib import ExitStack

import ml_dtypes
import numpy as np

import concourse.bass as bass
import concourse.mybir as mybir
from concourse.bass_utils import run_bass_kernel_spmd

F32 = mybir.dt.float32
BF16 = mybir.dt.bfloat16
AF = mybir.ActivationFunctionType
ALU = mybir.AluOpType

P = 128
D = 1024
T = 512
NB = T // P
NCH = T // 64
H = 8
DFF = 2816
NFF = DFF // P
ALPHA = 2.0 ** 0.25
LN_EPS = 1e-5
RMS_EPS = 1e-6
NROWS = 81
NSLOT = 5
RSTD_POW = True
MERGE_W = (1, 1)
N_CORES = 8
SEQ = 8192
BATCH = 4


class Buf:
    __slots__ = ("name", "w", "r")

    def __init__(self, name):
        self.name = name
        self.w = {}
        self.r = {}


class Chan:
    __slots__ = ("sem", "count", "key")

    def __init__(self, sem, key):
        self.sem = sem
        self.count = 0
        self.key = key


class Sched:
    def __init__(self):
        self.ops = []
        self.last = {}
        self.bar = {}

    def add(self, eng, fn, reads=(), writes=(), chan=None, skip_self=False):
        oid = len(self.ops)
        key = chan.key if chan is not None else eng
        deps = dict(self.bar)

        def need(evs):
            for k, o in evs.items():
                if skip_self and k == key:
                    continue
                if deps.get(k, -1) < o:
                    deps[k] = o

        for b in reads:
            need(b.w)
        for b in writes:
            need(b.w)
            need(b.r)
        val = None
        if chan is not None:
            chan.count += 1
            val = chan.count * 16
        self.ops.append(dict(eng=eng, fn=fn, deps=deps, chan=chan, key=key, marked=False, val=val))
        for b in reads:
            b.r[key] = oid
        for b in writes:
            b.w = {key: oid}
            b.r = {}
        self.last[key] = oid
        return oid

    def barrier(self, exclude=()):
        self.bar = {k: v for k, v in self.last.items() if k not in exclude}

    def emit(self, nc, sems):
        ops = self.ops
        for op in ops:
            for k, o in op["deps"].items():
                if k == "pe" and op["eng"] == "pe" and op["chan"] is None:
                    continue
                ops[o]["marked"] = True
        cnt = {}
        for op in ops:
            if op["chan"] is None and op["marked"]:
                cnt[op["key"]] = cnt.get(op["key"], 0) + 1
                op["val"] = cnt[op["key"]]
        by_eng = {}
        for op in ops:
            by_eng.setdefault(op["eng"], []).append(op)

        def body(engname):
            def f(e):
                waited = {}
                for op in by_eng.get(engname, []):
                    for k, o in sorted(op["deps"].items(), key=lambda kv: str(kv[0])):
                        if k == "pe" and engname == "pe" and op["chan"] is None:
                            continue
                        p = ops[o]
                        sem = p["chan"].sem if p["chan"] is not None else sems[p["key"]]
                        v = p["val"]
                        if waited.get(k, 0) >= v:
                            continue
                        e.wait_ge(sem, v)
                        waited[k] = v
                    if op["fn"] is None:
                        continue
                    ins = op["fn"](e)
                    if op["chan"] is not None:
                        ins.then_inc(op["chan"].sem, 16)
                    elif op["marked"]:
                        ins.then_inc(sems[engname], 1)

            return f

        with nc.Block() as block:
            block.tensor(body("pe"))
            block.scalar(body("act"))
            block.vector(body("dve"))
            block.gpsimd(body("pool"))
            block.sync(body("sp"))


def build(n_tiles, dumps=None):
    nc = bass.Bass("TRN2", target_bir_lowering=False)
    Tc = n_tiles * T
    es = ExitStack()
    S = Sched()
    dump_list = []

    def din(name, shape, dtype=F32):
        return nc.dram_tensor(name, list(shape), dtype, kind="ExternalInput").ap()

    x_main = din("x_main", [Tc, D])
    x_warm = din("x_warm", [Tc, D])
    rows_in = din("rows", [NROWS, P])
    flag_in = din("flag", [P, 1])
    w_ada = din("w_ada", [D, 6 * D])
    b_ada = din("b_ada", [1, 6 * D])
    w_in = din("w_in", [D, 8 * D])
    w_pa = din("w_proj_a", [D, D])
    w_pb = din("w_proj_b", [D, D])
    w_o = din("w_out", [D, D])
    w_fi = din("w_ffn_in", [D, 2 * DFF])
    w_fo = din("w_ffn_out", [DFF, D])
    lnb_in = din("gmlp_ln_b", [1, D])
    ws_in = din("gmlp_ws", [H, P, P])
    bs_in = din("gmlp_bs", [1, H * P])
    ln1w_in = din("ln1_w", [1, D])
    ln1b_in = din("ln1_b", [1, D])
    ln2w_in = din("ln2_w", [1, D])
    ln2b_in = din("ln2_b", [1, D])
    identf_in = din("ident_f", [P, P])
    identb_in = din("ident_b", [P, P], BF16)
    mask2_in = din("mask2", [P, P])
    m64_in = din("m64", [P, T])
    m32_in = din("m32", [P, T])
    out_d = nc.dram_tensor("out", [Tc, D], F32, kind="ExternalOutput").ap()

    wi_bf = nc.dram_tensor("wi_bf", [16, P, 8, 512], BF16).ap()
    wpa_bf = nc.dram_tensor("wpa_bf", [2, P, 8, 512], BF16).ap()
    wpb_bf = nc.dram_tensor("wpb_bf", [2, P, 8, 512], BF16).ap()
    wo_bf = nc.dram_tensor("wo_bf", [2, P, 8, 512], BF16).ap()
    wfi_bf = nc.dram_tensor("wfi_bf", [11, P, 8, 512], BF16).ap()
    wfo_bf = nc.dram_tensor("wfo_bf", [6, P, 8, 512], BF16).ap()

    nsem = [0]

    def new_sem(name):
        nsem[0] += 1
        return es.enter_context(nc.semaphore(name))

    sems = {k: new_sem("s_" + k) for k in ("pe", "act", "dve", "pool", "sp")}

    def new_chan(name):
        return Chan(new_sem("c_" + name), ("dma", name))

    def sb(name, shape, dtype=F32):
        return es.enter_context(nc.sbuf_tensor("sb_" + name, list(shape), dtype))

    def dma(q, out, in_, reads, writes, chan, skip_self=False):
        S.add(q, lambda e: e.dma_start(out=out, in_=in_), reads, writes, chan=chan, skip_self=skip_self)

    def mm(out, lhsT, rhs, start, stop, reads, writes):
        S.add("pe", lambda e: e.matmul(out, lhsT, rhs, start=start, stop=stop), reads, writes)

    def tr(out, in_, ident, reads, writes):
        S.add("pe", lambda e: e.transpose(out, in_, ident), reads, writes)

    def act(out, in_, func, reads, writes, scale=1.0, bias=0.0):
        S.add("act", lambda e: e.activation(out=out, in_=in_, func=func, bias=bias, scale=scale), reads, writes)

    def tt(eng, out, in0, in1, op, reads, writes):
        S.add(eng, lambda e: e.tensor_tensor(out=out, in0=in0, in1=in1, op=op), reads, writes)

    def ts(eng, out, in0, s1, s2, op0, op1, reads, writes):
        if op1 is None:
            S.add(eng, lambda e: e.tensor_scalar(out=out, in0=in0, scalar1=s1, scalar2=None, op0=op0), reads, writes)
        else:
            S.add(eng, lambda e: e.tensor_scalar(out=out, in0=in0, scalar1=s1, scalar2=s2, op0=op0, op1=op1),
                  reads, writes)

    def stt(out, in0, scalar, in1, op0, op1, reads, writes):
        S.add("dve", lambda e: e.scalar_tensor_tensor(out=out, in0=in0, scalar=scalar, in1=in1, op0=op0, op1=op1),
              reads, writes)

    def copy(eng, out, in_, reads, writes):
        if eng == "act":
            act(out, in_, AF.Copy, reads, writes)
        else:
            S.add(eng, lambda e: e.tensor_copy(out=out, in_=in_), reads, writes)

    def dump(name, ap, buf):
        if dumps is None or name not in dumps:
            return
        shape = list(ap.shape)
        d = nc.dram_tensor("dbg_" + name, shape, ap.dtype, kind="ExternalOutput").ap()
        ch = new_chan("dbg_" + name)
        b = Buf("dbgd_" + name)
        dma("pool", d, ap, [buf], [b], ch)
        dump_list.append(b)

    NBANK = 8
    banks = [es.enter_context(nc.psum_tensor("bank%d" % i, [P, 512], F32)) for i in range(NBANK)]
    bank_bufs = [Buf("bank%d" % i) for i in range(NBANK)]
    rings = {"all": [0, list(range(NBANK))], "z": [0, [0, 1, 2]], "m": [0, [3, 4, 5, 6, 7]]}

    def psum(pool="all"):
        r = rings[pool]
        i = r[1][r[0] % len(r[1])]
        r[0] += 1
        return banks[i], bank_bufs[i]

    ident_f = sb("ident_f", [P, P])
    ident_b = sb("ident_b", [P, P], BF16)
    mask2 = sb("mask2", [P, P])
    m64 = sb("m64", [P, T])
    m32 = sb("m32", [P, T])
    ones_f = sb("ones_f", [P, P])
    ones_b = sb("ones_b", [P, P], BF16)
    colv = sb("colv", [P, NROWS])
    modv = sb("modv", [P, 32])
    scp = sb("scp", [P, 16])
    lbv = sb("lbv", [P, 24])
    flag = sb("flag", [P, 1])
    wmT = sb("wmT", [P, H, P], BF16)
    r_sb = sb("r_sb", [P, H, P])
    g1_bc = sb("g1_bc", [P, D])
    g2_bc = sb("g2_bc", [P, D])
    ln1w_bc = sb("ln1w_bc", [P, D])
    ln1b_bc = sb("ln1b_bc", [P, D])
    ln2w_bc = sb("ln2w_bc", [P, D])
    ln2b_bc = sb("ln2b_bc", [P, D])
    carry = sb("carry", [P, H, P])
    B_const = Buf("consts")
    B_carry = [Buf("carry%d" % h) for h in range(H)]

    conv_A = Buf("conv_A")
    conv_B = Buf("conv_B")
    ch_A = new_chan("conv_A")
    ch_B = new_chan("conv_B")

    def w3(w):
        return w.rearrange("(kc p) n -> p kc n", p=P)

    pieces = {}

    deferred = []
    deferred_A = []

    def conv(name, dst, srcs, grpbuf, ch, kcn=8):
        for src, co in srcs:
            ncols = src.shape[2]
            args = ("pool", dst[:, 0:kcn, co:co + ncols], src, [], [grpbuf], ch, True)
            if grpbuf is conv_A:
                deferred_A.append(args)
            else:
                deferred.append(args)
        pieces[name] = (dst, grpbuf, kcn)

    def issue_deferred(n):
        for _ in range(min(n, len(deferred))):
            dma(*deferred.pop(0))

    NAMES_IN = ["q0", "q1", "f0", "f1", "i0", "i1", "g0", "g1", "u0", "u1", "v0", "v1", "ga0", "ga1", "gb0", "gb1"]
    first = ["f0", "f1", "i0", "i1"]
    order = first + [n for n in NAMES_IN if n not in first]
    for n in order:
        j = NAMES_IN.index(n)
        grp = (conv_A, ch_A) if n in first else (conv_B, ch_B)
        conv(n, wi_bf[j], [(w3(w_in)[:, :, j * 512:(j + 1) * 512], 0)], *grp)
    for j in range(2):
        conv("pa%d" % j, wpa_bf[j], [(w3(w_pa)[:, :, j * 512:(j + 1) * 512], 0)], conv_B, ch_B)
        conv("pb%d" % j, wpb_bf[j], [(w3(w_pb)[:, :, j * 512:(j + 1) * 512], 0)], conv_B, ch_B)
    for j in range(2):
        conv("o%d" % j, wo_bf[j], [(w3(w_o)[:, :, j * 512:(j + 1) * 512], 0)], conv_B, ch_B)
    for m in range(11):
        conv("fi%d" % m, wfi_bf[m],
             [(w3(w_fi)[:, :, 256 * m:256 * (m + 1)], 0),
              (w3(w_fi)[:, :, DFF + 256 * m:DFF + 256 * (m + 1)], 256)], conv_B, ch_B)
    for kg in range(3):
        k0, k1 = kg * 8, min(NFF, kg * 8 + 8)
        for half in range(2):
            conv("fo%d_%d" % (kg, half), wfo_bf[kg * 2 + half],
                 [(w3(w_fo)[:, k0:k1, half * 512:(half + 1) * 512], 0)], conv_B, ch_B, kcn=k1 - k0)

    ch_c = new_chan("consts")
    ch_r = new_chan("consts_early")
    B_early = Buf("consts_early")
    dma("sp", ident_f[:], identf_in, [], [B_early], ch_r, skip_self=True)
    for dst, src in ((ident_b, identb_in), (mask2, mask2_in), (m64, m64_in), (m32, m32_in),
                     (flag, flag_in)):
        dma("sp", dst[:], src, [], [B_const], ch_c, skip_self=True)
    for dst, src in ((ln1w_bc, ln1w_in), (ln1b_bc, ln1b_in), (ln2w_bc, ln2w_in), (ln2b_bc, ln2b_in)):
        dma("sp", dst[:], src[0:1, :].broadcast_to([P, D]), [], [B_const], ch_c, skip_self=True)

    with ExitStack() as ses:
        def ssb(name, shape, dtype=F32):
            return ses.enter_context(nc.sbuf_tensor("ss_" + name, list(shape), dtype))

        rows_sb = ssb("rows_sb", [NROWS, P])
        cond2 = ssb("cond2", [P, 8, 2])
        condB = ssb("condB", [P, 8, P])
        NWA = 6
        wa = [ssb("wa%d" % i, [P, 8, 512]) for i in range(NWA)]
        bada_bc = ssb("bada_bc", [P, 2, D])
        lnb_bc = ssb("lnb_bc", [P, D])
        ws_sb = ssb("ws_sb", [P, H, P])
        wmT_f = ssb("wmT_f", [P, H, P])
        bs_sb = ssb("bs_sb", [1, H * P])
        dl = ssb("dl", [P, 8])
        B_rows = Buf("rows")
        B_set = Buf("setup_small")
        B_wa = [Buf("wa%d" % i) for i in range(NWA)]
        ch_wa = [new_chan("wa%d" % i) for i in range(NWA)]
        B_condB = Buf("condB")
        B_gm = Buf("gm")

        dma("sp", rows_sb[:], rows_in, [], [B_early], ch_r, skip_self=True)
        dma("sp", bada_bc[:, 0, :], b_ada[0:1, 2 * D:3 * D].broadcast_to([P, D]), [], [B_const], ch_c, skip_self=True)
        dma("sp", bada_bc[:, 1, :], b_ada[0:1, 5 * D:6 * D].broadcast_to([P, D]), [], [B_const], ch_c, skip_self=True)
        dma("sp", lnb_bc[:], lnb_in[0:1, :].broadcast_to([P, D]), [], [B_const], ch_c, skip_self=True)
        dma("sp", ws_sb[:], ws_in.rearrange("g t s -> t g s"), [], [B_const], ch_c, skip_self=True)
        dma("sp", bs_sb[:], bs_in, [], [B_const], ch_c, skip_self=True)

        B_ones = Buf("ones")
        S.add("dve", lambda e: e.memset(ones_f[:], 1.0), [], [B_ones])
        S.add("dve", lambda e: e.memset(ones_b[:], 1.0), [], [B_ones])
        S.add("dve", lambda e: e.memset(carry[:], 0.0), [], B_carry)
        for i in range(NBANK):
            S.add("dve", (lambda i: lambda e: e.memset(banks[i][:], 0.0))(i), [], [bank_bufs[i]])

        pb, pbuf = psum()
        tr(pb[:, 0:NROWS], rows_sb[:], ident_f[0:NROWS, 0:NROWS], [B_early], [pbuf])
        copy("dve", colv[:], pb[:, 0:NROWS], [pbuf], [B_set])
        act(cond2[:, :, 0], colv[:, 0:8], AF.Silu, [B_set], [B_condB])
        act(cond2[:, :, 1], colv[:, 0:8], AF.Silu, [B_set], [B_condB])
        for kc in range(8):
            ts("dve", condB[:, kc, :], ones_f[:], cond2[:, kc, 0:1], None, ALU.mult, None, [B_condB, B_ones],
               [B_condB])
        tt("dve", dl[:], colv[:, 32:40], colv[:, 40:48], ALU.subtract, [B_set], [B_set])
        act(lbv[:, 0:8], dl[:], AF.Sigmoid, [B_set], [B_set])
        ts("dve", lbv[:, 8:16], lbv[:, 0:8], -1.0, 1.0, ALU.mult, ALU.add, [B_set], [B_set])
        ts("dve", lbv[:, 16:24], lbv[:, 0:8], -1.0, None, ALU.add, None, [B_set], [B_set])

        modp, modpbuf = psum()
        modp_v = modp[:, 0:64].rearrange("p (a b) -> p a b", b=2)
        wa3 = w3(w_ada)
        col_of = {0: 0, 1: 8, 3: 16, 4: 24}
        for ct in range(12):
            gi = ct // 2
            sl = ct % NWA
            dma("sp", wa[sl][:], wa3[:, :, ct * 512:(ct + 1) * 512], [], [B_wa[sl]], ch_wa[sl])
            if gi in col_of:
                for jb in range(4):
                    col = col_of[gi] + (ct % 2) * 4 + jb
                    for kc in range(8):
                        mm(modp_v[:, col, :], wa[sl][:, kc, jb * P:(jb + 1) * P], cond2[:, kc, :], kc == 0, kc == 7,
                           [B_wa[sl], B_condB], [modpbuf])
            else:
                gb_, gbuf = psum()
                for kc in range(8):
                    mm(gb_[:], condB[:, kc, :], wa[sl][:, kc, :], kc == 0, kc == 7, [B_wa[sl], B_condB], [gbuf])
                gdst = g1_bc if gi == 2 else g2_bc
                half = ct % 2
                tt("dve", gdst[:, half * 512:(half + 1) * 512], gb_[:],
                   bada_bc[:, 0 if gi == 2 else 1, half * 512:(half + 1) * 512], ALU.add, [gbuf, B_const], [B_const])
        for i_, args in enumerate(deferred_A):
            q_, o_, i__, r_, w_, c_, sk_ = args
            dma(q_, o_, i__, (B_wa if i_ == 0 else []), w_, c_, sk_)
        tt("dve", modv[:], modp_v[:, 0:32, 0], colv[:, 49:81], ALU.add, [modpbuf, B_set], [B_set])
        ts("dve", scp[:, 0:8], modv[:, 8:16], 1.0, None, ALU.add, None, [B_set], [B_set])
        ts("dve", scp[:, 8:16], modv[:, 24:32], 1.0, None, ALU.add, None, [B_set], [B_set])

        for g in range(H):
            pb, pbuf = psum()
            tr(pb[:, 0:P], ws_sb[:, g, :], ident_f[:], [B_const], [pbuf])
            copy("dve", wmT_f[:, g, :], pb[:, 0:P], [pbuf], [B_gm])
        S.add("dve", lambda e: e.memset(wmT_f[64:128, :, 0:64], 0.0), [], [B_gm])
        copy("dve", wmT[:], wmT_f[:], [B_gm], [B_const])
        for g in range(H):
            pb, pbuf = psum()
            mm(pb[:, 0:P], lnb_bc[:, g * P:(g + 1) * P], wmT_f[:, g, :], True, False, [B_const, B_gm], [pbuf])
            mm(pb[:, 0:P], ones_f[0:1, :], bs_sb[0:1, g * P:(g + 1) * P], False, True, [B_const, B_ones], [pbuf])
            copy("dve", r_sb[:, g, :], pb[:, 0:P], [pbuf], [B_const])
        S.barrier(exclude=(ch_A.key, ch_B.key))

    sh1 = modv[:, 0:8]
    sc1p = scp[:, 0:8]
    sh2 = modv[:, 16:24]
    sc2p = scp[:, 8:16]
    lb = lbv[:, 0:8]
    oml = lbv[:, 8:16]
    noml = lbv[:, 16:24]
    normw = colv[:, 48:49]
    lnw_col = colv[:, 8:16]
    bga = colv[:, 16:24]
    bgb = colv[:, 24:32]

    h_sb = sb("h_sb", [P, NB, D])
    arena1 = sb("arena1", [P, 6 * 8 * T], BF16)

    def a1(i, a):
        return arena1[:, i * 8 * T:(i + 1) * 8 * T].rearrange("p (a b) -> p a b", a=a)

    uT = a1(0, 8)
    v_sb = a1(1, NB)
    vn_sb = a1(2, NB)
    uG = a1(3, H)
    hgT = a1(4, H)
    ubT = a1(5, H)
    u2T = a1(0, 8)
    actT = arena1[:, 8 * T:(8 + NFF) * T].rearrange("p (a b) -> p a b", a=NFF)
    xpre = arena1[:, 32 * T:48 * T].bitcast(F32).rearrange("p (a b) -> p a b", a=NB)
    B_xpre = [Buf("xpre%d" % b) for b in range(NB)]
    tokb2 = [arena1[:, 16 * T + i * D:16 * T + (i + 1) * D] for i in range(NB)]
    B_tokb2 = [Buf("tokb2_%d" % i) for i in range(NB)]
    sgall = arena1[:, 16 * T:32 * T].bitcast(F32).rearrange("p (a b) -> p a b", a=H)
    B_sgall = [Buf("sgall%d" % h) for h in range(H)]
    wslots = [sb("wslot%d" % i, [P, 8, 512], BF16) for i in range(NSLOT)]
    tbig = sb("tbig", [P, 4, T])
    tq, tsg, tgt, tlf = tbig[:, 0, :], tbig[:, 1, :], tbig[:, 2, :], tbig[:, 3, :]
    mixT = tbig[:, :, :].bitcast(BF16).rearrange("p a (c b) -> p (a c) b", b=T)
    tkk = sb("tkk", [P, T])
    tcum = sb("tcum", [P, T])
    tc32 = sb("tc32", [P, T])
    te1 = sb("te1", [P, T])
    te3 = sb("te3", [P, T])
    ten = sb("ten", [P, T])
    trc = sb("trc", [P, T])
    td1 = sb("td1", [P, NCH, 32])
    qt2 = [sb("qt%d" % i, [P, T], BF16) for i in range(2)]
    qs2 = [sb("qs%d" % i, [P, T], BF16) for i in range(2)]
    kd2 = [sb("kd%d" % i, [P, T], BF16) for i in range(2)]
    K02 = [sb("K0_%d" % i, [P, NCH, 32], BF16) for i in range(2)]
    K12 = [sb("K1_%d" % i, [P, NCH, 64], BF16) for i in range(2)]
    dec2 = [sb("dec%d" % i, [P, NCH]) for i in range(2)]
    kdT_sb = sb("kdT_sb", [P, NB, P], BF16)
    A_sb = sb("A_sb", [P, NB, P], BF16)
    osq = sb("osq", [P, T], BF16)
    Sst = sb("Sst", [P, NCH + 1, P])
    Sbf = sb("Sbf", [P, NCH, P], BF16)
    tokb = [arena1[:, 32 * T + i * D:32 * T + (i + 1) * D] for i in range(NB)]
    tok = [arena1[:, 40 * T + i * 2 * D:40 * T + (i + 1) * 2 * D].bitcast(F32) for i in range(2)]
    st6 = [sb("st6_%d" % i, [P, 12]) for i in range(2 * NB)]
    mv = [sb("mv%d" % i, [P, 4]) for i in range(2 * NB)]
    ftmp = [sb("ftmp%d" % i, [P, T]) for i in range(2)]
    ftmp2 = [sb("ftmp2_%d" % i, [P, T]) for i in range(2)]
    neghalf = sb("neghalf", [P, 1])
    S.add("pool", lambda e: e.memset(neghalf[:], -0.5), [], [B_const])

    B_h = [Buf("h%d" % b) for b in range(NB)]
    ch_h = [new_chan("h%d" % b) for b in range(NB)]
    B_uT = [Buf("uT%d" % b) for b in range(NB)]
    B_v = [[Buf("v%d_%d" % (b, j)) for j in range(2)] for b in range(NB)]
    B_vn = [Buf("vn%d" % b) for b in range(NB)]
    B_uG = [Buf("uG%d" % g) for g in range(H)]
    B_hgT = [Buf("hgT%d" % h) for h in range(H)]
    B_ubT = [Buf("ubT%d" % g) for g in range(H)]
    B_mixT = [Buf("mixT%d" % o) for o in range(8)]
    B_u2T = [Buf("u2T%d" % b) for b in range(NB)]
    B_actT = [Buf("actT%d" % j) for j in range(NFF)]
    ch_xpre = [new_chan("xpre%d" % b) for b in range(NB)]
    B_slot = [Buf("slot%d" % i) for i in range(NSLOT)]
    ch_slot = [new_chan("slot%d" % i) for i in range(NSLOT)]
    B_t = {n: Buf("t_" + n) for n in ("q", "sg", "gt", "lf", "kk", "cum", "c32", "e1", "e3", "en", "rc", "d1", "qt",
                                     "qs", "kd", "K0", "K1a", "K1b", "kdT", "A", "osq", "Sst", "Sbf")}
    B_p = [{n: Buf("p%d_%s" % (i, n)) for n in ("qt", "qs", "kd", "K0", "K1a", "K1b", "dec")} for i in range(2)]
    B_tok = [Buf("tok0"), Buf("tok1")]
    B_tokb = [Buf("tokb%d" % i) for i in range(NB)]
    tokb_main, B_tokb_main = tokb, B_tokb
    B_mv = [Buf("mv%d" % i) for i in range(2 * NB)]
    B_ftmp = [Buf("ftmp0"), Buf("ftmp1")]
    B_ftmp2 = [Buf("ftmp2_0"), Buf("ftmp2_1")]
    tokc = [0]
    ftc = [0]

    wmap = {}
    wnext = [0]

    def wload(name):
        if name in wmap:
            return
        i = wnext[0]
        wnext[0] = (i + 1) % NSLOT
        for k in [k for k, vv in wmap.items() if vv == i]:
            del wmap[k]
        wmap[name] = i
        src, gbuf, kcn = pieces[name]
        dma("sp", wslots[i][:, 0:kcn, :], src[:, 0:kcn, :], [gbuf], [B_slot[i]], ch_slot[i])

    wseq = [[]]

    def wget(name):
        wload(name)
        seq = wseq[0]
        if name in seq:
            i = seq.index(name)
            if i + 1 < len(seq):
                wload(seq[i + 1])
        i = wmap[name]
        return wslots[i], B_slot[i]

    def g_layernorm_stats_multi(items):
        for src_ap, src_bufs, k in items:
            S.add("dve", (lambda src_ap, k: lambda e: e.bn_stats(out=st6[k][:, 0:6], in_=src_ap[:, 0:512]))(src_ap, k),
                  src_bufs, [B_mv[k]])
            S.add("dve", (lambda src_ap, k: lambda e: e.bn_stats(out=st6[k][:, 6:12], in_=src_ap[:, 512:1024]))(src_ap, k),
                  src_bufs, [B_mv[k]])
        yield
        for src_ap, src_bufs, k in items:
            S.add("dve", (lambda k: lambda e: e.bn_aggr(out=mv[k][:, 0:2], in_=st6[k][:, 0:12]))(k), [B_mv[k]],
                  [B_mv[k]])
            ts("dve", mv[k][:, 2:3], mv[k][:, 1:2], LN_EPS, None, ALU.add, None, [B_mv[k]], [B_mv[k]])
        yield
        for src_ap, src_bufs, k in items:
            if RSTD_POW:
                tt("pool", mv[k][:, 2:3], mv[k][:, 2:3], neghalf[:], ALU.pow, [B_mv[k], B_const], [B_mv[k]])
            else:
                act(mv[k][:, 2:3], mv[k][:, 2:3], AF.Ln, [B_mv[k]], [B_mv[k]])
                act(mv[k][:, 2:3], mv[k][:, 2:3], AF.Exp, [B_mv[k]], [B_mv[k]], scale=-0.5)
        yield
        for src_ap, src_bufs, k in items:
            stt(mv[k][:, 3:4], mv[k][:, 0:1], -1.0, mv[k][:, 2:3], ALU.mult, ALU.mult, [B_mv[k]], [B_mv[k]])
        yield

    def layernorm_stats_multi(items):
        for _ in g_layernorm_stats_multi(items):
            pass

    def layernorm_stats(src_ap, src_bufs, k):
        layernorm_stats_multi([(src_ap, src_bufs, k)])

    def to_featmajor(src_bf, src_bufs, dstT, dst_buf, blk, scale_cols, bias_cols):
        pb, pbuf = psum()
        pv = pb.bitcast(BF16)[:, :].rearrange("p (a b) -> p a b", a=8)
        for kc in range(8):
            tr(pv[:, kc, :], src_bf[:, kc * P:(kc + 1) * P], ident_b[:], src_bufs + [B_const], [pbuf])
        for kc in range(8):
            act(dstT[:, kc, blk * P:(blk + 1) * P], pv[:, kc, :], AF.Identity, [pbuf, B_const], [dst_buf],
                scale=scale_cols[:, kc:kc + 1], bias=bias_cols[:, kc:kc + 1])

    def load_x(src, ti):
        for b in range(NB):
            r0 = (ti * NB + b) * P
            dma("act", h_sb[:, b, :], src[r0:r0 + P, :], [], [B_h[b]], ch_h[b])

    def g_adaln_all(dstT, B_dst, scale_cols, bias_cols, src=None, tb=None, koff=0):
        if src is None:
            src = [(h_sb[:, blk, :], [B_h[blk]]) for blk in range(NB)]
        tokb, B_tokb = tb if tb is not None else (tokb_main, B_tokb_main)
        yield from g_layernorm_stats_multi([(src[blk][0], src[blk][1], blk + koff) for blk in range(NB)])
        for blk in range(NB):
            act(tokb[blk][:], src[blk][0], AF.Identity, src[blk][1] + [B_mv[blk + koff]], [B_tokb[blk]],
                scale=mv[blk + koff][:, 2:3], bias=mv[blk + koff][:, 3:4])
            yield
        for j in range(4):
            pb, pbuf = psum()
            pv = pb.bitcast(BF16)[:, :].rearrange("p (k a b) -> p k a b", k=2, a=NB)
            for k2 in range(2):
                kc = 2 * j + k2
                for blk in range(NB):
                    tr(pv[:, k2, blk, :], tokb[blk][:, kc * P:(kc + 1) * P], ident_b[:], [B_tokb[blk], B_const],
                       [pbuf])
                yield
            for k2 in range(2):
                kc = 2 * j + k2
                act(dstT[:, kc, :], pv[:, k2, :, :].rearrange("p a b -> p (a b)"), AF.Identity, [pbuf, B_const],
                    B_dst, scale=scale_cols[:, kc:kc + 1], bias=bias_cols[:, kc:kc + 1])
                yield

    def adaln_all(*a_, **k_):
        for _ in g_adaln_all(*a_, **k_):
            pass

    def proj_feat(wt, wbuf, off, xT, xbufs, pool="all"):
        pb, pbuf = psum(pool)
        for kc in range(8):
            mm(pb[:], wt[:, kc, off:off + P], xT[:, kc, :], kc == 0, kc == 7, [wbuf] + xbufs, [pbuf])
        return pb, pbuf

    def proj_tok(wt, wbuf, xT, xbuf, blk, pool="all"):
        pb, pbuf = psum(pool)
        for kc in range(8):
            mm(pb[:], xT[:, kc, blk * P:(blk + 1) * P], wt[:, kc, :], kc == 0, kc == 7, [wbuf, xbuf], [pbuf])
        return pb, pbuf

    def g_make_v(js):
        for j in js:
            wt, wbuf = wget("i%d" % j)
            for blk in range(NB):
                pb, pbuf = proj_tok(wt, wbuf, uT, B_uT[blk], blk, pool="m")
                copy("act" if blk % 2 == 0 else "dve", v_sb[:, blk, j * 512:(j + 1) * 512], pb[:], [pbuf],
                     [B_v[blk][j]])
                yield

    c3 = lambda t_: t_[:, :].rearrange("p (c s) -> p c s", s=64)

    def hgrn_head_z(h, full):
        j, off = h // 4, (h % 4) * P
        res = {}
        names = ("f", "q", "g") if full else ("f",)
        for nm in names:
            wt, wbuf = wget("%s%d" % (nm, j))
            res[nm] = proj_feat(wt, wbuf, off, uT, B_uT, pool="z")
            yield res

    def hgrn_head_ew(h, z, full, pre_sg=False):
        B = dict(B_t)
        p = h % 2
        Bp = B_p[p]
        qt, qs, kd, K0, K1, dec = qt2[p], qs2[p], kd2[p], K02[p], K12[p], dec2[p]
        tgt_ = ftmp2[p]
        tsg = tbig[:, 1, :]
        if pre_sg:
            tsg = sgall[:, h, :]
            B["sg"] = B_sgall[h]
        else:
            zf, zfb = z["f"]
            act(tsg[:], zf[:], AF.Sigmoid, [zfb], [B["sg"]])
            yield
        if full:
            zq, zqb = z["q"]
            zg, zgb = z["g"]
            act(tq[:], zq[:], AF.Silu, [zqb], [B["q"]])
            yield
            act(tgt_[:], zg[:], AF.Silu, [zgb], [B_ftmp2[p]])
            yield
        act(tlf[:], tsg[:], AF.Ln, [B["sg"], B_const], [B["lf"]], scale=oml[:, h:h + 1], bias=lb[:, h:h + 1])
        yield
        S.add("dve", lambda e: e.tensor_tensor_scan(out=tcum[:], data0=m64[:], data1=tlf[:], initial=0.0,
                                                    op0=ALU.mult, op1=ALU.add), [B["lf"], B_const], [B["cum"]])
        yield
        if full:
            S.add("dve", lambda e: e.tensor_tensor_scan(out=tc32[:], data0=m32[:], data1=tlf[:], initial=0.0,
                                                        op0=ALU.mult, op1=ALU.add), [B["lf"], B_const], [B["c32"]])
            yield
        ts("dve", tkk[:], tsg[:], noml[:, h:h + 1], oml[:, h:h + 1], ALU.mult, ALU.add, [B["sg"], B_const], [B["kk"]])
        yield
        act(te1[:], tcum[:], AF.Exp, [B["cum"]], [B["e1"]])
        yield
        if full:
            act(te3[:], tc32[:], AF.Exp, [B["c32"]], [B["e3"]])
            yield
            act(ten[:], tc32[:], AF.Exp, [B["c32"]], [B["en"]], scale=-1.0)
            yield
        cum3 = c3(tcum)
        tt("dve", c3(trc), cum3[:, :, 63:64].broadcast_to([P, NCH, 64]), cum3, ALU.subtract, [B["cum"]], [B["rc"]])
        yield
        if full:
            tt("dve", td1[:], cum3[:, :, 31:32].broadcast_to([P, NCH, 32]), cum3[:, :, 0:32], ALU.subtract,
               [B["cum"]], [B["d1"]])
            yield
        act(trc[:], trc[:], AF.Exp, [B["rc"]], [B["rc"]])
        yield
        if full:
            act(td1[:], td1[:], AF.Exp, [B["d1"]], [B["d1"]])
            yield
        copy("pool", dec[:], c3(te1)[:, :, 63], [B["e1"]], [Bp["dec"]])
        yield
        if full:
            kk3, en3 = c3(tkk), c3(ten)
            tt("dve", qt[:], tq[:], te1[:], ALU.mult, [B["q"], B["e1"]], [Bp["qt"]])
            yield
            tt("dve", qs[:], tq[:], te3[:], ALU.mult, [B["q"], B["e3"]], [Bp["qs"]])
            yield
            tt("dve", K0[:], kk3[:, :, 0:32], en3[:, :, 0:32], ALU.mult, [B["kk"], B["en"]], [Bp["K0"]])
            yield
            tt("dve", K1[:, :, 32:64], kk3[:, :, 32:64], en3[:, :, 32:64], ALU.mult, [B["kk"], B["en"]], [Bp["K1b"]])
            yield
        tt("dve", kd[:], tkk[:], trc[:], ALU.mult, [B["kk"], B["rc"]], [Bp["kd"]])
        yield
        if full:
            tt("dve", K1[:, :, 0:32], kk3[:, :, 0:32], td1[:], ALU.mult, [B["kk"], B["d1"]], [Bp["K1a"]])
            yield

    def hgrn_head_small(h, full):
        B = B_t
        p = h % 2
        Bp = B_p[p]
        qt, qs, kd, K0, K1, dec = qt2[p], qs2[p], kd2[p], K02[p], K12[p], dec2[p]
        tgt_, tX = ftmp2[p], ftmp[p]
        vb_all = [B_v[b][h // 4] for b in range(NB)]
        vcol = slice(h * P, (h + 1) * P)
        if full:
            pa, pab = psum("m")
            pav = pa[:, :].rearrange("p (a b) -> p a b", a=NB)
            S.add("dve", lambda e: e.memset(pa[:], 0.0), [], [pab])
            yield
            for c in range(NCH):
                blk, po = c // 2, (c % 2) * 64
                mm(pav[po:po + 32, blk, po:po + 32], K0[:, c, :], qs[:, c * 64:c * 64 + 32], True, True,
                   [Bp["K0"], Bp["qs"]], [pab])
                mm(pav[po:po + 64, blk, po + 32:po + 64], K1[:, c, :], qs[:, c * 64 + 32:c * 64 + 64], True, True,
                   [Bp["K1a"], Bp["K1b"], Bp["qs"]], [pab])
            yield
            mask_b = bass.AP(mask2[:].tensor, mask2[:].offset, [list(mask2[:].ap[0]), [0, NB], [1, P]])
            tt("dve", A_sb[:], pav, mask_b, ALU.mult, [pab, B_const], [B["A"]])
            yield
        pk, pkb = psum("m")
        pkv = pk.bitcast(BF16)[:, 0:NB * P].rearrange("p (a b) -> p a b", a=NB)
        for blk in range(NB):
            tr(pkv[:, blk, :], kd[:, blk * P:(blk + 1) * P], ident_b[:], [Bp["kd"], B_const], [pkb])
        yield
        copy("act", kdT_sb[:], pkv, [pkb], [B["kdT"]])
        yield
        Ue, Ueb = psum("m")
        Uo, Uob = psum("m")
        Uv = (Ue[:, :].rearrange("p (a b) -> p a b", a=NB), Uo[:, :].rearrange("p (a b) -> p a b", a=NB))
        Ub = (Ueb, Uob)
        for c in range(NCH):
            blk, po = c // 2, (c % 2) * 64
            mm(Uv[c % 2][:, blk, :], kdT_sb[po:po + 64, blk, :], v_sb[po:po + 64, blk, vcol], True, True,
               [B["kdT"], vb_all[blk]], [Ub[c % 2]])
        yield
        copy("pool", Sst[:, 0, :], carry[:, h, :], [B_carry[h]], [B["Sst"]])
        yield
        for c in range(NCH):
            stt(Sst[:, c + 1, :], Sst[:, c, :], dec[:, c:c + 1], Uv[c % 2][:, c // 2, :], ALU.mult,
                ALU.add, [B["Sst"], Bp["dec"], Ub[c % 2]], [B["Sst"]])
            yield
        copy("pool", carry[:, h, :], Sst[:, NCH, :], [B["Sst"]], [B_carry[h]])
        yield
        if not full:
            return
        copy("act", Sbf[:], Sst[:, 0:NCH, :], [B["Sst"]], [B["Sbf"]])
        yield
        po_, pob = psum("m")
        for blk in range(NB):
            mm(po_[:, blk * P:(blk + 1) * P], v_sb[:, blk, vcol], A_sb[:, blk, :], True, False,
               [vb_all[blk], B["A"]], [pob])
            for c in (2 * blk, 2 * blk + 1):
                mm(po_[:, c * 64:(c + 1) * 64], Sbf[:, c, :], qt[:, c * 64:(c + 1) * 64], False, c % 2 == 1,
                   [B["Sbf"], Bp["qt"]], [pob])
        yield
        act(osq[:], po_[:], AF.Square, [pob], [B["osq"]])
        yield
        pss, pssb = psum("m")
        mm(pss[:], ones_b[:], osq[:], True, True, [B["osq"], B_const], [pssb])
        yield
        act(tX[:], pss[:], AF.Ln, [pssb], [B_ftmp[p]], scale=1.0 / P, bias=RMS_EPS)
        yield
        act(tX[:], tX[:], AF.Exp, [B_ftmp[p]], [B_ftmp[p]], scale=-0.5)
        yield
        tt("dve", tX[:], po_[:], tX[:], ALU.mult, [pob, B_ftmp[p]], [B_ftmp[p]])
        yield
        stt(hgT[:, h, :], tX[:], normw, tgt_[:], ALU.mult, ALU.mult, [B_ftmp[p], B_ftmp2[p], B_const],
            [B_hgT[h]] + B_tokb + B_xpre)
        yield

    def delayed(gen, n):
        for _ in range(n):
            yield
        yield from gen

    def run_merged(gens, weights=None):
        active = [(g, (weights[i] if weights else 1)) for i, g in enumerate(gens) if g is not None]
        while active:
            for g, w in list(active):
                try:
                    for _ in range(w):
                        next(g)
                except StopIteration:
                    active.remove((g, w))

    def hgrn_all(full, extra=None):
        zres = {}

        def zgen(h):
            last = None
            for last in hgrn_head_z(h, full):
                yield
            zres[h] = last

        if not full:
            for h in range(H):
                run_merged([zgen(h)])
                zf, zfb = zres[h]["f"]
                act(sgall[:, h, :], zf[:], AF.Sigmoid, [zfb], [B_sgall[h]])
            for i in range(-1, H):
                gens, wts = [], []
                if i + 1 < H:
                    gens.append(hgrn_head_ew(i + 1, None, False, pre_sg=True))
                    wts.append(MERGE_W[0])
                if i >= 0:
                    gens.append(hgrn_head_small(i, False))
                    wts.append(MERGE_W[1])
                if i == -1 and extra is not None:
                    gens.append(extra)
                    wts.append(1)
                run_merged(gens, wts)
            return
        run_merged([zgen(0)])
        for i in range(-1, H):
            gens, wts = [], []
            if i + 1 < H:
                gens.append(hgrn_head_ew(i + 1, zres[i + 1], full))
                wts.append(MERGE_W[0])
            if i >= 0:
                gens.append(hgrn_head_small(i, full))
                wts.append(MERGE_W[1])
            if i + 2 <= H - 1:
                gens.append(delayed(zgen(i + 2), 4 if full else 2))
                wts.append(1)
            if i == -1 and extra is not None:
                gens.append(extra)
                wts.append(1)
            run_merged(gens, wts)


    def gmlp():
        for g in range(H):
            wt, wbuf = wget("u%d" % (g // 4))
            pb, pbuf = proj_feat(wt, wbuf, (g % 4) * P, uT, B_uT)
            act(uG[:, g, :], pb[:], AF.Gelu, [pbuf], [B_uG[g]])
        for blk in range(NB):
            k = tokc[0] % 2
            tokc[0] += 1
            for j in range(2):
                wt, wbuf = wget("v%d" % j)
                pb, pbuf = proj_tok(wt, wbuf, uT, B_uT[blk], blk)
                act(tok[k][:, j * 512:(j + 1) * 512], pb[:], AF.Gelu, [pbuf], [B_tok[k]] + B_xpre)
            layernorm_stats(tok[k], [B_tok[k]], k)
            ts("dve", vn_sb[:, blk, :], tok[k][:], mv[k][:, 0:1], mv[k][:, 2:3], ALU.subtract, ALU.mult,
               [B_tok[k], B_mv[k]], [B_vn[blk]] + B_tokb2)
        for g in range(H):
            pb, pbuf = psum()
            for blk in range(NB):
                mm(pb[:, blk * P:(blk + 1) * P], vn_sb[:, blk, g * P:(g + 1) * P], wmT[:, g, :], True, True,
                   [B_vn[blk], B_const], [pbuf])
            k = ftc[0] % 2
            ftc[0] += 1
            r_b = bass.AP(r_sb[:].tensor, r_sb[:, g, :].offset, [list(r_sb[:].ap[0]), [0, NB], [1, P]])
            stt(ftmp[k][:, :].rearrange("p (a b) -> p a b", a=NB), pb[:, :].rearrange("p (a b) -> p a b", a=NB),
                lnw_col[:, g:g + 1], r_b, ALU.mult, ALU.add, [pbuf, B_const], [B_ftmp[k]])
            tt("pool", ubT[:, g, :], ftmp[k][:], uG[:, g, :], ALU.mult, [B_ftmp[k], B_uG[g]],
               [B_ubT[g], B_tok[0], B_tok[1]])

    def mixer_out():
        for oc in range(8):
            j, off = oc // 4, (oc % 4) * P
            wga, wgab = wget("ga%d" % j)
            pga, pgab = proj_feat(wga, wgab, off, uT, B_uT)
            wgb, wgbb = wget("gb%d" % j)
            pgb, pgbb = proj_feat(wgb, wgbb, off, uT, B_uT)
            wa_, wab = wget("pa%d" % j)
            pya, pyab = proj_feat(wa_, wab, off, hgT, B_hgT)
            wb_, wbb = wget("pb%d" % j)
            pyb, pybb = proj_feat(wb_, wbb, off, ubT, B_ubT)
            k = ftc[0] % 2
            ftc[0] += 1
            act(ftmp[k][:], pga[:], AF.Sigmoid, [pgab, B_const], [B_ftmp[k]], bias=bga[:, oc:oc + 1])
            act(ftmp2[k][:], pgb[:], AF.Sigmoid, [pgbb, B_const], [B_ftmp2[k]], bias=bgb[:, oc:oc + 1])
            tt("dve", ftmp[k][:], ftmp[k][:], pya[:], ALU.mult, [B_ftmp[k], pyab], [B_ftmp[k]])
            tt("dve", ftmp2[k][:], ftmp2[k][:], pyb[:], ALU.mult, [B_ftmp2[k], pybb], [B_ftmp2[k]])
            tt("pool", mixT[:, oc, :], ftmp[k][:], ftmp2[k][:], ALU.add, [B_ftmp[k], B_ftmp2[k]],
               [B_mixT[oc], B_t["q"], B_t["sg"], B_t["gt"], B_t["lf"]])

    def g_post_ln_all(w_bc, b_bc):
        yield from g_layernorm_stats_multi([(h_sb[:, blk, :], [B_h[blk]], blk) for blk in range(NB)])
        for blk in range(NB):
            act(h_sb[:, blk, :], h_sb[:, blk, :], AF.Identity, [B_h[blk], B_mv[blk]], [B_h[blk]],
                scale=mv[blk][:, 2:3], bias=mv[blk][:, 3:4])
            yield
        for blk in range(NB):
            tt("dve" if blk % 2 == 0 else "pool", h_sb[:, blk, :], h_sb[:, blk, :], w_bc[:], ALU.mult,
               [B_h[blk], B_const], [B_h[blk]])
            yield
        for blk in range(NB):
            tt("pool" if blk == 2 else "dve", h_sb[:, blk, :], h_sb[:, blk, :], b_bc[:], ALU.add,
               [B_h[blk], B_const], [B_h[blk]])
            yield

    def post_ln_all(*a_):
        for _ in g_post_ln_all(*a_):
            pass

    def residual_update(pb, pbuf, blk, hs, g_bc):
        k = ftc[0] % 2
        ftc[0] += 1
        tt("dve", ftmp[k][:], pb[:], g_bc[:, hs], ALU.mult, [pbuf, B_const], [B_ftmp[k]])
        stt(h_sb[:, blk, hs], h_sb[:, blk, hs], ALPHA, ftmp[k][:], ALU.mult, ALU.add, [B_h[blk], B_ftmp[k]],
            [B_h[blk]])

    def mix_and_ln1():
        for blk in range(NB):
            for half in range(2):
                wt, wbuf = wget("o%d" % half)
                pb, pbuf = psum()
                for kc in range(8):
                    mm(pb[:], mixT[:, kc, blk * P:(blk + 1) * P], wt[:, kc, :], kc == 0, kc == 7,
                       [wbuf] + B_mixT, [pbuf])
                residual_update(pb, pbuf, blk, slice(half * 512, (half + 1) * 512), g1_bc)
        post_ln_all(ln1w_bc, ln1b_bc)
        adaln_all(u2T, B_u2T, sc2p, sh2)

    def prefetch_x(ti):
        for b in range(NB):
            r0 = (ti * NB + b) * P
            dma("act", xpre[:, b, :], x_main[r0:r0 + P, :], [], [B_xpre[b]] + B_tokb + B_tok, ch_xpre[b])

    def ffn(next_ti=None):
        if next_ti is not None:
            prefetch_x(next_ti)
        for m in range(11):
            wt, wbuf = wget("fi%d" % m)
            for jj in range(2):
                j = 2 * m + jj
                pa_, pab_ = proj_feat(wt, wbuf, jj * P, u2T, B_u2T)
                pbb, pbbb = proj_feat(wt, wbuf, 256 + jj * P, u2T, B_u2T)
                k = ftc[0] % 2
                ftc[0] += 1
                act(ftmp2[k][:], pa_[:], AF.Silu, [pab_], [B_ftmp2[k]])
                tt("dve", actT[:, j, :], ftmp2[k][:], pbb[:], ALU.mult, [B_ftmp2[k], pbbb], [B_actT[j]])
        for half in range(2):
            hs = slice(half * 512, (half + 1) * 512)
            pbs = [psum() for _ in range(NB)]
            for kg in range(3):
                wt, wbuf = wget("fo%d_%d" % (kg, half))
                k0, k1 = kg * 8, min(NFF, kg * 8 + 8)
                for blk in range(NB):
                    for kc in range(k0, k1):
                        mm(pbs[blk][0][:], actT[:, kc, blk * P:(blk + 1) * P], wt[:, kc - k0, :], kc == 0,
                           kc == NFF - 1, [wbuf, B_actT[kc]], [pbs[blk][1]])
            for blk in range(NB):
                residual_update(pbs[blk][0], pbs[blk][1], blk, hs, g2_bc)

    B_out = [Buf("out%d" % b) for b in range(NB)]

    def g_ln2_and_store(ti):
        yield from g_post_ln_all(ln2w_bc, ln2b_bc)
        for blk in range(NB):
            r0 = (ti * NB + blk) * P
            dma("sp", out_d[r0:r0 + P, :], h_sb[:, blk, :], [B_h[blk]], [B_out[blk]], ch_h[blk])
            yield

    SEQ_WARM = ["i0", "f0", "f1", "i1"]
    SEQ_MAIN = (["i0", "f0", "q0", "g0", "i1", "f1", "q1", "g1", "u0", "u1", "v0", "v1",
                 "ga0", "gb0", "pa0", "pb0", "ga1", "gb1", "pa1", "pb1", "o0", "o1"]
                + ["fi%d" % m for m in range(11)]
                + ["fo%d_%d" % (kg, half) for half in range(2) for kg in range(3)])

    for ti in range(n_tiles):
        wseq[0] = SEQ_WARM + (SEQ_WARM[:1] if ti + 1 < n_tiles else SEQ_MAIN[:1])
        issue_deferred(len(deferred) if ti == n_tiles - 1 else -(-len(deferred) // (n_tiles - ti)))
        load_x(x_warm, ti)
        adaln_all(uT, B_uT, sc1p, sh1)
        run_merged([g_make_v([0])])
        hgrn_all(False, extra=g_make_v([1]))
    for h in range(H):
        ts("dve", carry[:, h, :], carry[:, h, :], flag[:, 0:1], None, ALU.mult, None, [B_carry[h], B_const],
           [B_carry[h]])
    dump("carry", carry[:], B_carry[H - 1])
    S.barrier(exclude=(ch_A.key, ch_B.key))

    for ti in range(n_tiles):
        wseq[0] = SEQ_MAIN + (SEQ_MAIN[:1] if ti + 1 < n_tiles else [])
        if ti == 0:
            load_x(x_main, ti)
            adaln_all(uT, B_uT, sc1p, sh1)
        if ti == 0:
            dump("uT", uT, B_uT[NB - 1])
        run_merged([g_make_v([0])])
        hgrn_all(True, extra=g_make_v([1]))
        if ti == 0:
            dump("hgT", hgT, B_hgT[H - 1])
        gmlp()
        if ti == 0:
            dump("ubT", ubT, B_ubT[H - 1])
        mixer_out()
        if ti == 0:
            dump("mixT", mixT, B_mixT[7])
        wload("o0")
        wload("o1")
        S.barrier()
        mix_and_ln1()
        if ti == 0:
            dump("u2T", u2T, B_u2T[NB - 1])
            dump("h1", h_sb[:], B_h[NB - 1])
        ffn(ti + 1 if ti + 1 < n_tiles else None)
        S.barrier()
        if ti + 1 < n_tiles:
            run_merged([g_ln2_and_store(ti),
                        g_adaln_all(uT, B_uT, sc1p, sh1, src=[(xpre[:, b, :], [B_xpre[b]]) for b in range(NB)],
                                    tb=(tokb2, B_tokb2), koff=NB)])
            for b in range(NB):
                dma("pool", h_sb[:, b, :], xpre[:, b, :], [B_xpre[b]], [B_h[b]], ch_h[b])
        else:
            run_merged([g_ln2_and_store(ti)])

    S.add("sp", None, B_out + dump_list, [])
    S.emit(nc, sems)
    es.close()
    return nc


_CONST_CACHE = {}


def _consts():
    if not _CONST_CACHE:
        s = np.arange(P)
        mask2 = ((s[:, None] // 64 == s[None, :] // 64) & (s[:, None] <= s[None, :])).astype(np.float32)
        t = np.arange(T)
        m64 = np.broadcast_to((t % 64 != 0).astype(np.float32), (P, T)).copy()
        m32 = np.broadcast_to((t % 32 != 0).astype(np.float32), (P, T)).copy()
        _CONST_CACHE.update(
            ident_f=np.eye(P, dtype=np.float32),
            ident_b=np.eye(P, dtype=np.float32).astype(ml_dtypes.bfloat16),
            mask2=mask2, m64=m64, m32=m32)
    return _CONST_CACHE


def make_in_maps(inp, n_tiles, seq):
    f = lambda a: np.ascontiguousarray(np.asarray(a, dtype=np.float32))
    x = f(inp["x"])
    Tc = n_tiles * T
    assert seq == 2 * Tc
    shared = dict(
        w_ada=f(inp["w_ada"][0]), b_ada=f(inp["b_ada"][0]).reshape(1, -1), w_in=f(inp["w_in"][0]),
        w_proj_a=f(inp["w_proj_a"][0]), w_proj_b=f(inp["w_proj_b"][0]), w_out=f(inp["w_out"][0]),
        w_ffn_in=f(inp["w_ffn_in"][0]), w_ffn_out=f(inp["w_ffn_out"][0]),
        gmlp_ln_b=f(inp["gmlp_ln_b"][0]).reshape(1, -1), gmlp_ws=f(inp["gmlp_ws"][0]),
        gmlp_bs=f(inp["gmlp_bs"][0]).reshape(1, -1),
        ln1_w=f(inp["ln1_w"][0]).reshape(1, -1), ln1_b=f(inp["ln1_b"][0]).reshape(1, -1),
        ln2_w=f(inp["ln2_w"][0]).reshape(1, -1), ln2_b=f(inp["ln2_b"][0]).reshape(1, -1),
        **_consts())
    b_ada = f(inp["b_ada"][0])
    maps = []
    for core in range(N_CORES):
        b, half = core // 2, core % 2
        rows = np.concatenate([
            f(inp["c"])[b].reshape(8, P),
            f(inp["gmlp_ln_w"][0]).reshape(8, P),
            f(inp["b_gate"][0, 0]).reshape(8, P),
            f(inp["b_gate"][0, 1]).reshape(8, P),
            f(inp["hgrn_lb_logits"][0]).reshape(8, P),
            f(inp["hgrn_lb_logits"][1]).reshape(8, P),
            f(inp["hgrn_norm_w"][0]).reshape(1, P),
            b_ada[0:D].reshape(8, P), b_ada[D:2 * D].reshape(8, P),
            b_ada[3 * D:4 * D].reshape(8, P), b_ada[4 * D:5 * D].reshape(8, P)], axis=0)
        m = dict(shared)
        m["x_main"] = np.ascontiguousarray(x[b, half * Tc:(half + 1) * Tc])
        m["x_warm"] = np.ascontiguousarray(x[b, 0:Tc])
        m["rows"] = np.ascontiguousarray(rows)
        m["flag"] = np.full((P, 1), float(half), np.float32)
        maps.append(m)
    return maps


_NC_CACHE = {}


def run(inp, n_tiles, seq, dumps=None):
    key = (n_tiles, tuple(sorted(dumps)) if dumps else None)
    if key not in _NC_CACHE:
        _NC_CACHE[key] = build(n_tiles, dumps)
    nc = _NC_CACHE[key]
    maps = make_in_maps(inp, n_tiles, seq)
    res = run_bass_kernel_spmd(nc, maps, core_ids=list(range(N_CORES)))
    Tc = n_tiles * T
    out = np.empty((BATCH, seq, D), np.float32)
    for core in range(N_CORES):
        b, half = core // 2, core % 2
        out[b, half * Tc:(half + 1) * Tc] = res.results[core]["out"]
    return out, res


def kernel(**inputs):
    out, _ = run(inputs, SEQ // 2 // T, SEQ)
    return out
```

```python
from contextlib import ExitStack

import ml_dtypes
import numpy as np

import concourse.bass as bass
import concourse.mybir as mybir
from concourse.bass_utils import run_bass_kernel_spmd

F32 = mybir.dt.float32
BF16 = mybir.dt.bfloat16
AF = mybir.ActivationFunctionType
ALU = mybir.AluOpType

P = 128
D = 1024
T = 512
NB = T // P
NCH = T // 64
H = 8
DFF = 2816
NFF = DFF // P
ALPHA = 2.0 ** 0.25
LN_EPS = 1e-5
RMS_EPS = 1e-6
NROWS = 81
NSLOT = 5
RSTD_POW = True
MERGE_W = (1, 1)
N_CORES = 8
SEQ = 8192
BATCH = 4


class Buf:
    __slots__ = ("name", "w", "r")

    def __init__(self, name):
        self.name = name
        self.w = {}
        self.r = {}


class Chan:
    __slots__ = ("sem", "count", "key")

    def __init__(self, sem, key):
        self.sem = sem
        self.count = 0
        self.key = key


class Sched:
    def __init__(self):
        self.ops = []
        self.last = {}
        self.bar = {}

    def add(self, eng, fn, reads=(), writes=(), chan=None, skip_self=False):
        oid = len(self.ops)
        key = chan.key if chan is not None else eng
        deps = dict(self.bar)

        def need(evs):
            for k, o in evs.items():
                if skip_self and k == key:
                    continue
                if deps.get(k, -1) < o:
                    deps[k] = o

        for b in reads:
            need(b.w)
        for b in writes:
            need(b.w)
            need(b.r)
        val = None
        if chan is not None:
            chan.count += 1
            val = chan.count * 16
        self.ops.append(dict(eng=eng, fn=fn, deps=deps, chan=chan, key=key, marked=False, val=val))
        for b in reads:
            b.r[key] = oid
        for b in writes:
            b.w = {key: oid}
            b.r = {}
        self.last[key] = oid
        return oid

    def barrier(self, exclude=()):
        self.bar = {k: v for k, v in self.last.items() if k not in exclude}

    def emit(self, nc, sems):
        ops = self.ops
        for op in ops:
            for k, o in op["deps"].items():
                if k == "pe" and op["eng"] == "pe" and op["chan"] is None:
                    continue
                ops[o]["marked"] = True
        cnt = {}
        for op in ops:
            if op["chan"] is None and op["marked"]:
                cnt[op["key"]] = cnt.get(op["key"], 0) + 1
                op["val"] = cnt[op["key"]]
        by_eng = {}
        for op in ops:
            by_eng.setdefault(op["eng"], []).append(op)

        def body(engname):
            def f(e):
                waited = {}
                for op in by_eng.get(engname, []):
                    for k, o in sorted(op["deps"].items(), key=lambda kv: str(kv[0])):
                        if k == "pe" and engname == "pe" and op["chan"] is None:
                            continue
                        p = ops[o]
                        sem = p["chan"].sem if p["chan"] is not None else sems[p["key"]]
                        v = p["val"]
                        if waited.get(k, 0) >= v:
                            continue
                        e.wait_ge(sem, v)
                        waited[k] = v
                    if op["fn"] is None:
                        continue
                    ins = op["fn"](e)
                    if op["chan"] is not None:
                        ins.then_inc(op["chan"].sem, 16)
                    elif op["marked"]:
                        ins.then_inc(sems[engname], 1)

            return f

        with nc.Block() as block:
            block.tensor(body("pe"))
            block.scalar(body("act"))
            block.vector(body("dve"))
            block.gpsimd(body("pool"))
            block.sync(body("sp"))


def build(n_tiles, dumps=None):
    nc = bass.Bass("TRN2", target_bir_lowering=False)
    Tc = n_tiles * T
    es = ExitStack()
    S = Sched()
    dump_list = []

    def din(name, shape, dtype=F32):
        return nc.dram_tensor(name, list(shape), dtype, kind="ExternalInput").ap()

    x_main = din("x_main", [Tc, D])
    x_warm = din("x_warm", [Tc, D])
    rows_in = din("rows", [NROWS, P])
    flag_in = din("flag", [P, 1])
    w_ada = din("w_ada", [D, 6 * D])
    b_ada = din("b_ada", [1, 6 * D])
    w_in = din("w_in", [D, 8 * D])
    w_pa = din("w_proj_a", [D, D])
    w_pb = din("w_proj_b", [D, D])
    w_o = din("w_out", [D, D])
    w_fi = din("w_ffn_in", [D, 2 * DFF])
    w_fo = din("w_ffn_out", [DFF, D])
    lnb_in = din("gmlp_ln_b", [1, D])
    ws_in = din("gmlp_ws", [H, P, P])
    bs_in = din("gmlp_bs", [1, H * P])
    ln1w_in = din("ln1_w", [1, D])
    ln1b_in = din("ln1_b", [1, D])
    ln2w_in = din("ln2_w", [1, D])
    ln2b_in = din("ln2_b", [1, D])
    identf_in = din("ident_f", [P, P])
    identb_in = din("ident_b", [P, P], BF16)
    mask2_in = din("mask2", [P, P])
    m64_in = din("m64", [P, T])
    m32_in = din("m32", [P, T])
    out_d = nc.dram_tensor("out", [Tc, D], F32, kind="ExternalOutput").ap()

    wi_bf = nc.dram_tensor("wi_bf", [16, P, 8, 512], BF16).ap()
    wpa_bf = nc.dram_tensor("wpa_bf", [2, P, 8, 512], BF16).ap()
    wpb_bf = nc.dram_tensor("wpb_bf", [2, P, 8, 512], BF16).ap()
    wo_bf = nc.dram_tensor("wo_bf", [2, P, 8, 512], BF16).ap()
    wfi_bf = nc.dram_tensor("wfi_bf", [11, P, 8, 512], BF16).ap()
    wfo_bf = nc.dram_tensor("wfo_bf", [6, P, 8, 512], BF16).ap()

    nsem = [0]

    def new_sem(name):
        nsem[0] += 1
        return es.enter_context(nc.semaphore(name))

    sems = {k: new_sem("s_" + k) for k in ("pe", "act", "dve", "pool", "sp")}

    def new_chan(name):
        return Chan(new_sem("c_" + name), ("dma", name))

    def sb(name, shape, dtype=F32):
        return es.enter_context(nc.sbuf_tensor("sb_" + name, list(shape), dtype))

    def dma(q, out, in_, reads, writes, chan, skip_self=False):
        S.add(q, lambda e: e.dma_start(out=out, in_=in_), reads, writes, chan=chan, skip_self=skip_self)

    def mm(out, lhsT, rhs, start, stop, reads, writes):
        S.add("pe", lambda e: e.matmul(out, lhsT, rhs, start=start, stop=stop), reads, writes)

    def tr(out, in_, ident, reads, writes):
        S.add("pe", lambda e: e.transpose(out, in_, ident), reads, writes)

    def act(out, in_, func, reads, writes, scale=1.0, bias=0.0):
        S.add("act", lambda e: e.activation(out=out, in_=in_, func=func, bias=bias, scale=scale), reads, writes)

    def tt(eng, out, in0, in1, op, reads, writes):
        S.add(eng, lambda e: e.tensor_tensor(out=out, in0=in0, in1=in1, op=op), reads, writes)

    def ts(eng, out, in0, s1, s2, op0, op1, reads, writes):
        if op1 is None:
            S.add(eng, lambda e: e.tensor_scalar(out=out, in0=in0, scalar1=s1, scalar2=None, op0=op0), reads, writes)
        else:
            S.add(eng, lambda e: e.tensor_scalar(out=out, in0=in0, scalar1=s1, scalar2=s2, op0=op0, op1=op1),
                  reads, writes)

    def stt(out, in0, scalar, in1, op0, op1, reads, writes):
        S.add("dve", lambda e: e.scalar_tensor_tensor(out=out, in0=in0, scalar=scalar, in1=in1, op0=op0, op1=op1),
              reads, writes)

    def copy(eng, out, in_, reads, writes):
        if eng == "act":
            act(out, in_, AF.Copy, reads, writes)
        else:
            S.add(eng, lambda e: e.tensor_copy(out=out, in_=in_), reads, writes)

    def dump(name, ap, buf):
        if dumps is None or name not in dumps:
            return
        shape = list(ap.shape)
        d = nc.dram_tensor("dbg_" + name, shape, ap.dtype, kind="ExternalOutput").ap()
        ch = new_chan("dbg_" + name)
        b = Buf("dbgd_" + name)
        dma("pool", d, ap, [buf], [b], ch)
        dump_list.append(b)

    NBANK = 8
    banks = [es.enter_context(nc.psum_tensor("bank%d" % i, [P, 512], F32)) for i in range(NBANK)]
    bank_bufs = [Buf("bank%d" % i) for i in range(NBANK)]
    rings = {"all": [0, list(range(NBANK))], "z": [0, [0, 1, 2]], "m": [0, [3, 4, 5, 6, 7]]}

    def psum(pool="all"):
        r = rings[pool]
        i = r[1][r[0] % len(r[1])]
        r[0] += 1
        return banks[i], bank_bufs[i]

    ident_f = sb("ident_f", [P, P])
    ident_b = sb("ident_b", [P, P], BF16)
    mask2 = sb("mask2", [P, P])
    m64 = sb("m64", [P, T])
    m32 = sb("m32", [P, T])
    ones_f = sb("ones_f", [P, P])
    ones_b = sb("ones_b", [P, P], BF16)
    colv = sb("colv", [P, NROWS])
    modv = sb("modv", [P, 32])
    scp = sb("scp", [P, 16])
    lbv = sb("lbv", [P, 24])
    flag = sb("flag", [P, 1])
    wmT = sb("wmT", [P, H, P], BF16)
    r_sb = sb("r_sb", [P, H, P])
    g1_bc = sb("g1_bc", [P, D])
    g2_bc = sb("g2_bc", [P, D])
    ln1w_bc = sb("ln1w_bc", [P, D])
    ln1b_bc = sb("ln1b_bc", [P, D])
    ln2w_bc = sb("ln2w_bc", [P, D])
    ln2b_bc = sb("ln2b_bc", [P, D])
    carry = sb("carry", [P, H, P])
    B_const = Buf("consts")
    B_carry = [Buf("carry%d" % h) for h in range(H)]

    conv_A = Buf("conv_A")
    conv_B = Buf("conv_B")
    ch_A = new_chan("conv_A")
    ch_B = new_chan("conv_B")

    def w3(w):
        return w.rearrange("(kc p) n -> p kc n", p=P)

    pieces = {}

    deferred = []
    deferred_A = []

    def conv(name, dst, srcs, grpbuf, ch, kcn=8):
        for src, co in srcs:
            ncols = src.shape[2]
            args = ("pool", dst[:, 0:kcn, co:co + ncols], src, [], [grpbuf], ch, True)
            if grpbuf is conv_A:
                deferred_A.append(args)
            else:
                deferred.append(args)
        pieces[name] = (dst, grpbuf, kcn)

    def issue_deferred(n):
        for _ in range(min(n, len(deferred))):
            dma(*deferred.pop(0))

    NAMES_IN = ["q0", "q1", "f0", "f1", "i0", "i1", "g0", "g1", "u0", "u1", "v0", "v1", "ga0", "ga1", "gb0", "gb1"]
    first = ["f0", "f1", "i0", "i1"]
    order = first + [n for n in NAMES_IN if n not in first]
    for n in order:
        j = NAMES_IN.index(n)
        grp = (conv_A, ch_A) if n in first else (conv_B, ch_B)
        conv(n, wi_bf[j], [(w3(w_in)[:, :, j * 512:(j + 1) * 512], 0)], *grp)
    for j in range(2):
        conv("pa%d" % j, wpa_bf[j], [(w3(w_pa)[:, :, j * 512:(j + 1) * 512], 0)], conv_B, ch_B)
        conv("pb%d" % j, wpb_bf[j], [(w3(w_pb)[:, :, j * 512:(j + 1) * 512], 0)], conv_B, ch_B)
    for j in range(2):
        conv("o%d" % j, wo_bf[j], [(w3(w_o)[:, :, j * 512:(j + 1) * 512], 0)], conv_B, ch_B)
    for m in range(11):
        conv("fi%d" % m, wfi_bf[m],
             [(w3(w_fi)[:, :, 256 * m:256 * (m + 1)], 0),
              (w3(w_fi)[:, :, DFF + 256 * m:DFF + 256 * (m + 1)], 256)], conv_B, ch_B)
    for kg in range(3):
        k0, k1 = kg * 8, min(NFF, kg * 8 + 8)
        for half in range(2):
            conv("fo%d_%d" % (kg, half), wfo_bf[kg * 2 + half],
                 [(w3(w_fo)[:, k0:k1, half * 512:(half + 1) * 512], 0)], conv_B, ch_B, kcn=k1 - k0)

    ch_c = new_chan("consts")
    ch_r = new_chan("consts_early")
    B_early = Buf("consts_early")
    dma("sp", ident_f[:], identf_in, [], [B_early], ch_r, skip_self=True)
    for dst, src in ((ident_b, identb_in), (mask2, mask2_in), (m64, m64_in), (m32, m32_in),
                     (flag, flag_in)):
        dma("sp", dst[:], src, [], [B_const], ch_c, skip_self=True)
    for dst, src in ((ln1w_bc, ln1w_in), (ln1b_bc, ln1b_in), (ln2w_bc, ln2w_in), (ln2b_bc, ln2b_in)):
        dma("sp", dst[:], src[0:1, :].broadcast_to([P, D]), [], [B_const], ch_c, skip_self=True)

    with ExitStack() as ses:
        def ssb(name, shape, dtype=F32):
            return ses.enter_context(nc.sbuf_tensor("ss_" + name, list(shape), dtype))

        rows_sb = ssb("rows_sb", [NROWS, P])
        cond2 = ssb("cond2", [P, 8, 2])
        condB = ssb("condB", [P, 8, P])
        NWA = 6
        wa = [ssb("wa%d" % i, [P, 8, 512]) for i in range(NWA)]
        bada_bc = ssb("bada_bc", [P, 2, D])
        lnb_bc = ssb("lnb_bc", [P, D])
        ws_sb = ssb("ws_sb", [P, H, P])
        wmT_f = ssb("wmT_f", [P, H, P])
        bs_sb = ssb("bs_sb", [1, H * P])
        dl = ssb("dl", [P, 8])
        B_rows = Buf("rows")
        B_set = Buf("setup_small")
        B_wa = [Buf("wa%d" % i) for i in range(NWA)]
        ch_wa = [new_chan("wa%d" % i) for i in range(NWA)]
        B_condB = Buf("condB")
        B_gm = Buf("gm")

        dma("sp", rows_sb[:], rows_in, [], [B_early], ch_r, skip_self=True)
        dma("sp", bada_bc[:, 0, :], b_ada[0:1, 2 * D:3 * D].broadcast_to([P, D]), [], [B_const], ch_c, skip_self=True)
        dma("sp", bada_bc[:, 1, :], b_ada[0:1, 5 * D:6 * D].broadcast_to([P, D]), [], [B_const], ch_c, skip_self=True)
        dma("sp", lnb_bc[:], lnb_in[0:1, :].broadcast_to([P, D]), [], [B_const], ch_c, skip_self=True)
        dma("sp", ws_sb[:], ws_in.rearrange("g t s -> t g s"), [], [B_const], ch_c, skip_self=True)
        dma("sp", bs_sb[:], bs_in, [], [B_const], ch_c, skip_self=True)

        B_ones = Buf("ones")
        S.add("dve", lambda e: e.memset(ones_f[:], 1.0), [], [B_ones])
        S.add("dve", lambda e: e.memset(ones_b[:], 1.0), [], [B_ones])
        S.add("dve", lambda e: e.memset(carry[:], 0.0), [], B_carry)
        for i in range(NBANK):
            S.add("dve", (lambda i: lambda e: e.memset(banks[i][:], 0.0))(i), [], [bank_bufs[i]])

        pb, pbuf = psum()
        tr(pb[:, 0:NROWS], rows_sb[:], ident_f[0:NROWS, 0:NROWS], [B_early], [pbuf])
        copy("dve", colv[:], pb[:, 0:NROWS], [pbuf], [B_set])
        act(cond2[:, :, 0], colv[:, 0:8], AF.Silu, [B_set], [B_condB])
        act(cond2[:, :, 1], colv[:, 0:8], AF.Silu, [B_set], [B_condB])
        for kc in range(8):
            ts("dve", condB[:, kc, :], ones_f[:], cond2[:, kc, 0:1], None, ALU.mult, None, [B_condB, B_ones],
               [B_condB])
        tt("dve", dl[:], colv[:, 32:40], colv[:, 40:48], ALU.subtract, [B_set], [B_set])
        act(lbv[:, 0:8], dl[:], AF.Sigmoid, [B_set], [B_set])
        ts("dve", lbv[:, 8:16], lbv[:, 0:8], -1.0, 1.0, ALU.mult, ALU.add, [B_set], [B_set])
        ts("dve", lbv[:, 16:24], lbv[:, 0:8], -1.0, None, ALU.add, None, [B_set], [B_set])

        modp, modpbuf = psum()
        modp_v = modp[:, 0:64].rearrange("p (a b) -> p a b", b=2)
        wa3 = w3(w_ada)
        col_of = {0: 0, 1: 8, 3: 16, 4: 24}
        for ct in range(12):
            gi = ct // 2
            sl = ct % NWA
            dma("sp", wa[sl][:], wa3[:, :, ct * 512:(ct + 1) * 512], [], [B_wa[sl]], ch_wa[sl])
            if gi in col_of:
                for jb in range(4):
                    col = col_of[gi] + (ct % 2) * 4 + jb
                    for kc in range(8):
                        mm(modp_v[:, col, :], wa[sl][:, kc, jb * P:(jb + 1) * P], cond2[:, kc, :], kc == 0, kc == 7,
                           [B_wa[sl], B_condB], [modpbuf])
            else:
                gb_, gbuf = psum()
                for kc in range(8):
                    mm(gb_[:], condB[:, kc, :], wa[sl][:, kc, :], kc == 0, kc == 7, [B_wa[sl], B_condB], [gbuf])
                gdst = g1_bc if gi == 2 else g2_bc
                half = ct % 2
                tt("dve", gdst[:, half * 512:(half + 1) * 512], gb_[:],
                   bada_bc[:, 0 if gi == 2 else 1, half * 512:(half + 1) * 512], ALU.add, [gbuf, B_const], [B_const])
        for i_, args in enumerate(deferred_A):
            q_, o_, i__, r_, w_, c_, sk_ = args
            dma(q_, o_, i__, (B_wa if i_ == 0 else []), w_, c_, sk_)
        tt("dve", modv[:], modp_v[:, 0:32, 0], colv[:, 49:81], ALU.add, [modpbuf, B_set], [B_set])
        ts("dve", scp[:, 0:8], modv[:, 8:16], 1.0, None, ALU.add, None, [B_set], [B_set])
        ts("dve", scp[:, 8:16], modv[:, 24:32], 1.0, None, ALU.add, None, [B_set], [B_set])

        for g in range(H):
            pb, pbuf = psum()
            tr(pb[:, 0:P], ws_sb[:, g, :], ident_f[:], [B_const], [pbuf])
            copy("dve", wmT_f[:, g, :], pb[:, 0:P], [pbuf], [B_gm])
        S.add("dve", lambda e: e.memset(wmT_f[64:128, :, 0:64], 0.0), [], [B_gm])
        copy("dve", wmT[:], wmT_f[:], [B_gm], [B_const])
        for g in range(H):
            pb, pbuf = psum()
            mm(pb[:, 0:P], lnb_bc[:, g * P:(g + 1) * P], wmT_f[:, g, :], True, False, [B_const, B_gm], [pbuf])
            mm(pb[:, 0:P], ones_f[0:1, :], bs_sb[0:1, g * P:(g + 1) * P], False, True, [B_const, B_ones], [pbuf])
            copy("dve", r_sb[:, g, :], pb[:, 0:P], [pbuf], [B_const])
        S.barrier(exclude=(ch_A.key, ch_B.key))

    sh1 = modv[:, 0:8]
    sc1p = scp[:, 0:8]
    sh2 = modv[:, 16:24]
    sc2p = scp[:, 8:16]
    lb = lbv[:, 0:8]
    oml = lbv[:, 8:16]
    noml = lbv[:, 16:24]
    normw = colv[:, 48:49]
    lnw_col = colv[:, 8:16]
    bga = colv[:, 16:24]
    bgb = colv[:, 24:32]

    h_sb = sb("h_sb", [P, NB, D])
    arena1 = sb("arena1", [P, 6 * 8 * T], BF16)

    def a1(i, a):
        return arena1[:, i * 8 * T:(i + 1) * 8 * T].rearrange("p (a b) -> p a b", a=a)

    uT = a1(0, 8)
    v_sb = a1(1, NB)
    vn_sb = a1(2, NB)
    uG = a1(3, H)
    hgT = a1(4, H)
    ubT = a1(5, H)
    u2T = a1(0, 8)
    actT = arena1[:, 8 * T:(8 + NFF) * T].rearrange("p (a b) -> p a b", a=NFF)
    xpre = arena1[:, 32 * T:48 * T].bitcast(F32).rearrange("p (a b) -> p a b", a=NB)
    B_xpre = [Buf("xpre%d" % b) for b in range(NB)]
    tokb2 = [arena1[:, 16 * T + i * D:16 * T + (i + 1) * D] for i in range(NB)]
    B_tokb2 = [Buf("tokb2_%d" % i) for i in range(NB)]
    sgall = arena1[:, 16 * T:32 * T].bitcast(F32).rearrange("p (a b) -> p a b", a=H)
    B_sgall = [Buf("sgall%d" % h) for h in range(H)]
    wslots = [sb("wslot%d" % i, [P, 8, 512], BF16) for i in range(NSLOT)]
    tbig = sb("tbig", [P, 4, T])
    tq, tsg, tgt, tlf = tbig[:, 0, :], tbig[:, 1, :], tbig[:, 2, :], tbig[:, 3, :]
    mixT = tbig[:, :, :].bitcast(BF16).rearrange("p a (c b) -> p (a c) b", b=T)
    tkk = sb("tkk", [P, T])
    tcum = sb("tcum", [P, T])
    tc32 = sb("tc32", [P, T])
    te1 = sb("te1", [P, T])
    te3 = sb("te3", [P, T])
    ten = sb("ten", [P, T])
    trc = sb("trc", [P, T])
    td1 = sb("td1", [P, NCH, 32])
    qt2 = [sb("qt%d" % i, [P, T], BF16) for i in range(2)]
    qs2 = [sb("qs%d" % i, [P, T], BF16) for i in range(2)]
    kd2 = [sb("kd%d" % i, [P, T], BF16) for i in range(2)]
    K02 = [sb("K0_%d" % i, [P, NCH, 32], BF16) for i in range(2)]
    K12 = [sb("K1_%d" % i, [P, NCH, 64], BF16) for i in range(2)]
    dec2 = [sb("dec%d" % i, [P, NCH]) for i in range(2)]
    kdT_sb = sb("kdT_sb", [P, NB, P], BF16)
    A_sb = sb("A_sb", [P, NB, P], BF16)
    osq = sb("osq", [P, T], BF16)
    Sst = sb("Sst", [P, NCH + 1, P])
    Sbf = sb("Sbf", [P, NCH, P], BF16)
    tokb = [arena1[:, 32 * T + i * D:32 * T + (i + 1) * D] for i in range(NB)]
    tok = [arena1[:, 40 * T + i * 2 * D:40 * T + (i + 1) * 2 * D].bitcast(F32) for i in range(2)]
    st6 = [sb("st6_%d" % i, [P, 12]) for i in range(2 * NB)]
    mv = [sb("mv%d" % i, [P, 4]) for i in range(2 * NB)]
    ftmp = [sb("ftmp%d" % i, [P, T]) for i in range(2)]
    ftmp2 = [sb("ftmp2_%d" % i, [P, T]) for i in range(2)]
    neghalf = sb("neghalf", [P, 1])
    S.add("pool", lambda e: e.memset(neghalf[:], -0.5), [], [B_const])

    B_h = [Buf("h%d" % b) for b in range(NB)]
    ch_h = [new_chan("h%d" % b) for b in range(NB)]
    B_uT = [Buf("uT%d" % b) for b in range(NB)]
    B_v = [[Buf("v%d_%d" % (b, j)) for j in range(2)] for b in range(NB)]
    B_vn = [Buf("vn%d" % b) for b in range(NB)]
    B_uG = [Buf("uG%d" % g) for g in range(H)]
    B_hgT = [Buf("hgT%d" % h) for h in range(H)]
    B_ubT = [Buf("ubT%d" % g) for g in range(H)]
    B_mixT = [Buf("mixT%d" % o) for o in range(8)]
    B_u2T = [Buf("u2T%d" % b) for b in range(NB)]
    B_actT = [Buf("actT%d" % j) for j in range(NFF)]
    ch_xpre = [new_chan("xpre%d" % b) for b in range(NB)]
    B_slot = [Buf("slot%d" % i) for i in range(NSLOT)]
    ch_slot = [new_chan("slot%d" % i) for i in range(NSLOT)]
    B_t = {n: Buf("t_" + n) for n in ("q", "sg", "gt", "lf", "kk", "cum", "c32", "e1", "e3", "en", "rc", "d1", "qt",
                                     "qs", "kd", "K0", "K1a", "K1b", "kdT", "A", "osq", "Sst", "Sbf", "Sbf2")}
    B_p = [{n: Buf("p%d_%s" % (i, n)) for n in ("qt", "qs", "kd", "K0", "K1a", "K1b", "dec")} for i in range(2)]
    B_tok = [Buf("tok0"), Buf("tok1")]
    B_tokb = [Buf("tokb%d" % i) for i in range(NB)]
    tokb_main, B_tokb_main = tokb, B_tokb
    B_mv = [Buf("mv%d" % i) for i in range(2 * NB)]
    B_ftmp = [Buf("ftmp0"), Buf("ftmp1")]
    B_ftmp2 = [Buf("ftmp2_0"), Buf("ftmp2_1")]
    tokc = [0]
    ftc = [0]

    wmap = {}
    wnext = [0]

    def wload(name):
        if name in wmap:
            return
        i = wnext[0]
        wnext[0] = (i + 1) % NSLOT
        for k in [k for k, vv in wmap.items() if vv == i]:
            del wmap[k]
        wmap[name] = i
        src, gbuf, kcn = pieces[name]
        dma("sp", wslots[i][:, 0:kcn, :], src[:, 0:kcn, :], [gbuf], [B_slot[i]], ch_slot[i])

    wseq = [[]]

    def wget(name):
        wload(name)
        seq = wseq[0]
        if name in seq:
            i = seq.index(name)
            if i + 1 < len(seq):
                wload(seq[i + 1])
        i = wmap[name]
        return wslots[i], B_slot[i]

    def g_layernorm_stats_multi(items):
        for src_ap, src_bufs, k in items:
            S.add("dve", (lambda src_ap, k: lambda e: e.bn_stats(out=st6[k][:, 0:6], in_=src_ap[:, 0:512]))(src_ap, k),
                  src_bufs, [B_mv[k]])
            S.add("dve", (lambda src_ap, k: lambda e: e.bn_stats(out=st6[k][:, 6:12], in_=src_ap[:, 512:1024]))(src_ap, k),
                  src_bufs, [B_mv[k]])
        yield
        for src_ap, src_bufs, k in items:
            S.add("dve", (lambda k: lambda e: e.bn_aggr(out=mv[k][:, 0:2], in_=st6[k][:, 0:12]))(k), [B_mv[k]],
                  [B_mv[k]])
            ts("dve", mv[k][:, 2:3], mv[k][:, 1:2], LN_EPS, None, ALU.add, None, [B_mv[k]], [B_mv[k]])
        yield
        for src_ap, src_bufs, k in items:
            if RSTD_POW:
                tt("pool", mv[k][:, 2:3], mv[k][:, 2:3], neghalf[:], ALU.pow, [B_mv[k], B_const], [B_mv[k]])
            else:
                act(mv[k][:, 2:3], mv[k][:, 2:3], AF.Ln, [B_mv[k]], [B_mv[k]])
                act(mv[k][:, 2:3], mv[k][:, 2:3], AF.Exp, [B_mv[k]], [B_mv[k]], scale=-0.5)
        yield
        for src_ap, src_bufs, k in items:
            stt(mv[k][:, 3:4], mv[k][:, 0:1], -1.0, mv[k][:, 2:3], ALU.mult, ALU.mult, [B_mv[k]], [B_mv[k]])
        yield

    def layernorm_stats_multi(items):
        for _ in g_layernorm_stats_multi(items):
            pass

    def layernorm_stats(src_ap, src_bufs, k):
        layernorm_stats_multi([(src_ap, src_bufs, k)])

    def to_featmajor(src_bf, src_bufs, dstT, dst_buf, blk, scale_cols, bias_cols):
        pb, pbuf = psum()
        pv = pb.bitcast(BF16)[:, :].rearrange("p (a b) -> p a b", a=8)
        for kc in range(8):
            tr(pv[:, kc, :], src_bf[:, kc * P:(kc + 1) * P], ident_b[:], src_bufs + [B_const], [pbuf])
        for kc in range(8):
            act(dstT[:, kc, blk * P:(blk + 1) * P], pv[:, kc, :], AF.Identity, [pbuf, B_const], [dst_buf],
                scale=scale_cols[:, kc:kc + 1], bias=bias_cols[:, kc:kc + 1])

    def load_x(src, ti):
        for b in range(NB):
            r0 = (ti * NB + b) * P
            dma("act", h_sb[:, b, :], src[r0:r0 + P, :], [], [B_h[b]], ch_h[b])

    def g_adaln_all(dstT, B_dst, scale_cols, bias_cols, src=None, tb=None, koff=0):
        if src is None:
            src = [(h_sb[:, blk, :], [B_h[blk]]) for blk in range(NB)]
        tokb, B_tokb = tb if tb is not None else (tokb_main, B_tokb_main)
        yield from g_layernorm_stats_multi([(src[blk][0], src[blk][1], blk + koff) for blk in range(NB)])
        for blk in range(NB):
            act(tokb[blk][:], src[blk][0], AF.Identity, src[blk][1] + [B_mv[blk + koff]], [B_tokb[blk]],
                scale=mv[blk + koff][:, 2:3], bias=mv[blk + koff][:, 3:4])
            yield
        for j in range(4):
            pb, pbuf = psum()
            pv = pb.bitcast(BF16)[:, :].rearrange("p (k a b) -> p k a b", k=2, a=NB)
            for k2 in range(2):
                kc = 2 * j + k2
                for blk in range(NB):
                    tr(pv[:, k2, blk, :], tokb[blk][:, kc * P:(kc + 1) * P], ident_b[:], [B_tokb[blk], B_const],
                       [pbuf])
                yield
            for k2 in range(2):
                kc = 2 * j + k2
                act(dstT[:, kc, :], pv[:, k2, :, :].rearrange("p a b -> p (a b)"), AF.Identity, [pbuf, B_const],
                    B_dst, scale=scale_cols[:, kc:kc + 1], bias=bias_cols[:, kc:kc + 1])
                yield

    def adaln_all(*a_, **k_):
        for _ in g_adaln_all(*a_, **k_):
            pass

    def proj_feat(wt, wbuf, off, xT, xbufs, pool="all"):
        pb, pbuf = psum(pool)
        for kc in range(8):
            mm(pb[:], wt[:, kc, off:off + P], xT[:, kc, :], kc == 0, kc == 7, [wbuf] + xbufs, [pbuf])
        return pb, pbuf

    def proj_tok(wt, wbuf, xT, xbuf, blk, pool="all"):
        pb, pbuf = psum(pool)
        for kc in range(8):
            mm(pb[:], xT[:, kc, blk * P:(blk + 1) * P], wt[:, kc, :], kc == 0, kc == 7, [wbuf, xbuf], [pbuf])
        return pb, pbuf

    def g_make_v(js):
        for j in js:
            wt, wbuf = wget("i%d" % j)
            for blk in range(NB):
                pb, pbuf = proj_tok(wt, wbuf, uT, B_uT[blk], blk, pool="m")
                copy("act" if blk % 2 == 0 else "dve", v_sb[:, blk, j * 512:(j + 1) * 512], pb[:], [pbuf],
                     [B_v[blk][j]])
                yield

    c3 = lambda t_: t_[:, :].rearrange("p (c s) -> p c s", s=64)

    def hgrn_head_z(h, full):
        j, off = h // 4, (h % 4) * P
        res = {}
        names = ("f", "q", "g") if full else ("f",)
        for nm in names:
            wt, wbuf = wget("%s%d" % (nm, j))
            res[nm] = proj_feat(wt, wbuf, off, uT, B_uT, pool="z")
            yield res

    def hgrn_head_ew(h, z, full, pre_sg=False):
        B = dict(B_t)
        p = h % 2
        Bp = B_p[p]
        qt, qs, kd, K0, K1, dec = qt2[p], qs2[p], kd2[p], K02[p], K12[p], dec2[p]
        tgt_ = ftmp2[p]
        tsg = tbig[:, 1, :]
        if pre_sg:
            tsg = sgall[:, h, :]
            B["sg"] = B_sgall[h]
        else:
            zf, zfb = z["f"]
            act(tsg[:], zf[:], AF.Sigmoid, [zfb], [B["sg"]])
            yield
        if full:
            zq, zqb = z["q"]
            zg, zgb = z["g"]
            act(tq[:], zq[:], AF.Silu, [zqb], [B["q"]])
            yield
            act(tgt_[:], zg[:], AF.Silu, [zgb], [B_ftmp2[p]])
            yield
        act(tlf[:], tsg[:], AF.Ln, [B["sg"], B_const], [B["lf"]], scale=oml[:, h:h + 1], bias=lb[:, h:h + 1])
        yield
        S.add("dve", lambda e: e.tensor_tensor_scan(out=tcum[:], data0=m64[:], data1=tlf[:], initial=0.0,
                                                    op0=ALU.mult, op1=ALU.add), [B["lf"], B_const], [B["cum"]])
        yield
        if full:
            S.add("dve", lambda e: e.tensor_tensor_scan(out=tc32[:], data0=m32[:], data1=tlf[:], initial=0.0,
                                                        op0=ALU.mult, op1=ALU.add), [B["lf"], B_const], [B["c32"]])
            yield
        ts("dve", tkk[:], tsg[:], noml[:, h:h + 1], oml[:, h:h + 1], ALU.mult, ALU.add, [B["sg"], B_const], [B["kk"]])
        yield
        act(te1[:], tcum[:], AF.Exp, [B["cum"]], [B["e1"]])
        yield
        if full:
            act(te3[:], tc32[:], AF.Exp, [B["c32"]], [B["e3"]])
            yield
            act(ten[:], tc32[:], AF.Exp, [B["c32"]], [B["en"]], scale=-1.0)
            yield
        cum3 = c3(tcum)
        tt("dve", c3(trc), cum3[:, :, 63:64].broadcast_to([P, NCH, 64]), cum3, ALU.subtract, [B["cum"]], [B["rc"]])
        yield
        if full:
            tt("dve", td1[:], cum3[:, :, 31:32].broadcast_to([P, NCH, 32]), cum3[:, :, 0:32], ALU.subtract,
               [B["cum"]], [B["d1"]])
            yield
        act(trc[:], trc[:], AF.Exp, [B["rc"]], [B["rc"]])
        yield
        if full:
            act(td1[:], td1[:], AF.Exp, [B["d1"]], [B["d1"]])
            yield
        copy("pool", dec[:], c3(te1)[:, :, 63], [B["e1"]], [Bp["dec"]])
        yield
        if full:
            kk3, en3 = c3(tkk), c3(ten)
            tt("dve", qt[:], tq[:], te1[:], ALU.mult, [B["q"], B["e1"]], [Bp["qt"]])
            yield
            tt("dve", qs[:], tq[:], te3[:], ALU.mult, [B["q"], B["e3"]], [Bp["qs"]])
            yield
            tt("dve", K0[:], kk3[:, :, 0:32], en3[:, :, 0:32], ALU.mult, [B["kk"], B["en"]], [Bp["K0"]])
            yield
            tt("dve", K1[:, :, 32:64], kk3[:, :, 32:64], en3[:, :, 32:64], ALU.mult, [B["kk"], B["en"]], [Bp["K1b"]])
            yield
        tt("dve", kd[:], tkk[:], trc[:], ALU.mult, [B["kk"], B["rc"]], [Bp["kd"]])
        yield
        if full:
            tt("dve", K1[:, :, 0:32], kk3[:, :, 0:32], td1[:], ALU.mult, [B["kk"], B["d1"]], [Bp["K1a"]])
            yield

    def hgrn_head_small(h, full):
        B = B_t
        p = h % 2
        Bp = B_p[p]
        qt, qs, kd, K0, K1, dec = qt2[p], qs2[p], kd2[p], K02[p], K12[p], dec2[p]
        tgt_, tX = ftmp2[p], ftmp[p]
        vb_all = [B_v[b][h // 4] for b in range(NB)]
        vcol = slice(h * P, (h + 1) * P)
        if full:
            pa, pab = psum("m")
            pav = pa[:, :].rearrange("p (a b) -> p a b", a=NB)
            S.add("dve", lambda e: e.memset(pa[:], 0.0), [], [pab])
            yield
            for c in range(NCH):
                blk, po = c // 2, (c % 2) * 64
                mm(pav[po:po + 32, blk, po:po + 32], K0[:, c, :], qs[:, c * 64:c * 64 + 32], True, True,
                   [Bp["K0"], Bp["qs"]], [pab])
                mm(pav[po:po + 64, blk, po + 32:po + 64], K1[:, c, :], qs[:, c * 64 + 32:c * 64 + 64], True, True,
                   [Bp["K1a"], Bp["K1b"], Bp["qs"]], [pab])
            yield
            mask_b = bass.AP(mask2[:].tensor, mask2[:].offset, [list(mask2[:].ap[0]), [0, NB], [1, P]])
            tt("dve", A_sb[:], pav, mask_b, ALU.mult, [pab, B_const], [B["A"]])
            yield
        pk, pkb = psum("m")
        pkv = pk.bitcast(BF16)[:, 0:NB * P].rearrange("p (a b) -> p a b", a=NB)
        for blk in range(NB):
            tr(pkv[:, blk, :], kd[:, blk * P:(blk + 1) * P], ident_b[:], [Bp["kd"], B_const], [pkb])
        yield
        copy("act", kdT_sb[:], pkv, [pkb], [B["kdT"]])
        yield
        Ue, Ueb = psum("m")
        Uo, Uob = psum("m")
        Uv = (Ue[:, :].rearrange("p (a b) -> p a b", a=NB), Uo[:, :].rearrange("p (a b) -> p a b", a=NB))
        Ub = (Ueb, Uob)
        for c in range(NCH):
            blk, po = c // 2, (c % 2) * 64
            mm(Uv[c % 2][:, blk, :], kdT_sb[po:po + 64, blk, :], v_sb[po:po + 64, blk, vcol], True, True,
               [B["kdT"], vb_all[blk]], [Ub[c % 2]])
        yield
        copy("pool", Sst[:, 0, :], carry[:, h, :], [B_carry[h]], [B["Sst"]])
        yield
        for c in range(NCH):
            stt(Sst[:, c + 1, :], Sst[:, c, :], dec[:, c:c + 1], Uv[c % 2][:, c // 2, :], ALU.mult,
                ALU.add, [B["Sst"], Bp["dec"], Ub[c % 2]], [B["Sst"]])
            yield
            if full and c == 3:
                copy("act", Sbf[:, 0:4, :], Sst[:, 0:4, :], [B["Sst"]], [B["Sbf"]])
                yield
        copy("pool", carry[:, h, :], Sst[:, NCH, :], [B["Sst"]], [B_carry[h]])
        yield
        if not full:
            return
        copy("act", Sbf[:, 4:8, :], Sst[:, 4:8, :], [B["Sst"]], [B_t["Sbf2"]])
        yield
        po_, pob = psum("m")
        for blk in range(NB):
            mm(po_[:, blk * P:(blk + 1) * P], v_sb[:, blk, vcol], A_sb[:, blk, :], True, False,
               [vb_all[blk], B["A"]], [pob])
            for c in (2 * blk, 2 * blk + 1):
                mm(po_[:, c * 64:(c + 1) * 64], Sbf[:, c, :], qt[:, c * 64:(c + 1) * 64], False, c % 2 == 1,
                   [B["Sbf"] if c < 4 else B_t["Sbf2"], Bp["qt"]], [pob])
        yield
        act(osq[:], po_[:], AF.Square, [pob], [B["osq"]])
        yield
        pss, pssb = psum("m")
        mm(pss[:], ones_b[:], osq[:], True, True, [B["osq"], B_const], [pssb])
        yield
        act(tX[:], pss[:], AF.Ln, [pssb], [B_ftmp[p]], scale=1.0 / P, bias=RMS_EPS)
        yield
        act(tX[:], tX[:], AF.Exp, [B_ftmp[p]], [B_ftmp[p]], scale=-0.5)
        yield
        tt("dve", tX[:], po_[:], tX[:], ALU.mult, [pob, B_ftmp[p]], [B_ftmp[p]])
        yield
        stt(hgT[:, h, :], tX[:], normw, tgt_[:], ALU.mult, ALU.mult, [B_ftmp[p], B_ftmp2[p], B_const],
            [B_hgT[h]] + B_tokb + B_xpre)
        yield

    def delayed(gen, n):
        for _ in range(n):
            yield
        yield from gen

    def run_merged(gens, weights=None):
        active = [(g, (weights[i] if weights else 1)) for i, g in enumerate(gens) if g is not None]
        while active:
            for g, w in list(active):
                try:
                    for _ in range(w):
                        next(g)
                except StopIteration:
                    active.remove((g, w))

    def hgrn_all(full, extra=None):
        zres = {}

        def zgen(h):
            last = None
            for last in hgrn_head_z(h, full):
                yield
            zres[h] = last

        if not full:
            for h in range(H):
                run_merged([zgen(h)])
                zf, zfb = zres[h]["f"]
                act(sgall[:, h, :], zf[:], AF.Sigmoid, [zfb], [B_sgall[h]])
            for i in range(-1, H):
                gens, wts = [], []
                if i + 1 < H:
                    gens.append(hgrn_head_ew(i + 1, None, False, pre_sg=True))
                    wts.append(MERGE_W[0])
                if i >= 0:
                    gens.append(hgrn_head_small(i, False))
                    wts.append(MERGE_W[1])
                if i == -1 and extra is not None:
                    gens.append(extra)
                    wts.append(1)
                run_merged(gens, wts)
            return
        run_merged([zgen(0)])
        for i in range(-1, H):
            gens, wts = [], []
            if i + 1 < H:
                gens.append(hgrn_head_ew(i + 1, zres[i + 1], full))
                wts.append(MERGE_W[0])
            if i >= 0:
                gens.append(hgrn_head_small(i, full))
                wts.append(MERGE_W[1])
            if i + 2 <= H - 1:
                gens.append(delayed(zgen(i + 2), 4 if full else 2))
                wts.append(1)
            if i == -1 and extra is not None:
                gens.append(extra)
                wts.append(1)
            run_merged(gens, wts)


    def gmlp():
        for g in range(H):
            wt, wbuf = wget("u%d" % (g // 4))
            pb, pbuf = proj_feat(wt, wbuf, (g % 4) * P, uT, B_uT)
            act(uG[:, g, :], pb[:], AF.Gelu, [pbuf], [B_uG[g]])
        for blk in range(NB):
            k = tokc[0] % 2
            tokc[0] += 1
            for j in range(2):
                wt, wbuf = wget("v%d" % j)
                pb, pbuf = proj_tok(wt, wbuf, uT, B_uT[blk], blk)
                act(tok[k][:, j * 512:(j + 1) * 512], pb[:], AF.Gelu, [pbuf], [B_tok[k]] + B_xpre)
            layernorm_stats(tok[k], [B_tok[k]], k)
            ts("dve", vn_sb[:, blk, :], tok[k][:], mv[k][:, 0:1], mv[k][:, 2:3], ALU.subtract, ALU.mult,
               [B_tok[k], B_mv[k]], [B_vn[blk]] + B_tokb2)
        for g in range(H):
            pb, pbuf = psum()
            for blk in range(NB):
                mm(pb[:, blk * P:(blk + 1) * P], vn_sb[:, blk, g * P:(g + 1) * P], wmT[:, g, :], True, True,
                   [B_vn[blk], B_const], [pbuf])
            k = ftc[0] % 2
            ftc[0] += 1
            r_b = bass.AP(r_sb[:].tensor, r_sb[:, g, :].offset, [list(r_sb[:].ap[0]), [0, NB], [1, P]])
            stt(ftmp[k][:, :].rearrange("p (a b) -> p a b", a=NB), pb[:, :].rearrange("p (a b) -> p a b", a=NB),
                lnw_col[:, g:g + 1], r_b, ALU.mult, ALU.add, [pbuf, B_const], [B_ftmp[k]])
            tt("pool", ubT[:, g, :], ftmp[k][:], uG[:, g, :], ALU.mult, [B_ftmp[k], B_uG[g]],
               [B_ubT[g], B_tok[0], B_tok[1]])

    def mixer_out():
        for oc in range(8):
            j, off = oc // 4, (oc % 4) * P
            wga, wgab = wget("ga%d" % j)
            pga, pgab = proj_feat(wga, wgab, off, uT, B_uT)
            wgb, wgbb = wget("gb%d" % j)
            pgb, pgbb = proj_feat(wgb, wgbb, off, uT, B_uT)
            wa_, wab = wget("pa%d" % j)
            pya, pyab = proj_feat(wa_, wab, off, hgT, B_hgT)
            wb_, wbb = wget("pb%d" % j)
            pyb, pybb = proj_feat(wb_, wbb, off, ubT, B_ubT)
            k = ftc[0] % 2
            ftc[0] += 1
            act(ftmp[k][:], pga[:], AF.Sigmoid, [pgab, B_const], [B_ftmp[k]], bias=bga[:, oc:oc + 1])
            act(ftmp2[k][:], pgb[:], AF.Sigmoid, [pgbb, B_const], [B_ftmp2[k]], bias=bgb[:, oc:oc + 1])
            tt("dve", ftmp[k][:], ftmp[k][:], pya[:], ALU.mult, [B_ftmp[k], pyab], [B_ftmp[k]])
            tt("dve", ftmp2[k][:], ftmp2[k][:], pyb[:], ALU.mult, [B_ftmp2[k], pybb], [B_ftmp2[k]])
            tt("pool", mixT[:, oc, :], ftmp[k][:], ftmp2[k][:], ALU.add, [B_ftmp[k], B_ftmp2[k]],
               [B_mixT[oc], B_t["q"], B_t["sg"], B_t["gt"], B_t["lf"]])

    def g_post_ln_all(w_bc, b_bc):
        yield from g_layernorm_stats_multi([(h_sb[:, blk, :], [B_h[blk]], blk) for blk in range(NB)])
        for blk in range(NB):
            act(h_sb[:, blk, :], h_sb[:, blk, :], AF.Identity, [B_h[blk], B_mv[blk]], [B_h[blk]],
                scale=mv[blk][:, 2:3], bias=mv[blk][:, 3:4])
            yield
        for blk in range(NB):
            tt("dve" if blk % 2 == 0 else "pool", h_sb[:, blk, :], h_sb[:, blk, :], w_bc[:], ALU.mult,
               [B_h[blk], B_const], [B_h[blk]])
            yield
        for blk in range(NB):
            tt("pool" if blk == 2 else "dve", h_sb[:, blk, :], h_sb[:, blk, :], b_bc[:], ALU.add,
               [B_h[blk], B_const], [B_h[blk]])
            yield

    def post_ln_all(*a_):
        for _ in g_post_ln_all(*a_):
            pass

    def residual_update(pb, pbuf, blk, hs, g_bc):
        k = ftc[0] % 2
        ftc[0] += 1
        tt("dve", ftmp[k][:], pb[:], g_bc[:, hs], ALU.mult, [pbuf, B_const], [B_ftmp[k]])
        stt(h_sb[:, blk, hs], h_sb[:, blk, hs], ALPHA, ftmp[k][:], ALU.mult, ALU.add, [B_h[blk], B_ftmp[k]],
            [B_h[blk]])

    def mix_and_ln1():
        for blk in range(NB):
            for half in range(2):
                wt, wbuf = wget("o%d" % half)
                pb, pbuf = psum()
                for kc in range(8):
                    mm(pb[:], mixT[:, kc, blk * P:(blk + 1) * P], wt[:, kc, :], kc == 0, kc == 7,
                       [wbuf] + B_mixT, [pbuf])
                residual_update(pb, pbuf, blk, slice(half * 512, (half + 1) * 512), g1_bc)
        post_ln_all(ln1w_bc, ln1b_bc)
        adaln_all(u2T, B_u2T, sc2p, sh2)

    def prefetch_x(ti):
        for b in range(NB):
            r0 = (ti * NB + b) * P
            dma("act", xpre[:, b, :], x_main[r0:r0 + P, :], [], [B_xpre[b]] + B_tokb + B_tok, ch_xpre[b])

    def ffn(next_ti=None):
        if next_ti is not None:
            prefetch_x(next_ti)
        for m in range(11):
            wt, wbuf = wget("fi%d" % m)
            for jj in range(2):
                j = 2 * m + jj
                pa_, pab_ = proj_feat(wt, wbuf, jj * P, u2T, B_u2T)
                pbb, pbbb = proj_feat(wt, wbuf, 256 + jj * P, u2T, B_u2T)
                k = ftc[0] % 2
                ftc[0] += 1
                act(ftmp2[k][:], pa_[:], AF.Silu, [pab_], [B_ftmp2[k]])
                tt("dve", actT[:, j, :], ftmp2[k][:], pbb[:], ALU.mult, [B_ftmp2[k], pbbb], [B_actT[j]])
        for half in range(2):
            hs = slice(half * 512, (half + 1) * 512)
            pbs = [psum() for _ in range(NB)]
            for kg in range(3):
                wt, wbuf = wget("fo%d_%d" % (kg, half))
                k0, k1 = kg * 8, min(NFF, kg * 8 + 8)
                for blk in range(NB):
                    for kc in range(k0, k1):
                        mm(pbs[blk][0][:], actT[:, kc, blk * P:(blk + 1) * P], wt[:, kc - k0, :], kc == 0,
                           kc == NFF - 1, [wbuf, B_actT[kc]], [pbs[blk][1]])
            for blk in range(NB):
                residual_update(pbs[blk][0], pbs[blk][1], blk, hs, g2_bc)

    B_out = [Buf("out%d" % b) for b in range(NB)]

    def g_ln2_and_store(ti):
        yield from g_post_ln_all(ln2w_bc, ln2b_bc)
        for blk in range(NB):
            r0 = (ti * NB + blk) * P
            dma("sp", out_d[r0:r0 + P, :], h_sb[:, blk, :], [B_h[blk]], [B_out[blk]], ch_h[blk])
            yield

    SEQ_WARM = ["i0", "f0", "f1", "i1"]
    SEQ_MAIN = (["i0", "f0", "q0", "g0", "i1", "f1", "q1", "g1", "u0", "u1", "v0", "v1",
                 "ga0", "gb0", "pa0", "pb0", "ga1", "gb1", "pa1", "pb1", "o0", "o1"]
                + ["fi%d" % m for m in range(11)]
                + ["fo%d_%d" % (kg, half) for half in range(2) for kg in range(3)])

    for ti in range(n_tiles):
        wseq[0] = SEQ_WARM + (SEQ_WARM[:1] if ti + 1 < n_tiles else SEQ_MAIN[:1])
        issue_deferred(len(deferred) if ti == n_tiles - 1 else -(-len(deferred) // (n_tiles - ti)))
        load_x(x_warm, ti)
        adaln_all(uT, B_uT, sc1p, sh1)
        run_merged([g_make_v([0])])
        hgrn_all(False, extra=g_make_v([1]))
    for h in range(H):
        ts("dve", carry[:, h, :], carry[:, h, :], flag[:, 0:1], None, ALU.mult, None, [B_carry[h], B_const],
           [B_carry[h]])
    dump("carry", carry[:], B_carry[H - 1])
    S.barrier(exclude=(ch_A.key, ch_B.key))

    for ti in range(n_tiles):
        wseq[0] = SEQ_MAIN + (SEQ_MAIN[:1] if ti + 1 < n_tiles else [])
        if ti == 0:
            load_x(x_main, ti)
            adaln_all(uT, B_uT, sc1p, sh1)
        if ti == 0:
            dump("uT", uT, B_uT[NB - 1])
        run_merged([g_make_v([0])])
        hgrn_all(True, extra=g_make_v([1]))
        if ti == 0:
            dump("hgT", hgT, B_hgT[H - 1])
        gmlp()
        if ti == 0:
            dump("ubT", ubT, B_ubT[H - 1])
        mixer_out()
        if ti == 0:
            dump("mixT", mixT, B_mixT[7])
        wload("o0")
        wload("o1")
        S.barrier()
        mix_and_ln1()
        if ti == 0:
            dump("u2T", u2T, B_u2T[NB - 1])
            dump("h1", h_sb[:], B_h[NB - 1])
        ffn(ti + 1 if ti + 1 < n_tiles else None)
        S.barrier()
        if ti + 1 < n_tiles:
            run_merged([g_ln2_and_store(ti),
                        g_adaln_all(uT, B_uT, sc1p, sh1, src=[(xpre[:, b, :], [B_xpre[b]]) for b in range(NB)],
                                    tb=(tokb2, B_tokb2), koff=NB)])
            for b in range(NB):
                dma("pool", h_sb[:, b, :], xpre[:, b, :], [B_xpre[b]], [B_h[b]], ch_h[b])
        else:
            run_merged([g_ln2_and_store(ti)])

    S.add("sp", None, B_out + dump_list, [])
    S.emit(nc, sems)
    es.close()
    return nc


_CONST_CACHE = {}


def _consts():
    if not _CONST_CACHE:
        s = np.arange(P)
        mask2 = ((s[:, None] // 64 == s[None, :] // 64) & (s[:, None] <= s[None, :])).astype(np.float32)
        t = np.arange(T)
        m64 = np.broadcast_to((t % 64 != 0).astype(np.float32), (P, T)).copy()
        m32 = np.broadcast_to((t % 32 != 0).astype(np.float32), (P, T)).copy()
        _CONST_CACHE.update(
            ident_f=np.eye(P, dtype=np.float32),
            ident_b=np.eye(P, dtype=np.float32).astype(ml_dtypes.bfloat16),
            mask2=mask2, m64=m64, m32=m32)
    return _CONST_CACHE


def make_in_maps(inp, n_tiles, seq):
    f = lambda a: np.ascontiguousarray(np.asarray(a, dtype=np.float32))
    x = f(inp["x"])
    Tc = n_tiles * T
    assert seq == 2 * Tc
    shared = dict(
        w_ada=f(inp["w_ada"][0]), b_ada=f(inp["b_ada"][0]).reshape(1, -1), w_in=f(inp["w_in"][0]),
        w_proj_a=f(inp["w_proj_a"][0]), w_proj_b=f(inp["w_proj_b"][0]), w_out=f(inp["w_out"][0]),
        w_ffn_in=f(inp["w_ffn_in"][0]), w_ffn_out=f(inp["w_ffn_out"][0]),
        gmlp_ln_b=f(inp["gmlp_ln_b"][0]).reshape(1, -1), gmlp_ws=f(inp["gmlp_ws"][0]),
        gmlp_bs=f(inp["gmlp_bs"][0]).reshape(1, -1),
        ln1_w=f(inp["ln1_w"][0]).reshape(1, -1), ln1_b=f(inp["ln1_b"][0]).reshape(1, -1),
        ln2_w=f(inp["ln2_w"][0]).reshape(1, -1), ln2_b=f(inp["ln2_b"][0]).reshape(1, -1),
        **_consts())
    b_ada = f(inp["b_ada"][0])
    maps = []
    for core in range(N_CORES):
        b, half = core // 2, core % 2
        rows = np.concatenate([
            f(inp["c"])[b].reshape(8, P),
            f(inp["gmlp_ln_w"][0]).reshape(8, P),
            f(inp["b_gate"][0, 0]).reshape(8, P),
            f(inp["b_gate"][0, 1]).reshape(8, P),
            f(inp["hgrn_lb_logits"][0]).reshape(8, P),
            f(inp["hgrn_lb_logits"][1]).reshape(8, P),
            f(inp["hgrn_norm_w"][0]).reshape(1, P),
            b_ada[0:D].reshape(8, P), b_ada[D:2 * D].reshape(8, P),
            b_ada[3 * D:4 * D].reshape(8, P), b_ada[4 * D:5 * D].reshape(8, P)], axis=0)
        m = dict(shared)
        m["x_main"] = np.ascontiguousarray(x[b, half * Tc:(half + 1) * Tc])
        m["x_warm"] = np.ascontiguousarray(x[b, 0:Tc])
        m["rows"] = np.ascontiguousarray(rows)
        m["flag"] = np.full((P, 1), float(half), np.float32)
        maps.append(m)
    return maps


_NC_CACHE = {}


def run(inp, n_tiles, seq, dumps=None):
    key = (n_tiles, tuple(sorted(dumps)) if dumps else None)
    if key not in _NC_CACHE:
        _NC_CACHE[key] = build(n_tiles, dumps)
    nc = _NC_CACHE[key]
    maps = make_in_maps(inp, n_tiles, seq)
    res = run_bass_kernel_spmd(nc, maps, core_ids=list(range(N_CORES)))
    Tc = n_tiles * T
    out = np.empty((BATCH, seq, D), np.float32)
    for core in range(N_CORES):
        b, half = core // 2, core % 2
        out[b, half * Tc:(half + 1) * Tc] = res.results[core]["out"]
    return out, res


def kernel(**inputs):
    out, _ = run(inputs, SEQ // 2 // T, SEQ)
    return out
```
